# Optimizing a Trainium2 kernel written in Bass

```python
import math
import jax
import jax.numpy as jnp
from jax import lax
import numpy as np


D_MODEL = 1024
BATCH = 2
SEQ = 8192
DEPTH = 4

CHUNK = 64
N_MIXERS = 3
N_A = (DEPTH + 2) // 3
N_B = (DEPTH + 1) // 3
N_C = DEPTH // 3
D_FF = 2816
EPS = 1e-6
NEG_INF = -1e30
HEADS_A = 16
HEAD_DIM_A = D_MODEL // HEADS_A
LEFT_CHUNKS = 8
BAND = (LEFT_CHUNKS + 1) * CHUNK
REL_CLIP = 128
GMLP_CHUNK = 128
D_GATE = D_MODEL
GMLP_GROUPS = 8
GMLP_GROUP_DIM = D_GATE // GMLP_GROUPS
HEADS_C = 8
HEAD_DIM_C = D_MODEL // (2 * HEADS_C)
Q_BLOCK = 128

kernel_name = 'hybrid_streaming_interleaved_block'


def rms_norm(x, g):
    xf = x.astype(jnp.float32)
    y = xf * lax.rsqrt(jnp.mean(xf * xf, axis=-1, keepdims=True) + EPS)
    return (y * g.astype(jnp.float32)).astype(x.dtype)


def swiglu(h, w_in, w_out):
    gate, up = jnp.split(h @ w_in, 2, axis=-1)
    return (jax.nn.silu(gate) * up) @ w_out


def mixer_a(h, w_qkv, rel_bias, w_o):
    B, S, D = h.shape
    nc = S // CHUNK
    pad = LEFT_CHUNKS * CHUNK
    q, k, v = jnp.split(h @ w_qkv, 3, axis=-1)
    q = q.reshape(B, nc, CHUNK, HEADS_A, HEAD_DIM_A).transpose(1, 0, 2, 3, 4)
    k = jnp.pad(k.reshape(B, S, HEADS_A, HEAD_DIM_A), ((0, 0), (pad, 0), (0, 0), (0, 0)))
    v = jnp.pad(v.reshape(B, S, HEADS_A, HEAD_DIM_A), ((0, 0), (pad, 0), (0, 0), (0, 0)))
    rel = jnp.clip(pad + jnp.arange(CHUNK)[:, None] - jnp.arange(BAND)[None, :], -REL_CLIP, REL_CLIP) + REL_CLIP
    bias = rel_bias.astype(jnp.float32)[:, rel]
    scale = HEAD_DIM_A ** -0.5

    def one_chunk(args):
        c, qc = args
        start = c * CHUNK
        kb = lax.dynamic_slice_in_dim(k, start, BAND, axis=1)
        vb = lax.dynamic_slice_in_dim(v, start, BAND, axis=1)
        s = jnp.einsum('bqhd,bkhd->bhqk', qc, kb).astype(jnp.float32) * scale + bias
        valid = (start - pad + jnp.arange(BAND)) >= 0
        s = jnp.where(valid, s, NEG_INF)
        p = jax.nn.softmax(s, axis=-1).astype(vb.dtype)
        return jnp.einsum('bhqk,bkhd->bqhd', p, vb)

    o = lax.map(one_chunk, (jnp.arange(nc), q))
    o = o.transpose(1, 0, 2, 3, 4).reshape(B, S, D)
    return o @ w_o


def mixer_b(h, w_in, ln_g, ln_b, w_s, b_s, w_o):
    B, S, _ = h.shape
    n = S // GMLP_CHUNK
    u, v = jnp.split(jax.nn.gelu(h @ w_in, approximate=False), 2, axis=-1)
    vf = v.astype(jnp.float32)
    mu = jnp.mean(vf, axis=-1, keepdims=True)
    var = jnp.mean(jnp.square(vf - mu), axis=-1, keepdims=True)
    v = ((vf - mu) * lax.rsqrt(var + EPS) * ln_g.astype(jnp.float32) + ln_b.astype(jnp.float32)).astype(h.dtype)
    v = v.reshape(B, n, GMLP_CHUNK, GMLP_GROUPS, GMLP_GROUP_DIM)
    causal = jnp.tril(jnp.ones((GMLP_CHUNK, GMLP_CHUNK), dtype=bool))
    w = jnp.where(causal[None], w_s, 0.0)
    sv = jnp.einsum('gts,bnsgc->bntgc', w, v) + b_s.T[None, None, :, :, None]
    y = u * sv.reshape(B, S, D_GATE)
    return y @ w_o


def mixer_c(h, w_qkv, lam, subln_g, w_o, lambda_init):
    B, S, D = h.shape
    nb = S // Q_BLOCK
    q, k, v = jnp.split(h @ w_qkv, 3, axis=-1)
    q = q.reshape(B, nb, Q_BLOCK, HEADS_C, 2, HEAD_DIM_C).transpose(1, 0, 2, 3, 4, 5)
    k = k.reshape(B, S, HEADS_C, 2, HEAD_DIM_C)
    v = v.reshape(B, S, HEADS_C, 2 * HEAD_DIM_C)
    lamf = lam.astype(jnp.float32)
    lam_full = jnp.exp(jnp.sum(lamf[0] * lamf[1])) - jnp.exp(jnp.sum(lamf[2] * lamf[3])) + lambda_init
    slopes = 2.0 ** (-8.0 * (jnp.arange(HEADS_C, dtype=jnp.float32) + 1.0) / HEADS_C)
    kpos = jnp.arange(S)
    scale = HEAD_DIM_C ** -0.5

    def one_block(args):
        blk, qblk = args
        qpos = blk * Q_BLOCK + jnp.arange(Q_BLOCK)
        dist = jnp.abs(qpos[:, None] - kpos[None, :]).astype(jnp.float32)
        alibi = -slopes[:, None, None] * dist
        allowed = kpos[None, :] < (qpos[:, None] // CHUNK + 1) * CHUNK
        s = jnp.einsum('bqhmd,bkhmd->bmhqk', qblk, k).astype(jnp.float32) * scale + alibi
        s = jnp.where(allowed, s, NEG_INF)
        p = jax.nn.softmax(s, axis=-1)
        a = p[:, 0] - lam_full * p[:, 1]
        return jnp.einsum('bhqk,bkhe->bqhe', a.astype(v.dtype), v)

    o = lax.map(one_block, (jnp.arange(nb), q))
    o = o.transpose(1, 0, 2, 3, 4).reshape(B, S, HEADS_C, 2 * HEAD_DIM_C)
    o = rms_norm(o, subln_g) * (1.0 - lambda_init)
    return o.reshape(B, S, D) @ w_o


def setup_inputs(seed: int = 0) -> dict:
    key = jax.random.key(seed)
    ks = jax.random.split(key, 20)

    def nrm(k, shape, scale):
        return jax.random.normal(k, shape, jnp.float32) * scale

    D = D_MODEL
    return {
        'x': nrm(ks[0], (BATCH, SEQ, D), 1.0),
        'norm_g': 1.0 + nrm(ks[1], (DEPTH, 6, D), 0.05),
        'ff1_w_in': nrm(ks[2], (DEPTH, D, 2 * D_FF), D ** -0.5),
        'ff1_w_out': nrm(ks[3], (DEPTH, D_FF, D), D_FF ** -0.5),
        'ff2_w_in': nrm(ks[4], (DEPTH, D, 2 * D_FF), D ** -0.5),
        'ff2_w_out': nrm(ks[5], (DEPTH, D_FF, D), D_FF ** -0.5),
        'a_w_qkv': nrm(ks[6], (N_A, D, 3 * D), D ** -0.5),
        'a_rel_bias': nrm(ks[7], (N_A, HEADS_A, 2 * REL_CLIP + 1), 0.5),
        'a_w_o': nrm(ks[8], (N_A, D, D), D ** -0.5),
        'b_w_in': nrm(ks[9], (N_B, D, 2 * D_GATE), D ** -0.5),
        'b_ln_g': 1.0 + nrm(ks[10], (N_B, D_GATE), 0.05),
        'b_ln_b': nrm(ks[11], (N_B, D_GATE), 0.05),
        'b_w_s': nrm(ks[12], (N_B, GMLP_GROUPS, GMLP_CHUNK, GMLP_CHUNK), GMLP_CHUNK ** -0.5),
        'b_b_s': 1.0 + nrm(ks[13], (N_B, GMLP_GROUPS, GMLP_CHUNK), 0.1),
        'b_w_o': nrm(ks[14], (N_B, D_GATE, D), D_GATE ** -0.5),
        'c_w_qkv': nrm(ks[15], (N_C, D, 3 * D), D ** -0.5),
        'c_lambda': nrm(ks[16], (N_C, 4, HEAD_DIM_C), 0.1),
        'c_subln_g': 1.0 + nrm(ks[17], (N_C, 2 * HEAD_DIM_C), 0.05),
        'c_w_o': nrm(ks[18], (N_C, D, D), D ** -0.5),
    }


def reference(x, norm_g, ff1_w_in, ff1_w_out, ff2_w_in, ff2_w_out, a_w_qkv, a_rel_bias, a_w_o, b_w_in, b_ln_g, b_ln_b, b_w_s, b_b_s, b_w_o, c_w_qkv, c_lambda, c_subln_g, c_w_o):
    for i in range(DEPTH):
        g = norm_g[i]
        x = x + 0.5 * rms_norm(swiglu(rms_norm(x, g[0]), ff1_w_in[i], ff1_w_out[i]), g[1])
        h = rms_norm(x, g[2])
        kind, j = i % N_MIXERS, i // N_MIXERS
        if kind == 0:
            m = mixer_a(h, a_w_qkv[j], a_rel_bias[j], a_w_o[j])
        elif kind == 1:
            m = mixer_b(h, b_w_in[j], b_ln_g[j], b_ln_b[j], b_w_s[j], b_b_s[j], b_w_o[j])
        else:
            lambda_init = 0.8 - 0.6 * math.exp(-0.3 * i)
            m = mixer_c(h, c_w_qkv[j], c_lambda[j], c_subln_g[j], c_w_o[j], lambda_init)
        x = x + rms_norm(m, g[3])
        x = x + 0.5 * rms_norm(swiglu(rms_norm(x, g[4]), ff2_w_in[i], ff2_w_out[i]), g[5])
    return x
```

```python
import numpy as np
import concourse.bass as bass
import concourse.mybir as mybir
from concourse.bass_utils import run_bass_kernel_spmd

F32 = mybir.dt.float32
BF16 = mybir.dt.bfloat16
AF = mybir.ActivationFunctionType
ALU = mybir.AluOpType

D = 1024
DC = 8
DFF = 2816
FC = 22
NTOK = 2048
TT = 512
NT = NTOK // TT
EPS = 1e-6
DEPTH = 4
NCORES = 8


_FENCE = {}


class Buf:
    __slots__ = ("name", "w", "r")

    def __init__(self, name):
        self.name = name
        self.w = None
        self.r = dict(_FENCE)


class DSem:
    def __init__(self, key, sem):
        self.key = key
        self.sem = sem
        self.cnt = 0


class EngState:
    def __init__(self, name, eng, sem):
        self.name = name
        self.eng = eng
        self.sem = sem
        self.cnt = 0
        self.waited = {}


class MK:
    def __init__(self, nc, stack):
        self.nc = nc
        self.stack = stack
        self.engs = {}
        for name, eng in (("pe", nc.tensor), ("act", nc.scalar), ("dve", nc.vector),
                          ("pool", nc.gpsimd), ("sp", nc.sync)):
            sem = stack.enter_context(nc.semaphore("s_" + name))
            self.engs[name] = EngState(name, eng, sem)
        self.dsems = []
        self.free_dsems = {"sp": [], "pool": []}
        self.stage_dsems = []
        _FENCE.clear()

    def dsem(self, name, q="sp"):
        if self.free_dsems[q]:
            d = self.free_dsems[q].pop()
        else:
            sem = self.stack.enter_context(self.nc.semaphore("d_%d" % len(self.dsems)))
            d = DSem("d_%d" % len(self.dsems), sem)
            d.q = q
            self.dsems.append(d)
        self.stage_dsems.append(d)
        return d

    def stage_boundary(self):
        for d in self.stage_dsems:
            self.free_dsems[d.q].append(d)
        self.stage_dsems = []
        _FENCE.clear()
        for E in self.engs.values():
            if E.cnt > 0:
                _FENCE[E.name] = (E.name, E.sem, E.cnt)
        for d in self.dsems:
            if d.cnt > 0:
                _FENCE[d.key] = (d.key, d.sem, d.cnt)

    def _wait(self, E, toks):
        need = {}
        for tok in toks:
            if tok is None:
                continue
            key, sem, val = tok
            if key == "pe" and E.name == "pe":
                continue
            if key not in need or need[key][1] < val:
                need[key] = (sem, val)
        for key, (sem, val) in need.items():
            if E.waited.get(key, 0) < val:
                E.eng.wait_ge(sem, val)
                E.waited[key] = val

    def _deps(self, reads, writes):
        toks = []
        for b in reads:
            toks.append(b.w)
        for b in writes:
            toks.append(b.w)
            toks.extend(b.r.values())
        return toks

    def op(self, ename, fn, reads=(), writes=()):
        E = self.engs[ename]
        self._wait(E, self._deps(reads, writes))
        ins = fn(E.eng)
        E.cnt += 1
        ins.then_inc(E.sem, 1)
        tok = (ename, E.sem, E.cnt)
        for b in reads:
            b.r[ename] = tok
        for b in writes:
            b.w = tok
            b.r = {}
        return tok

    def dma(self, qname, dsem, pairs, reads=(), writes=(), extra=(), **kw):
        E = self.engs[qname]
        assert dsem.q == qname, (dsem.key, dsem.q, qname)
        self._wait(E, self._deps(reads, writes) + list(extra))
        for pr in pairs:
            if len(pr) == 3:
                E.eng.indirect_dma_start(out=pr[0], out_offset=None, in_=pr[1],
                                         in_offset=bass.IndirectOffsetOnAxis(ap=pr[2], axis=0)).then_inc(dsem.sem, 16)
            else:
                E.eng.dma_start(out=pr[0], in_=pr[1], **kw).then_inc(dsem.sem, 16)
            dsem.cnt += 16
        tok = (dsem.key, dsem.sem, dsem.cnt)
        for b in reads:
            b.r[dsem.key] = tok
        for b in writes:
            b.w = tok
            b.r = {}
        return tok

    def allgather(self, csem, src_ap, dst_ap, extra=(), writes=()):
        E = self.engs["pool"]
        self._wait(E, self._deps((), writes) + list(extra))
        E.eng.collective_compute("AllGather", ALU.bypass, replica_groups=[[0, 1, 2, 3], [4, 5, 6, 7]],
                                 ins=[src_ap], outs=[dst_ap]).then_inc(csem.sem, 1)
        csem.cnt += 1
        tok = (csem.key, csem.sem, csem.cnt)
        for b in writes:
            b.w = tok
            b.r = {}
        return tok

    def wait_all(self, ename, toks):
        self._wait(self.engs[ename], toks)


from contextlib import ExitStack

SEQ = 8192
NKB = SEQ // 128
NEG = -30000.0
LAMBDA_INIT_L2 = 0.8 - 0.6 * float(np.exp(-0.3 * 2))


class Prog:
    def __init__(self, nc, stack, mk):
        self.nc = nc
        self.stack = stack
        self.mk = mk

    def sb(self, name, shape, dt, stack=None):
        self.uid = getattr(self, "uid", 0) + 1
        return (stack or self.stack).enter_context(self.nc.sbuf_tensor(f"{name}_{self.uid}", shape, dt))

    def ps(self, name, shape, dt=F32, stack=None):
        self.uid = getattr(self, "uid", 0) + 1
        return (stack or self.stack).enter_context(self.nc.psum_tensor(f"{name}_{self.uid}", shape, dt))


def gcol(l, n, dc):
    return (l * 6 + n) * DC + dc


def setup_common(P, gT_ap):
    nc, mk = P.nc, P.mk
    P.xs = P.sb("xs", [128, DC, NTOK], F32)
    P.X = [[Buf(f"x{dc}_{t}") for t in range(NT)] for dc in range(DC)]
    P.ones = P.sb("ones", [128, 128], BF16)
    P.ONES = Buf("ones")
    P.epsb = P.sb("epsb", [128, 1], F32)
    P.EPSB = Buf("epsb")
    P.g = P.sb("g", [128, DEPTH * 6 * DC], F32)
    P.hg = P.sb("hg", [128, DEPTH * 6 * DC], F32)
    P.G = Buf("g")
    P.HG = Buf("hg")
    P.d_const = mk.dsem("const")
    mk.op("dve", lambda e: e.memset(P.ones[:], 1.0), writes=[P.ONES])
    mk.op("dve", lambda e: e.memset(P.epsb[:], EPS), writes=[P.EPSB])
    mk.dma("sp", P.d_const, [(P.g[:], gT_ap)], writes=[P.G])
    mk.op("dve", lambda e: e.tensor_scalar(P.hg[:], P.g[:], 0.5, None, ALU.mult),
          reads=[P.G], writes=[P.HG])


class NormWS:
    def __init__(self, P, st, tag):
        self.P = P
        self.sq = [P.sb(f"sq{i}_{tag}", [128, TT], BF16, st) for i in range(2)]
        self.SQ = [Buf(f"sq{i}") for i in range(2)]
        self.rt = P.sb(f"rt_{tag}", [128, TT], F32, st)
        self.RT = Buf("rt")
        self.rstd = P.sb(f"rstd_{tag}", [128, TT], F32, st)
        self.RSTD = Buf("rstd")
        self.ss_ps = P.ps(f"ss_ps_{tag}", [128, TT], F32, st)
        self.SSPS = Buf("ssps")
        self.k = 0

    def square(self, src_ap, SRC):
        i = self.k % 2
        self.k += 1
        self.P.mk.op("act", lambda e: e.activation(out=self.sq[i][:], in_=src_ap, func=AF.Square),
                     reads=SRC, writes=[self.SQ[i]])
        return i

    def accum(self, i, first, last):
        P = self.P
        P.mk.op("pe", lambda e: e.matmul(self.ss_ps[:], P.ones[:], self.sq[i][:], start=first, stop=last),
                reads=[P.ONES, self.SQ[i]], writes=[self.SSPS])

    def finish(self, n):
        P = self.P
        P.mk.op("act", lambda e: e.activation(out=self.rt[:], in_=self.ss_ps[:], func=AF.Sqrt,
                                              bias=P.epsb[:], scale=1.0 / n),
                reads=[self.SSPS, P.EPSB], writes=[self.RT])
        P.mk.op("dve", lambda e: e.reciprocal(self.rstd[:], self.rt[:]), reads=[self.RT], writes=[self.RSTD])


def emit_norm_in(P, ws, t, l, n, xn, XN):
    mk = P.mk
    tsl = slice(t * TT, (t + 1) * TT)
    for dc in range(DC):
        i = ws.square(P.xs[:, dc, tsl], [P.X[dc][t]])
        ws.accum(i, dc == 0, dc == DC - 1)
    ws.finish(D)
    for dc in range(DC):
        c = gcol(l, n, dc)
        mk.op("dve", lambda e: e.scalar_tensor_tensor(xn[:, dc, :], P.xs[:, dc, tsl], P.g[:, c:c + 1],
                                                      ws.rstd[:], ALU.mult, ALU.mult),
              reads=[P.X[dc][t], P.G, ws.RSTD], writes=[XN[dc]])


class ProjWS:
    def __init__(self, P, st, tag):
        self.y = P.sb(f"y_{tag}", [128, DC, TT], F32, st)
        self.Y = [Buf(f"y{dc}") for dc in range(DC)]
        self.y_ps = [P.ps(f"y_ps{i}_{tag}", [128, TT], F32, st) for i in range(2)]
        self.YPS = [Buf(f"yps{i}") for i in range(2)]
        self.tmp = [P.sb(f"tmp{i}_{tag}", [128, TT], F32, st) for i in range(2)]
        self.TMP = [Buf(f"tmp{i}") for i in range(2)]


def emit_proj_norm_res(P, ws, pw, t, rhs, RHS, nfc, wres, WRES, l, n, half):
    mk = P.mk
    tsl = slice(t * TT, (t + 1) * TT)
    gt = P.hg if half else P.g
    GT = P.HG if half else P.G
    pend = None
    for dc in range(DC):
        b = dc % 2
        for fc in range(nfc):
            mk.op("pe", lambda e: e.matmul(pw.y_ps[b][:], wres[:, fc, dc * 128:(dc + 1) * 128], rhs[:, fc, :],
                                           start=(fc == 0), stop=(fc == nfc - 1)),
                  reads=[WRES, RHS[fc]], writes=[pw.YPS[b]])
        if pend is not None:
            ws.accum(pend, dc - 1 == 0, False)
        mk.op("dve", lambda e: e.tensor_copy(pw.y[:, dc, :], pw.y_ps[b][:]), reads=[pw.YPS[b]], writes=[pw.Y[dc]])
        pend = ws.square(pw.y[:, dc, :], [pw.Y[dc]])
    ws.accum(pend, False, True)
    ws.finish(D)
    for dc in range(DC):
        i = dc % 2
        c = gcol(l, n, dc)
        mk.op("dve", lambda e: e.tensor_tensor(pw.tmp[i][:], pw.y[:, dc, :], ws.rstd[:], ALU.mult),
              reads=[pw.Y[dc], ws.RSTD], writes=[pw.TMP[i]])
        mk.op("dve", lambda e: e.scalar_tensor_tensor(P.xs[:, dc, tsl], pw.tmp[i][:], gt[:, c:c + 1],
                                                      P.xs[:, dc, tsl], ALU.mult, ALU.add),
              reads=[pw.TMP[i], GT, P.X[dc][t]], writes=[P.X[dc][t]])


def emit_ffn(P, l, which, w_in_ap, w_out_ap):
    nc, mk = P.nc, P.mk
    na, nb = (0, 1) if which == 0 else (4, 5)
    GF = 2
    NG = FC // GF
    NS = 2
    tag = f"f{l}{which}"
    mk.stage_boundary()
    with ExitStack() as st:
        xn2 = [P.sb(f"xn{i}", [128, DC, TT], BF16, st) for i in range(2)]
        XN2 = [[Buf(f"xn{i}_{dc}") for dc in range(DC)] for i in range(2)]
        a = P.sb("a", [128, FC, TT], BF16, st)
        A = [Buf(f"a{fc}") for fc in range(FC)]
        win = [P.sb(f"win{s}", [128, DC, 2 * GF * 128], BF16, st) for s in range(NS)]
        WIN = [Buf(f"win{s}") for s in range(NS)]
        d_win = [mk.dsem(f"win{s}_{tag}", "pool") for s in range(NS)]
        wout = P.sb("wout", [128, FC, D], BF16, st)
        WOUT = Buf("wout")
        d_wout = mk.dsem(f"wout_{tag}", "pool")
        sg = [P.sb(f"sg{i}", [128, TT], F32, st) for i in range(2)]
        SG = [Buf(f"sg{i}") for i in range(2)]
        gate_ps = [P.ps(f"gate_ps{i}", [128, TT], F32, st) for i in range(2)]
        GPS = [Buf(f"gps{i}") for i in range(2)]
        up_ps = [P.ps(f"up_ps{i}", [128, TT], F32, st) for i in range(2)]
        UPS = [Buf(f"ups{i}") for i in range(2)]
        ws = NormWS(P, st, tag)
        pw = ProjWS(P, st, tag)

        w_in_v = w_in_ap.rearrange("(kc p) c -> p kc c", p=128)
        w_out_v = w_out_ap.rearrange("(fc p) d -> p fc d", p=128)

        def load_win(t, g):
            s = (t * NG + g) % NS
            c0 = g * GF * 128
            mk.dma("pool", d_win[s],
                   [(win[s][:, :, 0:GF * 128], w_in_v[:, :, c0:c0 + GF * 128]),
                    (win[s][:, :, GF * 128:2 * GF * 128], w_in_v[:, :, DFF + c0:DFF + c0 + GF * 128])],
                   writes=[WIN[s]])

        mk.dma("pool", d_wout,
               [(wout[:, 0:11, :], w_out_v[:, 0:11, :]), (wout[:, 11:22, :], w_out_v[:, 11:22, :])],
               writes=[WOUT])

        emit_norm_in(P, ws, 0, l, na, xn2[0], XN2[0])
        for t in range(NT):
            xn, XN = xn2[t % 2], XN2[t % 2]
            load_win(t, 0)
            k = 0
            for g in range(NG):
                s = (t * NG + g) % NS
                if g + 1 < NG:
                    load_win(t, g + 1)
                if g == NG - 3 and t + 1 < NT:
                    emit_norm_in(P, ws, t + 1, l, na, xn2[(t + 1) % 2], XN2[(t + 1) % 2])
                for j in range(GF):
                    fc = g * GF + j
                    b = k % 2
                    k += 1
                    for kc in range(DC):
                        mk.op("pe", lambda e: e.matmul(gate_ps[b][:], win[s][:, kc, j * 128:(j + 1) * 128],
                                                       xn[:, kc, :], start=(kc == 0), stop=(kc == DC - 1)),
                              reads=[WIN[s], XN[kc]], writes=[GPS[b]])
                    for kc in range(DC):
                        mk.op("pe", lambda e: e.matmul(up_ps[b][:],
                                                       win[s][:, kc, GF * 128 + j * 128:GF * 128 + (j + 1) * 128],
                                                       xn[:, kc, :], start=(kc == 0), stop=(kc == DC - 1)),
                              reads=[WIN[s], XN[kc]], writes=[UPS[b]])
                    mk.op("act", lambda e: e.activation(out=sg[b][:], in_=gate_ps[b][:], func=AF.Silu),
                          reads=[GPS[b]], writes=[SG[b]])
                    mk.op("dve", lambda e: e.tensor_tensor(a[:, fc, :], sg[b][:], up_ps[b][:], ALU.mult),
                          reads=[SG[b], UPS[b]], writes=[A[fc]])
            emit_proj_norm_res(P, ws, pw, t, a, A, FC, wout, WOUT, l, nb, True)


def load_wres(P, st, name, w_ap, nfc, dsem):
    w = P.sb(name, [128, nfc, D], BF16, st)
    W = Buf(name)
    v = w_ap.rearrange("(fc p) d -> p fc d", p=128)
    h = nfc // 2
    P.mk.dma("pool", dsem, [(w[:, 0:h, :], v[:, 0:h, :]), (w[:, h:nfc, :], v[:, h:nfc, :])], writes=[W])
    return w, W


def emit_pre(P, l, wqkv_ap, X, tag):
    mk = P.mk
    mk.stage_boundary()
    with ExitStack() as st:
        hn = P.sb("hn", [128, DC, TT], BF16, st)
        HN = [Buf(f"hn{dc}") for dc in range(DC)]
        wq = P.sb("wqkv", [128, DC, 3 * D], BF16, st)
        WQ = Buf("wqkv")
        d_w = mk.dsem(f"wqkv_{tag}", "pool")
        ws = NormWS(P, st, tag)
        ps = [P.ps(f"pre_ps{i}", [128, TT], F32, st) for i in range(2)]
        PS = [Buf(f"preps{i}") for i in range(2)]
        stg = [P.sb(f"stg{i}", [128, TT], BF16, st) for i in range(4)]
        STG = [Buf(f"stg{i}") for i in range(4)]
        d_stg = [mk.dsem(f"stg{i}_{tag}") for i in range(4)]
        wv = wqkv_ap.rearrange("(kc p) c -> p kc c", p=128)
        mk.dma("pool", d_w, [(wq[:, :, i * 768:(i + 1) * 768], wv[:, :, i * 768:(i + 1) * 768]) for i in range(4)],
               writes=[WQ])
        k = 0
        otok = {}
        for t in range(NT):
            tsl = slice(t * TT, (t + 1) * TT)
            emit_norm_in(P, ws, t, l, 2, hn, HN)
            for fcg in range(16):
                b = k % 2
                si = k % 4
                k += 1
                for kc in range(DC):
                    mk.op("pe", lambda e: e.matmul(ps[b][:], wq[:, kc, fcg * 128:(fcg + 1) * 128], hn[:, kc, :],
                                                   start=(kc == 0), stop=(kc == DC - 1)),
                          reads=[WQ, HN[kc]], writes=[PS[b]])
                if fcg < 8:
                    mk.op("act", lambda e: e.activation(out=stg[si][:], in_=ps[b][:], func=AF.Copy, scale=0.125),
                          reads=[PS[b]], writes=[STG[si]])
                    dst = X.snd_q[t][fcg * 128:(fcg + 1) * 128, :]
                else:
                    mk.op("dve", lambda e: e.tensor_copy(stg[si][:], ps[b][:]), reads=[PS[b]], writes=[STG[si]])
                    dst = X.snd_k[t][(fcg - 8) * 128:(fcg - 7) * 128, :]
                otok[si] = mk.dma("sp", d_stg[si], [(dst, stg[si][:])], reads=[STG[si]])
            for tb in range(TT // 128):
                for hf in range(2):
                    b = k % 2
                    si = k % 4
                    k += 1
                    for kc in range(DC):
                        mk.op("pe", lambda e: e.matmul(ps[b][:], hn[:, kc, tb * 128:(tb + 1) * 128],
                                                       wq[:, kc, 2 * D + hf * 512:2 * D + (hf + 1) * 512],
                                                       start=(kc == 0), stop=(kc == DC - 1)),
                              reads=[WQ, HN[kc]], writes=[PS[b]])
                    if hf == 0:
                        mk.op("act", lambda e: e.activation(out=stg[si][:], in_=ps[b][:], func=AF.Copy),
                              reads=[PS[b]], writes=[STG[si]])
                    else:
                        mk.op("dve", lambda e: e.tensor_copy(stg[si][:], ps[b][:]), reads=[PS[b]], writes=[STG[si]])
                    pairs = []
                    for jj in range(2):
                        jb = hf * 2 + jj
                        pairs.append((X.snd_v[t][jb * TT + tb * 128:jb * TT + tb * 128 + 128, :],
                                      stg[si][:, jj * 256:(jj + 1) * 256]))
                    otok[si] = mk.dma("sp", d_stg[si], pairs, reads=[STG[si]])
            done = list(otok.values())
            for s_, g_ in ((X.snd_q[t], X.gath_q[t]), (X.snd_k[t], X.gath_k[t]), (X.snd_v[t], X.gath_v[t])):
                X.last = mk.allgather(P.csem, s_, g_, extra=done)


class Pipe:
    def __init__(self, depth, batch):
        self.q = []
        self.depth = depth
        self.batch = batch

    def _n(self):
        return sum(1 for w, _ in self.q if w)

    def push(self, fn, weight=1):
        self.q.append((weight, fn))
        while self._n() >= self.depth + self.batch:
            self._pop()

    def _pop(self):
        items = []
        while self.q and self.q[0][0] == 1 and len(items) < self.batch:
            items.append(self.q.pop(0)[1])
        order = sorted(range(len(items)), key=lambda i: items[i][0])
        for i in order:
            items[i][1]()
        for i in order:
            items[i][2]()
        while self.q and self.q[0][0] == 0:
            self.q.pop(0)[1]()

    def flush(self):
        while self.q:
            self._pop()


def load_kv(P, X, d_kv, kt, vv, KTB, VB):
    pairs = []
    for r in range(4):
        for t in range(4):
            for c in range(2):
                col0 = r * NTOK + t * TT
                pairs.append((kt[:, c, col0:col0 + TT], X.gath_k[t], P.idx[:, r * 2 + c:r * 2 + c + 1]))
            for tb in range(4):
                kbi = r * 16 + t * 4 + tb
                pairs.append((vv[:, kbi, :], X.gath_v[t], P.idx[:, 8 + r * 4 + tb:8 + r * 4 + tb + 1]))
    P.mk.dma("pool", d_kv, pairs, reads=[P.IDX], writes=[KTB, VB], extra=[X.last])


def emit_core_A(P, X, IN, biasT, tag):
    mk = P.mk
    NQG = SEQ // TT
    mk.stage_boundary()
    with ExitStack() as st:
        kt = P.sb("kt", [128, 2, SEQ], BF16, st)
        KTB = Buf("kt")
        vv = P.sb("vv", [128, NKB, 256], BF16, st)
        VB = Buf("vv")
        d_kv = mk.dsem(f"kv_{tag}", "pool")
        bias = P.sb("biasA", [128, 8 * TT], F32, st)
        BIAS = Buf("biasA")
        d_bias = mk.dsem(f"bias_{tag}")
        qs = [P.sb(f"qs{i}", [128, TT], BF16, st) for i in range(2)]
        QS = [Buf(f"qs{i}") for i in range(2)]
        d_q = [mk.dsem(f"q{i}_{tag}", "pool") for i in range(2)]
        tt_ = [P.sb(f"tA{i}", [128, TT], F32, st) for i in range(12)]
        TTB = [Buf(f"tA{i}") for i in range(12)]
        pp = [P.sb(f"pA{i}", [128, TT], BF16, st) for i in range(12)]
        PP = [Buf(f"pA{i}") for i in range(12)]
        s_ps = [P.ps(f"sA_ps{i}", [128, TT], F32, st) for i in range(4)]
        SPS = [Buf(f"sps{i}") for i in range(4)]
        o_ps = [P.ps(f"oA_ps{i}", [64, TT], F32, st) for i in range(2)]
        OPS = [Buf(f"ops{i}") for i in range(2)]
        l_ps = [P.ps(f"lA_ps{i}", [64, TT], F32, st) for i in range(2)]
        LPS = [Buf(f"lps{i}") for i in range(2)]
        rl = P.sb("rlA", [64, TT], F32, st)
        RL = Buf("rl")
        osb = P.sb("osbA", [64, TT], F32, st)
        OSB = Buf("osb")
        on = [P.sb(f"onA{i}", [64, TT], F32, st) for i in range(2)]
        ON = [Buf(f"on{i}") for i in range(2)]
        d_o = [mk.dsem(f"o{i}_{tag}") for i in range(2)]

        load_kv(P, X, d_kv, kt, vv, KTB, VB)
        it = 0
        kk = 0
        otok = {}
        pipe = Pipe(3, 8)
        for hl in range(4):
            c, u = hl // 2, hl % 2
            prow = slice(u * 64, (u + 1) * 64)
            mk.dma("sp", d_bias, [(bias[:, 0:2048], biasT[hl, :, 0:2048]), (bias[:, 2048:4096], biasT[hl, :, 2048:4096])],
                   reads=[IN], writes=[BIAS])
            for qn, qg in enumerate([rd_ * 4 + t_ for t_ in range(4) for rd_ in range(4)]):
                qi = it % 2
                it += 1
                mk.dma("pool", d_q[qi], [(qs[qi][:], X.gath_q[qg % 4], P.idx[:, (qg // 4) * 2 + c:(qg // 4) * 2 + c + 1])],
                       reads=[P.IDX], writes=[QS[qi]], extra=[X.last])
                kbs = [kb for kb in (3, 4, 0, 1, 2, 5, 6, 7) if qg * 4 - 4 + kb >= 0]
                for n_, kb in enumerate(kbs):
                    kbi = qg * 4 - 4 + kb
                    b = kk % 12
                    sb_ = kk % 4
                    kk += 1
                    first, last = (n_ == 0), (n_ == len(kbs) - 1)
                    cs = slice(64 * max(0, 2 * kb - 8), 64 * (min(7, 2 * kb + 1) + 1))
                    bs = slice(kb * TT + cs.start, kb * TT + cs.stop)
                    mk.op("pe", lambda e: e.matmul(s_ps[sb_][:, cs], kt[prow, c, kbi * 128:(kbi + 1) * 128], qs[qi][prow, cs],
                                                   start=True, stop=True),
                          reads=[KTB, QS[qi]], writes=[SPS[sb_]])
                    mk.op("dve", lambda e: e.tensor_tensor(tt_[b][:, cs], s_ps[sb_][:, cs], bias[:, bs], ALU.add),
                          reads=[SPS[sb_], BIAS], writes=[TTB[b]])
                    mk.op("act", lambda e: e.activation(out=pp[b][:, cs], in_=tt_[b][:, cs], func=AF.Exp),
                          reads=[TTB[b]], writes=[PP[b]])

                    def pv_o(b=b, kbi=kbi, hl=hl, qi=qi, first=first, last=last, cs=cs):
                        mk.op("pe", lambda e: e.matmul(o_ps[qi][:, cs], vv[:, kbi, hl * 64:(hl + 1) * 64], pp[b][:, cs],
                                                       start=first, stop=last),
                              reads=[VB, PP[b]], writes=[OPS[qi]])

                    def pv_l(b=b, qi=qi, first=first, last=last, cs=cs):
                        mk.op("pe", lambda e: e.matmul(l_ps[qi][:, cs], P.ones[:, 0:64], pp[b][:, cs], start=first, stop=last),
                              reads=[P.ONES, PP[b]], writes=[LPS[qi]])
                    pipe.push((0, pv_o, pv_l))

                def epi(qi=qi, qg=qg, c=c, u=u):
                    mk.op("dve", lambda e: e.reciprocal(rl[:], l_ps[qi][:]), reads=[LPS[qi]], writes=[RL])
                    mk.op("act", lambda e: e.activation(out=osb[:], in_=o_ps[qi][:], func=AF.Copy),
                          reads=[OPS[qi]], writes=[OSB])
                    mk.op("dve", lambda e: e.tensor_tensor(on[qi][:], osb[:], rl[:], ALU.mult),
                          reads=[OSB, RL], writes=[ON[qi]])
                    rd = qg // 4
                    otok[qi] = mk.dma("sp", d_o[qi], [(X.snd_o[qg % 4][c][rd * 128 + u * 64:rd * 128 + (u + 1) * 64, :], on[qi][:])],
                                      reads=[ON[qi]])
                pipe.push(epi, 0)
                if u == 1 and qn % 4 == 3:
                    def xchg(t_=qn // 4, c=c):
                        X.last_o = mk.allgather(P.csem, X.snd_o[t_][c], X.gath_o[t_][c], extra=list(otok.values()))
                    pipe.push(xchg, 0)
        pipe.flush()


def emit_core_C(P, X, IN, tabT, tabOff, tabD, lamB, tag, lambda_init):
    mk = P.mk
    NQT = SEQ // TT
    mk.stage_boundary()
    with ExitStack() as st:
        kt = P.sb("ktC", [128, 2, SEQ], BF16, st)
        KTB = Buf("kt")
        vv = P.sb("vvC", [128, NKB, 256], BF16, st)
        VB = Buf("vv")
        d_kv = mk.dsem(f"kv_{tag}", "pool")
        tT = P.sb("tT", [128, 2, TT], F32, st)
        tO = P.sb("tO", [128, 2 * 64], F32, st)
        tD = P.sb("tD", [128, 2, 4 * TT], F32, st)
        lam = P.sb("lam", [128, 256], F32, st)
        TAB = Buf("tabs")
        d_tab = mk.dsem(f"tab_{tag}")
        lp = P.sb("lamp", [128, 128], F32, st)
        LP = Buf("lp")
        lsum = P.sb("lsum", [128, 2], F32, st)
        LS = Buf("ls")
        lexp = P.sb("lexp", [128, 2], F32, st)
        LE = Buf("le")
        nlam = P.sb("nlam", [128, 1], F32, st)
        NL = Buf("nl")
        qs = [P.sb(f"qsC{i}", [128, TT], BF16, st) for i in range(2)]
        QS = [Buf(f"qs{i}") for i in range(2)]
        d_q = [mk.dsem(f"q{i}_{tag}", "pool") for i in range(2)]
        tt_ = [P.sb(f"tC{i}", [128, TT], F32, st) for i in range(12)]
        TTB = [Buf(f"tC{i}") for i in range(12)]
        pp = [P.sb(f"pC{i}", [128, TT], BF16, st) for i in range(12)]
        PP = [Buf(f"pC{i}") for i in range(12)]
        s_ps = [P.ps(f"sC_ps{i}", [128, TT], F32, st) for i in range(4)]
        SPS = [Buf(f"sps{i}") for i in range(4)]
        o_ps = [P.ps(f"oC_ps{i}", [128, TT], F32, st) for i in range(2)]
        OPS = [Buf(f"ops{i}") for i in range(2)]
        l_ps = [P.ps(f"lC_ps{i}", [128, TT], F32, st) for i in range(2)]
        LPS = [Buf(f"lps{i}") for i in range(2)]
        rl = [P.sb(f"rlC{i}", [128, TT], F32, st) for i in range(2)]
        RL = [Buf(f"rl{i}") for i in range(2)]
        osb = [P.sb(f"osbC{i}", [128, TT], F32, st) for i in range(2)]
        OSB = [Buf(f"osb{i}") for i in range(2)]
        am = [P.sb(f"amC{i}", [128, TT], F32, st) for i in range(2)]
        AM = [Buf(f"am{i}") for i in range(2)]
        on = [P.sb(f"onC{i}", [128, TT], F32, st) for i in range(2)]
        ON = [Buf(f"on{i}") for i in range(2)]
        d_o = [mk.dsem(f"o{i}_{tag}") for i in range(2)]

        mk.dma("sp", d_tab, [(tT[:, 0, :], tabT[0]), (tT[:, 1, :], tabT[1]), (tO[:], tabOff),
                             (tD[:, 0, :], tabD[0]), (tD[:, 1, :], tabD[1]), (lam[:], lamB)],
               reads=[IN], writes=[TAB])
        load_kv(P, X, d_kv, kt, vv, KTB, VB)
        mk.op("dve", lambda e: e.tensor_tensor(lp[:, 0:64], lam[:, 0:64], lam[:, 64:128], ALU.mult),
              reads=[TAB], writes=[LP])
        mk.op("dve", lambda e: e.tensor_tensor(lp[:, 64:128], lam[:, 128:192], lam[:, 192:256], ALU.mult),
              reads=[TAB, LP], writes=[LP])
        mk.op("dve", lambda e: e.reduce_sum(lsum[:, 0:1], lp[:, 0:64], mybir.AxisListType.X), reads=[LP], writes=[LS])
        mk.op("dve", lambda e: e.reduce_sum(lsum[:, 1:2], lp[:, 64:128], mybir.AxisListType.X), reads=[LP, LS], writes=[LS])
        mk.op("act", lambda e: e.activation(out=lexp[:], in_=lsum[:], func=AF.Exp), reads=[LS], writes=[LE])
        mk.op("dve", lambda e: e.tensor_tensor(nlam[:], lexp[:, 1:2], lexp[:, 0:1], ALU.subtract), reads=[LE], writes=[NL])
        mk.op("dve", lambda e: e.tensor_scalar(nlam[:], nlam[:], -float(lambda_init), None, ALU.add), reads=[NL], writes=[NL])

        it = 0
        kk = 0
        otok = {}
        pipe = Pipe(3, 8)
        for hl in range(2):
            for qn, qt in enumerate([rd_ * 4 + t_ for t_ in range(4) for rd_ in range(4)]):
                qi = it % 2
                it += 1
                mk.dma("pool", d_q[qi], [(qs[qi][:], X.gath_q[qt % 4], P.idx[:, (qt // 4) * 2 + hl:(qt // 4) * 2 + hl + 1])],
                       reads=[P.IDX], writes=[QS[qi]], extra=[X.last])
                nkb = 4 * qt + 4
                for kb in range(nkb):
                    diag = kb >= 4 * qt
                    first, last = (kb == 0), (kb == nkb - 1)
                    for m in range(2):
                        b = kk % 12
                        sb_ = kk % 4
                        kk += 1
                        prow = slice(m * 64, (m + 1) * 64)
                        mk.op("pe", lambda e: e.matmul(s_ps[sb_][:], kt[prow, hl, kb * 128:(kb + 1) * 128], qs[qi][prow, :],
                                                       start=True, stop=True),
                              reads=[KTB, QS[qi]], writes=[SPS[sb_]])
                        if diag:
                            j = kb - 4 * qt
                            mk.op("dve", lambda e: e.tensor_tensor(tt_[b][:], s_ps[sb_][:], tD[:, hl, j * TT:(j + 1) * TT], ALU.add),
                                  reads=[SPS[sb_], TAB], writes=[TTB[b]])
                            mk.op("act", lambda e: e.activation(out=pp[b][:], in_=tt_[b][:], func=AF.Exp),
                                  reads=[TTB[b]], writes=[PP[b]])
                        else:
                            n = 4 * qt - kb
                            mk.op("dve", lambda e: e.tensor_tensor(tt_[b][:], s_ps[sb_][:], tT[:, hl, :], ALU.add),
                                  reads=[SPS[sb_], TAB], writes=[TTB[b]])
                            mk.op("act", lambda e: e.activation(out=pp[b][:], in_=tt_[b][:], func=AF.Exp,
                                                                bias=tO[:, hl * 64 + n:hl * 64 + n + 1]),
                                  reads=[TTB[b], TAB], writes=[PP[b]])

                        def pv_o(b=b, kb=kb, hl=hl, m=m, first=first, last=last):
                            mk.op("pe", lambda e: e.matmul(o_ps[m][:], vv[:, kb, hl * 128:(hl + 1) * 128], pp[b][:],
                                                           start=first, stop=last),
                                  reads=[VB, PP[b]], writes=[OPS[m]])

                        def pv_l(b=b, m=m, first=first, last=last):
                            mk.op("pe", lambda e: e.matmul(l_ps[m][:], P.ones[:], pp[b][:], start=first, stop=last),
                                  reads=[P.ONES, PP[b]], writes=[LPS[m]])
                        pipe.push((m, pv_o, pv_l))

                def epi(qi=qi, qt=qt, hl=hl):
                    for m in range(2):
                        mk.op("dve", lambda e: e.reciprocal(rl[m][:], l_ps[m][:]), reads=[LPS[m]], writes=[RL[m]])
                        mk.op("act", lambda e: e.activation(out=osb[m][:], in_=o_ps[m][:], func=AF.Copy),
                              reads=[OPS[m]], writes=[OSB[m]])
                        mk.op("dve", lambda e: e.tensor_tensor(am[m][:], osb[m][:], rl[m][:], ALU.mult),
                              reads=[OSB[m], RL[m]], writes=[AM[m]])
                    mk.op("dve", lambda e: e.scalar_tensor_tensor(on[qi][:], am[1][:], nlam[:, 0:1], am[0][:], ALU.mult, ALU.add),
                          reads=[AM[0], AM[1], NL], writes=[ON[qi]])
                    rd = qt // 4
                    otok[qi] = mk.dma("sp", d_o[qi], [(X.snd_o[qt % 4][hl][rd * 128:(rd + 1) * 128, :], on[qi][:])],
                                      reads=[ON[qi]])
                pipe.push(epi, 0)
                if qn % 4 == 3:
                    def xchg(t_=qn // 4, hl=hl):
                        X.last_o = mk.allgather(P.csem, X.snd_o[t_][hl], X.gath_o[t_][hl], extra=list(otok.values()))
                    pipe.push(xchg, 0)
        pipe.flush()


def emit_post(P, l, X, IN, w_o_ap, tag, sublnT=None, lambda_init=None):
    mk = P.mk
    mk.stage_boundary()
    with ExitStack() as st:
        d_w = mk.dsem(f"wo_{tag}", "pool")
        wo, WO = load_wres(P, st, "wo", w_o_ap, DC, d_w)
        ws = NormWS(P, st, tag)
        pw = ProjWS(P, st, tag)
        ob = [P.sb(f"ob{i}", [128, DC, TT], F32, st) for i in range(2)]
        OB = [Buf(f"ob{i}") for i in range(2)]
        d_ob = [mk.dsem(f"ob{i}_{tag}", "pool") for i in range(2)]
        onb = P.sb("onb", [128, DC, TT], BF16, st)
        ONB = [Buf(f"onb{dc}") for dc in range(DC)]
        if sublnT is not None:
            sgc = P.sb("sgc", [128, 1], F32, st)
            SGC = Buf("sgc")
            d_sg = mk.dsem(f"sg_{tag}")
            mk.dma("sp", d_sg, [(sgc[:], sublnT)], reads=[IN], writes=[SGC])
            mk.op("dve", lambda e: e.tensor_scalar(sgc[:], sgc[:], float(1.0 - lambda_init), None, ALU.mult),
                  reads=[SGC], writes=[SGC])
        for t in range(NT):
            i = t % 2
            mk.dma("pool", d_ob[i], [(ob[i][:, 2 * r + c, :], X.gath_o[t][c], P.idx[:, 24 + r:25 + r])
                                     for r in range(4) for c in range(2)],
                   reads=[P.IDX], writes=[OB[i]], extra=[X.last_o])
            for fc in range(DC):
                if sublnT is None:
                    if fc % 2 == 0:
                        mk.op("dve", lambda e: e.tensor_copy(onb[:, fc, :], ob[i][:, fc, :]), reads=[OB[i]], writes=[ONB[fc]])
                    else:
                        mk.op("act", lambda e: e.activation(out=onb[:, fc, :], in_=ob[i][:, fc, :], func=AF.Copy),
                              reads=[OB[i]], writes=[ONB[fc]])
                else:
                    si = ws.square(ob[i][:, fc, :], [OB[i]])
                    ws.accum(si, True, True)
                    ws.finish(128)
                    mk.op("dve", lambda e: e.scalar_tensor_tensor(onb[:, fc, :], ob[i][:, fc, :], sgc[:, 0:1],
                                                                  ws.rstd[:], ALU.mult, ALU.mult),
                          reads=[OB[i], SGC, ws.RSTD], writes=[ONB[fc]])
            emit_proj_norm_res(P, ws, pw, t, onb, ONB, DC, wo, WO, l, 3, False)


def emit_mixer_B(P, l, w_in_ap, lngT, lnbT, wsT, triu, bsR, w_o_ap, IN, tag):
    mk = P.mk
    mk.stage_boundary()
    with ExitStack() as st:
        d_w = mk.dsem(f"wiB_{tag}", "pool")
        wi = P.sb("wiB", [128, DC, 2 * D], BF16, st)
        WI = Buf("wiB")
        wv = w_in_ap.rearrange("(kc p) c -> p kc c", p=128)
        mk.dma("pool", d_w, [(wi[:, :, i * 512:(i + 1) * 512], wv[:, :, i * 512:(i + 1) * 512]) for i in range(4)],
               writes=[WI])
        d_wo = mk.dsem(f"woB_{tag}", "pool")
        wo, WO = load_wres(P, st, "woB", w_o_ap, DC, d_wo)
        d_c = mk.dsem(f"cB_{tag}")
        lng = P.sb("lng", [128, 8], F32, st)
        lnb = P.sb("lnb", [128, 8], F32, st)
        addt = P.sb("addt", [128, 8 * 128], F32, st)
        ADDT = Buf("addt")
        wsb = P.sb("wsb", [128, 8 * 128], BF16, st)
        WSB = Buf("wsb")
        CB = Buf("cB")
        ps = [P.ps(f"B_ps{i}", [128, TT], F32, st) for i in range(4)]
        PS = [Buf(f"Bps{i}") for i in range(4)]
        with ExitStack() as st2:
            wsf = P.sb("wsf", [128, 8 * 128], F32, st2)
            tri = P.sb("tri", [128, 128], F32, st2)
            bsr = P.sb("bsr", [128, 8 * 128], F32, st2)
            mk.dma("sp", d_c, [(lng[:], lngT), (lnb[:], lnbT), (wsf[:], wsT), (tri[:], triu), (bsr[:], bsR)],
                   reads=[IN], writes=[CB])
            for g in range(8):
                gs = slice(g * 128, (g + 1) * 128)
                mk.op("dve", lambda e: e.tensor_tensor(wsb[:, gs], wsf[:, gs], tri[:], ALU.mult),
                      reads=[CB, WSB], writes=[WSB])
            for g in range(8):
                gs = slice(g * 128, (g + 1) * 128)
                b = g % 4
                mk.op("pe", lambda e: e.matmul(ps[b][:, 0:128], P.ones[:], wsb[:, gs], start=True, stop=True),
                      reads=[P.ONES, WSB], writes=[PS[b]])
                mk.op("dve", lambda e: e.scalar_tensor_tensor(addt[:, gs], ps[b][:, 0:128], lnb[:, g:g + 1], bsr[:, gs],
                                                              ALU.mult, ALU.add),
                      reads=[PS[b], CB, ADDT], writes=[ADDT])
        mk.stage_boundary()
        ws = NormWS(P, st, tag)
        pw = ProjWS(P, st, tag)
        hn = P.sb("hnB", [128, DC, TT], BF16, st)
        HN = [Buf(f"hn{dc}") for dc in range(DC)]
        u = P.sb("uB", [128, DC, TT], F32, st)
        U = [Buf(f"u{dc}") for dc in range(DC)]
        vf = P.sb("vfB", [128, D], F32, st)
        VF = Buf("vf")
        vn = P.sb("vnB", [128, 4, D], BF16, st)
        VN = [Buf(f"vn{tb}") for tb in range(4)]
        st4 = P.sb("st4", [128, 8], F32, st)
        ST4 = Buf("st4")
        junk = P.sb("junkB", [128, D], F32, st)
        JK = Buf("junk")
        t1 = [P.sb(f"t1B{i}", [128, TT], F32, st) for i in range(2)]
        T1 = [Buf(f"t1{i}") for i in range(2)]
        k = 0
        for t in range(NT):
            emit_norm_in(P, ws, t, l, 2, hn, HN)
            for fc in range(DC):
                b = k % 4
                k += 1
                for kc in range(DC):
                    mk.op("pe", lambda e: e.matmul(ps[b][:], wi[:, kc, fc * 128:(fc + 1) * 128], hn[:, kc, :],
                                                   start=(kc == 0), stop=(kc == DC - 1)),
                          reads=[WI, HN[kc]], writes=[PS[b]])
                mk.op("act", lambda e: e.activation(out=u[:, fc, :], in_=ps[b][:], func=AF.Gelu),
                      reads=[PS[b]], writes=[U[fc]])
            for tb in range(4):
                for hf in range(2):
                    b = k % 4
                    k += 1
                    for kc in range(DC):
                        mk.op("pe", lambda e: e.matmul(ps[b][:], hn[:, kc, tb * 128:(tb + 1) * 128],
                                                       wi[:, kc, D + hf * 512:D + (hf + 1) * 512],
                                                       start=(kc == 0), stop=(kc == DC - 1)),
                              reads=[WI, HN[kc]], writes=[PS[b]])
                    mk.op("act", lambda e: e.activation(out=vf[:, hf * 512:(hf + 1) * 512], in_=ps[b][:], func=AF.Gelu),
                          reads=[PS[b], VF], writes=[VF])
                mk.op("dve", lambda e: e.reduce_sum(st4[:, 0:1], vf[:], mybir.AxisListType.X),
                      reads=[VF, ST4], writes=[ST4])
                mk.op("act", lambda e: e.activation(out=junk[:], in_=vf[:], func=AF.Square),
                      reads=[VF, JK], writes=[JK])
                mk.op("dve", lambda e: e.reduce_sum(st4[:, 1:2], junk[:], mybir.AxisListType.X),
                      reads=[JK, ST4], writes=[ST4])
                mk.op("dve", lambda e: e.tensor_scalar(st4[:, 2:3], st4[:, 0:1], 1.0 / D, None, ALU.mult),
                      reads=[ST4], writes=[ST4])
                mk.op("dve", lambda e: e.tensor_tensor(st4[:, 5:6], st4[:, 2:3], st4[:, 2:3], ALU.mult),
                      reads=[ST4], writes=[ST4])
                mk.op("dve", lambda e: e.scalar_tensor_tensor(st4[:, 6:7], st4[:, 1:2], 1.0 / D, st4[:, 5:6],
                                                              ALU.mult, ALU.subtract),
                      reads=[ST4], writes=[ST4])
                mk.op("act", lambda e: e.activation(out=st4[:, 7:8], in_=st4[:, 6:7], func=AF.Sqrt, bias=P.epsb[:], scale=1.0),
                      reads=[ST4, P.EPSB], writes=[ST4])
                mk.op("dve", lambda e: e.reciprocal(st4[:, 3:4], st4[:, 7:8]), reads=[ST4], writes=[ST4])
                mk.op("dve", lambda e: e.scalar_tensor_tensor(st4[:, 4:5], st4[:, 2:3], -1.0, st4[:, 3:4], ALU.mult, ALU.mult),
                      reads=[ST4], writes=[ST4])
                mk.op("act", lambda e: e.activation(out=vn[:, tb, :], in_=vf[:], func=AF.Identity,
                                                    bias=st4[:, 4:5], scale=st4[:, 3:4]),
                      reads=[VF, ST4], writes=[VN[tb]])
            for g in range(8):
                gs = slice(g * 128, (g + 1) * 128)
                b = k % 4
                k += 1
                i = g % 2
                for tb in range(4):
                    mk.op("pe", lambda e: e.matmul(ps[b][:, tb * 128:(tb + 1) * 128], vn[:, tb, gs],
                                                   wsb[:, gs], start=True, stop=True),
                          reads=[VN[tb], WSB, PS[b]], writes=[PS[b]])
                for tb in range(4):
                    tbs = slice(tb * 128, (tb + 1) * 128)
                    mk.op("dve", lambda e: e.scalar_tensor_tensor(t1[i][:, tbs], ps[b][:, tbs], lng[:, g:g + 1],
                                                                  addt[:, gs], ALU.mult, ALU.add),
                          reads=[PS[b], CB, ADDT, T1[i]], writes=[T1[i]])
                mk.op("dve", lambda e: e.tensor_tensor(hn[:, g, :], t1[i][:], u[:, g, :], ALU.mult),
                      reads=[T1[i], U[g]], writes=[HN[g]])
            emit_proj_norm_res(P, ws, pw, t, hn, HN, DC, wo, WO, l, 3, False)


def load_x(P, xT_ap):
    mk = P.mk
    P.d_x = mk.dsem("xin")
    v = xT_ap.rearrange("(dc p) t -> p dc t", p=128)
    mk.dma("sp", P.d_x, [(P.xs[:, dc, :], v[:, dc, :]) for dc in range(DC)],
           writes=[b for dc in range(DC) for b in P.X[dc]])


def store_x(P, oT_ap):
    mk = P.mk
    P.d_xo = mk.dsem("xout")
    v = oT_ap.rearrange("(dc p) t -> p dc t", p=128)
    return mk.dma("sp", P.d_xo, [(v[:, dc, :], P.xs[:, dc, :]) for dc in range(DC)],
                  reads=[b for dc in range(DC) for b in P.X[dc]])


I32 = mybir.dt.int32
WNAMES = ["ff1_w_in", "ff1_w_out", "ff2_w_in", "ff2_w_out", "a_w_qkv", "a_w_o", "b_w_in", "b_w_o", "c_w_qkv", "c_w_o"]


class XChg:
    def __init__(self, nc, l):
        def dt_(name, shape, dt):
            return nc.dram_tensor(f"{name}_L{l}", list(shape), dt).ap()
        self.snd_q = [dt_(f"sq{t}", [D, TT], BF16) for t in range(NT)]
        self.snd_k = [dt_(f"sk{t}", [D, TT], BF16) for t in range(NT)]
        self.snd_v = [dt_(f"sv{t}", [4 * TT, 256], BF16) for t in range(NT)]
        self.gath_q = [dt_(f"gq{t}", [4 * D, TT], BF16) for t in range(NT)]
        self.gath_k = [dt_(f"gk{t}", [4 * D, TT], BF16) for t in range(NT)]
        self.gath_v = [dt_(f"gv{t}", [16 * TT, 256], BF16) for t in range(NT)]
        self.snd_o = [[dt_(f"so{t}_{c}", [4 * 128, TT], F32) for c in range(2)] for t in range(NT)]
        self.gath_o = [[dt_(f"go{t}_{c}", [16 * 128, TT], F32) for c in range(2)] for t in range(NT)]
        self.last = None
        self.last_o = None


def build_fused(layers=(0, 1, 2, 3), ffn=True):
    nc = bass.Bass("TRN2", target_bir_lowering=False)

    def ext(name, shape, dt=F32):
        return nc.dram_tensor(name, list(shape), dt, kind="ExternalInput").ap()
    xT = ext("xT", [D, NTOK])
    gT = ext("gT", [128, DEPTH * 6 * DC])
    idx = ext("idx", [128, 28], I32)
    W = {"ff1_w_in": ext("ff1_w_in", [DEPTH, D, 2 * DFF]), "ff1_w_out": ext("ff1_w_out", [DEPTH, DFF, D]),
         "ff2_w_in": ext("ff2_w_in", [DEPTH, D, 2 * DFF]), "ff2_w_out": ext("ff2_w_out", [DEPTH, DFF, D]),
         "a_w_qkv": ext("a_w_qkv", [2, D, 3 * D]), "a_w_o": ext("a_w_o", [2, D, D]),
         "b_w_in": ext("b_w_in", [1, D, 2 * D]), "b_w_o": ext("b_w_o", [1, D, D]),
         "c_w_qkv": ext("c_w_qkv", [1, D, 3 * D]), "c_w_o": ext("c_w_o", [1, D, D])}
    biasT = [ext("biasT0", [4, 128, 8 * TT]), ext("biasT1", [4, 128, 8 * TT])]
    tabT = ext("tabT", [2, 128, TT])
    tabOff = ext("tabOff", [128, 128])
    tabD = ext("tabD", [2, 128, 4 * TT])
    lamB = ext("lamB", [128, 256])
    sublnT = ext("sublnT", [128, 1])
    lngT = ext("lngT", [128, 8])
    lnbT = ext("lnbT", [128, 8])
    wsT = ext("wsT", [128, 8 * 128])
    triu = ext("triu", [128, 128])
    bsR = ext("bsR", [128, 8 * 128])
    xTo = nc.dram_tensor("xTo", [D, NTOK], F32, kind="ExternalOutput").ap()
    with ExitStack() as stack:
        mk = MK(nc, stack)
        P = Prog(nc, stack, mk)
        setup_common(P, gT)
        P.idx = P.sb("idx", [128, 28], I32)
        P.IDX = Buf("idx")
        mk.dma("sp", mk.dsem("idx"), [(P.idx[:], idx)], writes=[P.IDX])
        csem_h = stack.enter_context(nc.semaphore("csem"))
        P.csem = DSem("csem", csem_h)
        load_x(P, xT)
        IN = Buf("in")
        for l in layers:
            kind, jm = l % 3, l // 3
            if ffn:
                with nc.named_scope(f"L{l}_ffn1"):
                    emit_ffn(P, l, 0, W["ff1_w_in"][l], W["ff1_w_out"][l])
            if kind == 1:
                with nc.named_scope(f"L{l}_mixB"):
                    emit_mixer_B(P, l, W["b_w_in"][jm], lngT, lnbT, wsT, triu, bsR, W["b_w_o"][jm], IN, f"mB{l}")
            else:
                X = XChg(nc, l)
                if kind == 0:
                    with nc.named_scope(f"L{l}_pre"):
                        emit_pre(P, l, W["a_w_qkv"][jm], X, f"pre{l}")
                    with nc.named_scope(f"L{l}_core"):
                        emit_core_A(P, X, IN, biasT[jm], f"cA{l}")
                    with nc.named_scope(f"L{l}_post"):
                        emit_post(P, l, X, IN, W["a_w_o"][jm], f"post{l}")
                else:
                    li = 0.8 - 0.6 * float(np.exp(-0.3 * l))
                    with nc.named_scope(f"L{l}_pre"):
                        emit_pre(P, l, W["c_w_qkv"][jm], X, f"pre{l}")
                    with nc.named_scope(f"L{l}_core"):
                        emit_core_C(P, X, IN, tabT, tabOff, tabD, lamB, f"cC{l}", li)
                    with nc.named_scope(f"L{l}_post"):
                        emit_post(P, l, X, IN, W["c_w_o"][jm], f"post{l}", sublnT, li)
            if ffn:
                with nc.named_scope(f"L{l}_ffn2"):
                    emit_ffn(P, l, 1, W["ff2_w_in"][l], W["ff2_w_out"][l])
        mk.wait_all("sp", [store_x(P, xTo)])
    return nc


def gains_T(norm_g_l):
    out = np.zeros((128, DEPTH * 6 * DC), np.float32)
    out[:, :6 * DC] = norm_g_l.reshape(6, DC, 128).transpose(2, 0, 1).reshape(128, 6 * DC)
    return out


def feat_cols(v, n):
    return np.ascontiguousarray(v.reshape(n, 128).T)


def bias_tables_A(rel_bias, j):
    k_in = np.arange(128)[:, None, None]
    kb = np.arange(8)[None, :, None]
    q_rel = np.arange(512)[None, None, :]
    krel = kb * 128 + k_in
    valid = (krel // 64 >= q_rel // 64) & (krel // 64 <= q_rel // 64 + 8)
    idx = np.clip(q_rel + 512 - krel, -128, 128) + 128
    out = np.empty((4, 128, 8, 512), np.float32)
    for hl in range(4):
        g = rel_bias[4 * j + hl][idx]
        out[hl] = np.where(valid, g, np.float32(NEG))
    return out.reshape(4, 128, 8 * 512)


def tables_C(j):
    tabT = np.empty((2, 128, 512), np.float32)
    tabOff = np.empty((128, 128), np.float32)
    tabD = np.empty((2, 128, 4, 512), np.float32)
    k_in = np.arange(128)[:, None]
    q_rel = np.arange(512)[None, :]
    for hl in range(2):
        h = 2 * j + hl
        slope = np.float32(2.0 ** (-(h + 1)))
        tabT[hl] = -slope * (q_rel - k_in).astype(np.float32)
        tabOff[:, hl * 64:(hl + 1) * 64] = (-slope * 128.0 * np.arange(64, dtype=np.float32))[None, :]
        for jj in range(4):
            k_rel = jj * 128 + k_in
            allowed = k_rel < (q_rel // 64 + 1) * 64
            tabD[hl, :, jj, :] = np.where(allowed, -slope * np.abs(q_rel - k_rel).astype(np.float32), np.float32(NEG))
    return tabT, tabOff, tabD.reshape(2, 128, 4 * 512)


def gains_all(norm_g):
    return np.ascontiguousarray(norm_g.reshape(DEPTH, 6, DC, 128).transpose(3, 0, 1, 2).reshape(128, DEPTH * 6 * DC))


def idx_table(j):
    p = np.arange(128, dtype=np.int32)
    t = np.zeros((128, 28), np.int32)
    for r in range(4):
        for c in range(2):
            t[:, r * 2 + c] = r * 1024 + 256 * j + c * 128 + p
        for tb in range(4):
            t[:, 8 + r * 4 + tb] = r * 2048 + j * 512 + tb * 128 + p
        t[:, 24 + r] = r * 512 + j * 128 + p
    return t


def make_in_maps(inp):
    x = inp["x"]
    shared = {k: inp[k] for k in WNAMES}
    ws = inp["b_w_s"][0]
    shared.update({
        "gT": gains_all(inp["norm_g"]),
        "lamB": np.ascontiguousarray(np.broadcast_to(inp["c_lambda"][0].reshape(1, 256), (128, 256))),
        "sublnT": np.ascontiguousarray(inp["c_subln_g"][0].reshape(128, 1)),
        "lngT": feat_cols(inp["b_ln_g"][0], 8), "lnbT": feat_cols(inp["b_ln_b"][0], 8),
        "wsT": np.ascontiguousarray(ws.transpose(2, 0, 1).reshape(128, 8 * 128)),
        "triu": np.triu(np.ones((128, 128), np.float32)),
        "bsR": np.ascontiguousarray(np.broadcast_to(inp["b_b_s"][0].reshape(1, 8 * 128), (128, 8 * 128))),
    })
    in_maps = []
    for c in range(NCORES):
        b, j = c // 4, c % 4
        tabT, tabOff, tabD = tables_C(j)
        m = dict(shared)
        m.update({"xT": np.ascontiguousarray(x[b, NTOK * j:NTOK * (j + 1), :].T), "idx": idx_table(j),
                  "biasT0": bias_tables_A(inp["a_rel_bias"][0], j), "biasT1": bias_tables_A(inp["a_rel_bias"][1], j),
                  "tabT": tabT, "tabOff": tabOff, "tabD": tabD})
        in_maps.append(m)
    return in_maps


def kernel(**inputs):
    inp = {k: np.ascontiguousarray(np.asarray(v)) for k, v in inputs.items()}
    x = inp["x"]
    nc = build_fused()
    in_maps = make_in_maps(inp)
    res = run_bass_kernel_spmd(nc, in_maps, core_ids=list(range(NCORES)))
    out = np.empty_like(x)
    for c in range(NCORES):
        b, j = c // 4, c % 4
        out[b, NTOK * j:NTOK * (j + 1), :] = res.results[c]["xTo"].T
    return out
```

```python
import numpy as np
import concourse.bass as bass
import concourse.mybir as mybir
from concourse.bass_utils import run_bass_kernel_spmd

F32 = mybir.dt.float32
BF16 = mybir.dt.bfloat16
AF = mybir.ActivationFunctionType
ALU = mybir.AluOpType

D = 1024
DC = 8
DFF = 2816
FC = 22
NTOK = 2048
TT = 512
NT = NTOK // TT
EPS = 1e-6
DEPTH = 4
NCORES = 8


_FENCE = {}


class Buf:
    __slots__ = ("name", "w", "r")

    def __init__(self, name):
        self.name = name
        self.w = None
        self.r = dict(_FENCE)


class DSem:
    def __init__(self, key, sem):
        self.key = key
        self.sem = sem
        self.cnt = 0


class EngState:
    def __init__(self, name, eng, sem):
        self.name = name
        self.eng = eng
        self.sem = sem
        self.cnt = 0
        self.waited = {}


class _PEProxy:
    def __init__(self, eng):
        self.eng = eng
        self.last_stop = True

    def matmul(self, *a, **kw):
        self.last_stop = bool(kw.get("stop", True))
        return self.eng.matmul(*a, **kw)


class MK:
    def __init__(self, nc, stack):
        self.nc = nc
        self.stack = stack
        self.engs = {}
        for name, eng in (("pe", nc.tensor), ("act", nc.scalar), ("dve", nc.vector),
                          ("pool", nc.gpsimd), ("sp", nc.sync)):
            sem = stack.enter_context(nc.semaphore("s_" + name))
            self.engs[name] = EngState(name, eng, sem)
        self.dsems = []
        self.free_dsems = {"sp": [], "pool": []}
        self.stage_dsems = []
        _FENCE.clear()

    def dsem(self, name, q="sp"):
        if self.free_dsems[q]:
            d = self.free_dsems[q].pop()
        else:
            sem = self.stack.enter_context(self.nc.semaphore("d_%d" % len(self.dsems)))
            d = DSem("d_%d" % len(self.dsems), sem)
            d.q = q
            self.dsems.append(d)
        self.stage_dsems.append(d)
        return d

    def stage_boundary(self):
        for d in self.stage_dsems:
            self.free_dsems[d.q].append(d)
        self.stage_dsems = []
        _FENCE.clear()
        for E in self.engs.values():
            if E.cnt > 0:
                _FENCE[E.name] = (E.name, E.sem, E.cnt)
        for d in self.dsems:
            if d.cnt > 0:
                _FENCE[d.key] = (d.key, d.sem, d.cnt)

    def _wait(self, E, toks):
        need = {}
        for tok in toks:
            if tok is None:
                continue
            key, sem, val = tok
            if key == "pe" and E.name == "pe":
                continue
            if key not in need or need[key][1] < val:
                need[key] = (sem, val)
        for key, (sem, val) in need.items():
            if E.waited.get(key, 0) < val:
                E.eng.wait_ge(sem, val)
                E.waited[key] = val

    def _deps(self, reads, writes):
        toks = []
        for b in reads:
            toks.append(b.w)
        for b in writes:
            toks.append(b.w)
            toks.extend(b.r.values())
        return toks

    def op(self, ename, fn, reads=(), writes=(), force_inc=False):
        E = self.engs[ename]
        self._wait(E, self._deps(reads, writes))
        if ename == "pe":
            prox = _PEProxy(E.eng)
            ins = fn(prox)
            if not prox.last_stop and not force_inc:
                tok = (ename, E.sem, E.cnt + 1)
                for b in reads:
                    b.r[ename] = tok
                for b in writes:
                    b.w = tok
                    b.r = {}
                return tok
        else:
            ins = fn(E.eng)
        E.cnt += 1
        ins.then_inc(E.sem, 1)
        tok = (ename, E.sem, E.cnt)
        for b in reads:
            b.r[ename] = tok
        for b in writes:
            b.w = tok
            b.r = {}
        return tok

    def dma(self, qname, dsem, pairs, reads=(), writes=(), extra=(), **kw):
        E = self.engs[qname]
        assert dsem.q == qname, (dsem.key, dsem.q, qname)
        self._wait(E, self._deps(reads, writes) + list(extra))
        for pr in pairs:
            if len(pr) == 3:
                E.eng.indirect_dma_start(out=pr[0], out_offset=None, in_=pr[1],
                                         in_offset=bass.IndirectOffsetOnAxis(ap=pr[2], axis=0)).then_inc(dsem.sem, 16)
            else:
                E.eng.dma_start(out=pr[0], in_=pr[1], **kw).then_inc(dsem.sem, 16)
            dsem.cnt += 16
        tok = (dsem.key, dsem.sem, dsem.cnt)
        for b in reads:
            b.r[dsem.key] = tok
        for b in writes:
            b.w = tok
            b.r = {}
        return tok

    def allgather(self, csem, src_ap, dst_ap, extra=(), writes=()):
        E = self.engs["pool"]
        self._wait(E, self._deps((), writes) + list(extra))
        E.eng.collective_compute("AllGather", ALU.bypass, replica_groups=[[0, 1, 2, 3], [4, 5, 6, 7]],
                                 ins=[src_ap], outs=[dst_ap]).then_inc(csem.sem, 1)
        csem.cnt += 1
        tok = (csem.key, csem.sem, csem.cnt)
        for b in writes:
            b.w = tok
            b.r = {}
        return tok

    def wait_all(self, ename, toks):
        self._wait(self.engs[ename], toks)


from contextlib import ExitStack

SEQ = 8192
NKB = SEQ // 128
NEG = -30000.0
LAMBDA_INIT_L2 = 0.8 - 0.6 * float(np.exp(-0.3 * 2))


class Prog:
    def __init__(self, nc, stack, mk):
        self.nc = nc
        self.stack = stack
        self.mk = mk

    def sb(self, name, shape, dt, stack=None):
        self.uid = getattr(self, "uid", 0) + 1
        return (stack or self.stack).enter_context(self.nc.sbuf_tensor(f"{name}_{self.uid}", shape, dt))

    def ps(self, name, shape, dt=F32, stack=None):
        self.uid = getattr(self, "uid", 0) + 1
        return (stack or self.stack).enter_context(self.nc.psum_tensor(f"{name}_{self.uid}", shape, dt))


def gcol(l, n, dc):
    return (l * 6 + n) * DC + dc


def setup_common(P, gT_ap):
    nc, mk = P.nc, P.mk
    P.xs = P.sb("xs", [128, DC, NTOK], F32)
    P.X = [[Buf(f"x{dc}_{t}") for t in range(NT)] for dc in range(DC)]
    P.ones = P.sb("ones", [128, 128], BF16)
    P.ONES = Buf("ones")
    P.epsb = P.sb("epsb", [128, 1], F32)
    P.EPSB = Buf("epsb")
    P.g = P.sb("g", [128, DEPTH * 6 * DC], F32)
    P.hg = P.sb("hg", [128, DEPTH * 6 * DC], F32)
    P.G = Buf("g")
    P.HG = Buf("hg")
    P.d_const = mk.dsem("const")
    mk.op("dve", lambda e: e.memset(P.ones[:], 1.0), writes=[P.ONES])
    mk.op("dve", lambda e: e.memset(P.epsb[:], EPS), writes=[P.EPSB])
    mk.dma("sp", P.d_const, [(P.g[:], gT_ap)], writes=[P.G])
    mk.op("dve", lambda e: e.tensor_scalar(P.hg[:], P.g[:], 0.5, None, ALU.mult),
          reads=[P.G], writes=[P.HG])


class NormWS:
    def __init__(self, P, st, tag):
        self.P = P
        self.sq = [P.sb(f"sq{i}_{tag}", [128, TT], BF16, st) for i in range(2)]
        self.SQ = [Buf(f"sq{i}") for i in range(2)]
        self.rt = P.sb(f"rt_{tag}", [128, TT], F32, st)
        self.RT = Buf("rt")
        self.rstd = P.sb(f"rstd_{tag}", [128, TT], F32, st)
        self.RSTD = Buf("rstd")
        self.ss_ps = P.ps(f"ss_ps_{tag}", [128, TT], F32, st)
        self.SSPS = Buf("ssps")
        self.k = 0

    def square(self, src_ap, SRC):
        i = self.k % 2
        self.k += 1
        self.P.mk.op("act", lambda e: e.activation(out=self.sq[i][:], in_=src_ap, func=AF.Square),
                     reads=SRC, writes=[self.SQ[i]])
        return i

    def accum(self, i, first, last):
        P = self.P
        P.mk.op("pe", lambda e: e.matmul(self.ss_ps[:], P.ones[:], self.sq[i][:], start=first, stop=last),
                reads=[P.ONES, self.SQ[i]], writes=[self.SSPS], force_inc=True)

    def finish(self, n):
        P = self.P
        P.mk.op("act", lambda e: e.activation(out=self.rt[:], in_=self.ss_ps[:], func=AF.Sqrt,
                                              bias=P.epsb[:], scale=1.0 / n),
                reads=[self.SSPS, P.EPSB], writes=[self.RT])
        P.mk.op("dve", lambda e: e.reciprocal(self.rstd[:], self.rt[:]), reads=[self.RT], writes=[self.RSTD])


def emit_norm_in(P, ws, t, l, n, xn, XN):
    mk = P.mk
    tsl = slice(t * TT, (t + 1) * TT)
    for dc in range(DC):
        i = ws.square(P.xs[:, dc, tsl], [P.X[dc][t]])
        ws.accum(i, dc == 0, dc == DC - 1)
    ws.finish(D)
    for dc in range(DC):
        c = gcol(l, n, dc)
        mk.op("dve", lambda e: e.scalar_tensor_tensor(xn[:, dc, :], P.xs[:, dc, tsl], P.g[:, c:c + 1],
                                                      ws.rstd[:], ALU.mult, ALU.mult),
              reads=[P.X[dc][t], P.G, ws.RSTD], writes=[XN[dc]])


class ProjWS:
    def __init__(self, P, st, tag):
        self.y = P.sb(f"y_{tag}", [128, DC, TT], F32, st)
        self.Y = [Buf(f"y{dc}") for dc in range(DC)]
        self.y_ps = [P.ps(f"y_ps{i}_{tag}", [128, TT], F32, st) for i in range(2)]
        self.YPS = [Buf(f"yps{i}") for i in range(2)]
        self.tmp = [P.sb(f"tmp{i}_{tag}", [128, TT], F32, st) for i in range(2)]
        self.TMP = [Buf(f"tmp{i}") for i in range(2)]


def emit_proj_norm_res(P, ws, pw, t, rhs, RHS, nfc, wres, WRES, l, n, half):
    mk = P.mk
    tsl = slice(t * TT, (t + 1) * TT)
    gt = P.hg if half else P.g
    GT = P.HG if half else P.G
    pend = None
    for dc in range(DC):
        b = dc % 2
        for fc in range(nfc):
            mk.op("pe", lambda e: e.matmul(pw.y_ps[b][:], wres[:, fc, dc * 128:(dc + 1) * 128], rhs[:, fc, :],
                                           start=(fc == 0), stop=(fc == nfc - 1)),
                  reads=[WRES, RHS[fc]], writes=[pw.YPS[b]])
        if pend is not None:
            ws.accum(pend, dc - 1 == 0, False)
        mk.op("dve", lambda e: e.tensor_copy(pw.y[:, dc, :], pw.y_ps[b][:]), reads=[pw.YPS[b]], writes=[pw.Y[dc]])
        pend = ws.square(pw.y[:, dc, :], [pw.Y[dc]])
    ws.accum(pend, False, True)
    ws.finish(D)
    for dc in range(DC):
        i = dc % 2
        c = gcol(l, n, dc)
        mk.op("dve", lambda e: e.tensor_tensor(pw.tmp[i][:], pw.y[:, dc, :], ws.rstd[:], ALU.mult),
              reads=[pw.Y[dc], ws.RSTD], writes=[pw.TMP[i]])
        mk.op("dve", lambda e: e.scalar_tensor_tensor(P.xs[:, dc, tsl], pw.tmp[i][:], gt[:, c:c + 1],
                                                      P.xs[:, dc, tsl], ALU.mult, ALU.add),
              reads=[pw.TMP[i], GT, P.X[dc][t]], writes=[P.X[dc][t]])


def emit_ffn(P, l, which, w_in_ap, w_out_ap):
    nc, mk = P.nc, P.mk
    na, nb = (0, 1) if which == 0 else (4, 5)
    GF = 2
    NG = FC // GF
    NS = 2
    tag = f"f{l}{which}"
    mk.stage_boundary()
    with ExitStack() as st:
        xn2 = [P.sb(f"xn{i}", [128, DC, TT], BF16, st) for i in range(2)]
        XN2 = [[Buf(f"xn{i}_{dc}") for dc in range(DC)] for i in range(2)]
        a = P.sb("a", [128, FC, TT], BF16, st)
        A = [Buf(f"a{fc}") for fc in range(FC)]
        win = [P.sb(f"win{s}", [128, DC, 2 * GF * 128], BF16, st) for s in range(NS)]
        WIN = [Buf(f"win{s}") for s in range(NS)]
        d_win = [mk.dsem(f"win{s}_{tag}", "pool") for s in range(NS)]
        wout = P.sb("wout", [128, FC, D], BF16, st)
        WOUT = Buf("wout")
        d_wout = mk.dsem(f"wout_{tag}", "pool")
        sg = [P.sb(f"sg{i}", [128, TT], F32, st) for i in range(2)]
        SG = [Buf(f"sg{i}") for i in range(2)]
        gate_ps = [P.ps(f"gate_ps{i}", [128, TT], F32, st) for i in range(2)]
        GPS = [Buf(f"gps{i}") for i in range(2)]
        up_ps = [P.ps(f"up_ps{i}", [128, TT], F32, st) for i in range(2)]
        UPS = [Buf(f"ups{i}") for i in range(2)]
        ws = NormWS(P, st, tag)
        pw = ProjWS(P, st, tag)

        w_in_v = w_in_ap.rearrange("(kc p) c -> p kc c", p=128)
        w_out_v = w_out_ap.rearrange("(fc p) d -> p fc d", p=128)

        def load_win(t, g):
            s = (t * NG + g) % NS
            c0 = g * GF * 128
            mk.dma("pool", d_win[s],
                   [(win[s][:, :, 0:GF * 128], w_in_v[:, :, c0:c0 + GF * 128]),
                    (win[s][:, :, GF * 128:2 * GF * 128], w_in_v[:, :, DFF + c0:DFF + c0 + GF * 128])],
                   writes=[WIN[s]])

        mk.dma("pool", d_wout,
               [(wout[:, 0:11, :], w_out_v[:, 0:11, :]), (wout[:, 11:22, :], w_out_v[:, 11:22, :])],
               writes=[WOUT])

        emit_norm_in(P, ws, 0, l, na, xn2[0], XN2[0])
        for t in range(NT):
            xn, XN = xn2[t % 2], XN2[t % 2]
            load_win(t, 0)
            k = 0
            for g in range(NG):
                s = (t * NG + g) % NS
                if g + 1 < NG:
                    load_win(t, g + 1)
                if g == NG - 3 and t + 1 < NT:
                    emit_norm_in(P, ws, t + 1, l, na, xn2[(t + 1) % 2], XN2[(t + 1) % 2])
                for j in range(GF):
                    fc = g * GF + j
                    b = k % 2
                    k += 1
                    for kc in range(DC):
                        mk.op("pe", lambda e: e.matmul(gate_ps[b][:], win[s][:, kc, j * 128:(j + 1) * 128],
                                                       xn[:, kc, :], start=(kc == 0), stop=(kc == DC - 1)),
                              reads=[WIN[s], XN[kc]], writes=[GPS[b]])
                    for kc in range(DC):
                        mk.op("pe", lambda e: e.matmul(up_ps[b][:],
                                                       win[s][:, kc, GF * 128 + j * 128:GF * 128 + (j + 1) * 128],
                                                       xn[:, kc, :], start=(kc == 0), stop=(kc == DC - 1)),
                              reads=[WIN[s], XN[kc]], writes=[UPS[b]])
                    mk.op("act", lambda e: e.activation(out=sg[b][:], in_=gate_ps[b][:], func=AF.Silu),
                          reads=[GPS[b]], writes=[SG[b]])
                    mk.op("dve", lambda e: e.tensor_tensor(a[:, fc, :], sg[b][:], up_ps[b][:], ALU.mult),
                          reads=[SG[b], UPS[b]], writes=[A[fc]])
            emit_proj_norm_res(P, ws, pw, t, a, A, FC, wout, WOUT, l, nb, True)


def load_wres(P, st, name, w_ap, nfc, dsem):
    w = P.sb(name, [128, nfc, D], BF16, st)
    W = Buf(name)
    v = w_ap.rearrange("(fc p) d -> p fc d", p=128)
    h = nfc // 2
    P.mk.dma("pool", dsem, [(w[:, 0:h, :], v[:, 0:h, :]), (w[:, h:nfc, :], v[:, h:nfc, :])], writes=[W])
    return w, W


def emit_pre(P, l, wqkv_ap, X, tag):
    mk = P.mk
    mk.stage_boundary()
    with ExitStack() as st:
        hn = P.sb("hn", [128, DC, TT], BF16, st)
        HN = [Buf(f"hn{dc}") for dc in range(DC)]
        wq = P.sb("wqkv", [128, DC, 3 * D], BF16, st)
        WQ = Buf("wqkv")
        d_w = mk.dsem(f"wqkv_{tag}", "pool")
        ws = NormWS(P, st, tag)
        ps = [P.ps(f"pre_ps{i}", [128, TT], F32, st) for i in range(2)]
        PS = [Buf(f"preps{i}") for i in range(2)]
        stg = [P.sb(f"stg{i}", [128, TT], BF16, st) for i in range(4)]
        STG = [Buf(f"stg{i}") for i in range(4)]
        d_stg = [mk.dsem(f"stg{i}_{tag}") for i in range(4)]
        wv = wqkv_ap.rearrange("(kc p) c -> p kc c", p=128)
        mk.dma("pool", d_w, [(wq[:, :, i * 768:(i + 1) * 768], wv[:, :, i * 768:(i + 1) * 768]) for i in range(4)],
               writes=[WQ])
        k = 0
        otok = {}
        for t in range(NT):
            tsl = slice(t * TT, (t + 1) * TT)
            emit_norm_in(P, ws, t, l, 2, hn, HN)
            for fcg in range(16):
                b = k % 2
                si = k % 4
                k += 1
                for kc in range(DC):
                    mk.op("pe", lambda e: e.matmul(ps[b][:], wq[:, kc, fcg * 128:(fcg + 1) * 128], hn[:, kc, :],
                                                   start=(kc == 0), stop=(kc == DC - 1)),
                          reads=[WQ, HN[kc]], writes=[PS[b]])
                if fcg < 8:
                    mk.op("act", lambda e: e.activation(out=stg[si][:], in_=ps[b][:], func=AF.Copy, scale=0.125),
                          reads=[PS[b]], writes=[STG[si]])
                    dst = X.snd_q[t][fcg * 128:(fcg + 1) * 128, :]
                else:
                    mk.op("dve", lambda e: e.tensor_copy(stg[si][:], ps[b][:]), reads=[PS[b]], writes=[STG[si]])
                    dst = X.snd_k[t][(fcg - 8) * 128:(fcg - 7) * 128, :]
                otok[si] = mk.dma("sp", d_stg[si], [(dst, stg[si][:])], reads=[STG[si]])
            for tb in range(TT // 128):
                for hf in range(2):
                    b = k % 2
                    si = k % 4
                    k += 1
                    for kc in range(DC):
                        mk.op("pe", lambda e: e.matmul(ps[b][:], hn[:, kc, tb * 128:(tb + 1) * 128],
                                                       wq[:, kc, 2 * D + hf * 512:2 * D + (hf + 1) * 512],
                                                       start=(kc == 0), stop=(kc == DC - 1)),
                              reads=[WQ, HN[kc]], writes=[PS[b]])
                    if hf == 0:
                        mk.op("act", lambda e: e.activation(out=stg[si][:], in_=ps[b][:], func=AF.Copy),
                              reads=[PS[b]], writes=[STG[si]])
                    else:
                        mk.op("dve", lambda e: e.tensor_copy(stg[si][:], ps[b][:]), reads=[PS[b]], writes=[STG[si]])
                    pairs = []
                    for jj in range(2):
                        jb = hf * 2 + jj
                        pairs.append((X.snd_v[t][jb * TT + tb * 128:jb * TT + tb * 128 + 128, :],
                                      stg[si][:, jj * 256:(jj + 1) * 256]))
                    otok[si] = mk.dma("sp", d_stg[si], pairs, reads=[STG[si]])
            done = list(otok.values())
            for s_, g_ in ((X.snd_q[t], X.gath_q[t]), (X.snd_k[t], X.gath_k[t]), (X.snd_v[t], X.gath_v[t])):
                X.last = mk.allgather(P.csem, s_, g_, extra=done)


class Pipe:
    def __init__(self, depth, batch):
        self.q = []
        self.depth = depth
        self.batch = batch

    def _n(self):
        return sum(1 for w, _ in self.q if w)

    def push(self, fn, weight=1):
        self.q.append((weight, fn))
        while self._n() >= self.depth + self.batch:
            self._pop()

    def _pop(self):
        items = []
        while self.q and self.q[0][0] == 1 and len(items) < self.batch:
            items.append(self.q.pop(0)[1])
        order = sorted(range(len(items)), key=lambda i: items[i][0])
        for i in order:
            items[i][1]()
        for i in order:
            items[i][2]()
        while self.q and self.q[0][0] == 0:
            self.q.pop(0)[1]()

    def flush(self):
        while self.q:
            self._pop()


def load_kv(P, X, d_kv, kt, vv, KTB, VB):
    pairs = []
    for r in range(4):
        for t in range(4):
            for c in range(2):
                col0 = r * NTOK + t * TT
                pairs.append((kt[:, c, col0:col0 + TT], X.gath_k[t], P.idx[:, r * 2 + c:r * 2 + c + 1]))
            for tb in range(4):
                kbi = r * 16 + t * 4 + tb
                pairs.append((vv[:, kbi, :], X.gath_v[t], P.idx[:, 8 + r * 4 + tb:8 + r * 4 + tb + 1]))
    P.mk.dma("pool", d_kv, pairs, reads=[P.IDX], writes=[KTB, VB], extra=[X.last])


def emit_core_A(P, X, IN, biasT, tag):
    mk = P.mk
    NQG = SEQ // TT
    mk.stage_boundary()
    with ExitStack() as st:
        kt = P.sb("kt", [128, 2, SEQ], BF16, st)
        KTB = Buf("kt")
        vv = P.sb("vv", [128, NKB, 256], BF16, st)
        VB = Buf("vv")
        d_kv = mk.dsem(f"kv_{tag}", "pool")
        bias = P.sb("biasA", [128, 8 * TT], F32, st)
        BIAS = Buf("biasA")
        d_bias = mk.dsem(f"bias_{tag}")
        qs = [P.sb(f"qs{i}", [128, TT], BF16, st) for i in range(2)]
        QS = [Buf(f"qs{i}") for i in range(2)]
        d_q = [mk.dsem(f"q{i}_{tag}", "pool") for i in range(2)]
        tt_ = [P.sb(f"tA{i}", [128, TT], F32, st) for i in range(8)]
        TTB = [Buf(f"tA{i}") for i in range(8)]
        pp = [P.sb(f"pA{i}", [128, TT], BF16, st) for i in range(8)]
        PP = [Buf(f"pA{i}") for i in range(8)]
        s_ps = [P.ps(f"sA_ps{i}", [128, TT], F32, st) for i in range(4)]
        SPS = [Buf(f"sps{i}") for i in range(4)]
        o_ps = [P.ps(f"oA_ps{i}", [64, TT], F32, st) for i in range(2)]
        OPS = [Buf(f"ops{i}") for i in range(2)]
        l_ps = [P.ps(f"lA_ps{i}", [64, TT], F32, st) for i in range(2)]
        LPS = [Buf(f"lps{i}") for i in range(2)]
        rl = P.sb("rlA", [64, TT], F32, st)
        RL = Buf("rl")
        osb = P.sb("osbA", [64, TT], F32, st)
        OSB = Buf("osb")
        on = [P.sb(f"onA{i}", [64, TT], F32, st) for i in range(2)]
        ON = [Buf(f"on{i}") for i in range(2)]
        d_o = [mk.dsem(f"o{i}_{tag}") for i in range(2)]

        load_kv(P, X, d_kv, kt, vv, KTB, VB)
        it = 0
        kk = 0
        otok = {}
        pipe = Pipe(3, 4)
        for hl in range(4):
            c, u = hl // 2, hl % 2
            prow = slice(u * 64, (u + 1) * 64)
            mk.dma("sp", d_bias, [(bias[:, 0:2048], biasT[hl, :, 0:2048]), (bias[:, 2048:4096], biasT[hl, :, 2048:4096])],
                   reads=[IN], writes=[BIAS])
            for qn, qg in enumerate([rd_ * 4 + t_ for t_ in range(4) for rd_ in range(4)]):
                qi = it % 2
                it += 1
                mk.dma("pool", d_q[qi], [(qs[qi][:], X.gath_q[qg % 4], P.idx[:, (qg // 4) * 2 + c:(qg // 4) * 2 + c + 1])],
                       reads=[P.IDX], writes=[QS[qi]], extra=[X.last])
                kbs = [kb for kb in (3, 4, 0, 1, 2, 5, 6, 7) if qg * 4 - 4 + kb >= 0]
                for n_, kb in enumerate(kbs):
                    kbi = qg * 4 - 4 + kb
                    b = kk % 8
                    sb_ = kk % 4
                    kk += 1
                    first, last = (n_ == 0), (n_ == len(kbs) - 1)
                    cs = slice(64 * max(0, 2 * kb - 8), 64 * (min(7, 2 * kb + 1) + 1))
                    bs = slice(kb * TT + cs.start, kb * TT + cs.stop)
                    mk.op("pe", lambda e: e.matmul(s_ps[sb_][:, cs], kt[prow, c, kbi * 128:(kbi + 1) * 128], qs[qi][prow, cs],
                                                   start=True, stop=True),
                          reads=[KTB, QS[qi]], writes=[SPS[sb_]])
                    mk.op("dve", lambda e: e.tensor_tensor(tt_[b][:, cs], s_ps[sb_][:, cs], bias[:, bs], ALU.add),
                          reads=[SPS[sb_], BIAS], writes=[TTB[b]])
                    mk.op("act", lambda e: e.activation(out=pp[b][:, cs], in_=tt_[b][:, cs], func=AF.Exp),
                          reads=[TTB[b]], writes=[PP[b]])

                    def pv_o(b=b, kbi=kbi, hl=hl, qi=qi, first=first, last=last, cs=cs):
                        mk.op("pe", lambda e: e.matmul(o_ps[qi][:, cs], vv[:, kbi, hl * 64:(hl + 1) * 64], pp[b][:, cs],
                                                       start=first, stop=last),
                              reads=[VB, PP[b]], writes=[OPS[qi]])

                    def pv_l(b=b, qi=qi, first=first, last=last, cs=cs):
                        mk.op("pe", lambda e: e.matmul(l_ps[qi][:, cs], P.ones[:, 0:64], pp[b][:, cs], start=first, stop=last),
                              reads=[P.ONES, PP[b]], writes=[LPS[qi]])
                    pipe.push((0, pv_o, pv_l))

                def epi(qi=qi, qg=qg, c=c, u=u):
                    mk.op("dve", lambda e: e.reciprocal(rl[:], l_ps[qi][:]), reads=[LPS[qi]], writes=[RL])
                    mk.op("act", lambda e: e.activation(out=osb[:], in_=o_ps[qi][:], func=AF.Copy),
                          reads=[OPS[qi]], writes=[OSB])
                    mk.op("dve", lambda e: e.tensor_tensor(on[qi][:], osb[:], rl[:], ALU.mult),
                          reads=[OSB, RL], writes=[ON[qi]])
                    rd = qg // 4
                    otok[qi] = mk.dma("sp", d_o[qi], [(X.snd_o[qg % 4][c][rd * 128 + u * 64:rd * 128 + (u + 1) * 64, :], on[qi][:])],
                                      reads=[ON[qi]])
                pipe.push(epi, 0)
                if u == 1 and qn % 4 == 3:
                    def xchg(t_=qn // 4, c=c):
                        X.last_o = mk.allgather(P.csem, X.snd_o[t_][c], X.gath_o[t_][c], extra=list(otok.values()))
                    pipe.push(xchg, 0)
        pipe.flush()


def emit_core_C(P, X, IN, tabT, tabOff, tabD, lamB, tag, lambda_init):
    mk = P.mk
    NQT = SEQ // TT
    mk.stage_boundary()
    with ExitStack() as st:
        kt = P.sb("ktC", [128, 2, SEQ], BF16, st)
        KTB = Buf("kt")
        vv = P.sb("vvC", [128, NKB, 256], BF16, st)
        VB = Buf("vv")
        d_kv = mk.dsem(f"kv_{tag}", "pool")
        tT = P.sb("tT", [128, 2, TT], F32, st)
        tO = P.sb("tO", [128, 2 * 64], F32, st)
        tD = P.sb("tD", [128, 2, 4 * TT], F32, st)
        lam = P.sb("lam", [128, 256], F32, st)
        TAB = Buf("tabs")
        d_tab = mk.dsem(f"tab_{tag}")
        lp = P.sb("lamp", [128, 128], F32, st)
        LP = Buf("lp")
        lsum = P.sb("lsum", [128, 2], F32, st)
        LS = Buf("ls")
        lexp = P.sb("lexp", [128, 2], F32, st)
        LE = Buf("le")
        nlam = P.sb("nlam", [128, 1], F32, st)
        NL = Buf("nl")
        qs = [P.sb(f"qsC{i}", [128, TT], BF16, st) for i in range(2)]
        QS = [Buf(f"qs{i}") for i in range(2)]
        d_q = [mk.dsem(f"q{i}_{tag}", "pool") for i in range(2)]
        tt_ = [P.sb(f"tC{i}", [128, TT], F32, st) for i in range(8)]
        TTB = [Buf(f"tC{i}") for i in range(8)]
        pp = [P.sb(f"pC{i}", [128, TT], BF16, st) for i in range(8)]
        PP = [Buf(f"pC{i}") for i in range(8)]
        s_ps = [P.ps(f"sC_ps{i}", [128, TT], F32, st) for i in range(4)]
        SPS = [Buf(f"sps{i}") for i in range(4)]
        o_ps = [P.ps(f"oC_ps{i}", [128, TT], F32, st) for i in range(2)]
        OPS = [Buf(f"ops{i}") for i in range(2)]
        l_ps = [P.ps(f"lC_ps{i}", [128, TT], F32, st) for i in range(2)]
        LPS = [Buf(f"lps{i}") for i in range(2)]
        rl = [P.sb(f"rlC{i}", [128, TT], F32, st) for i in range(2)]
        RL = [Buf(f"rl{i}") for i in range(2)]
        osb = [P.sb(f"osbC{i}", [128, TT], F32, st) for i in range(2)]
        OSB = [Buf(f"osb{i}") for i in range(2)]
        am = [P.sb(f"amC{i}", [128, TT], F32, st) for i in range(2)]
        AM = [Buf(f"am{i}") for i in range(2)]
        on = [P.sb(f"onC{i}", [128, TT], F32, st) for i in range(2)]
        ON = [Buf(f"on{i}") for i in range(2)]
        d_o = [mk.dsem(f"o{i}_{tag}") for i in range(2)]

        mk.dma("sp", d_tab, [(tT[:, 0, :], tabT[0]), (tT[:, 1, :], tabT[1]), (tO[:], tabOff),
                             (tD[:, 0, :], tabD[0]), (tD[:, 1, :], tabD[1]), (lam[:], lamB)],
               reads=[IN], writes=[TAB])
        load_kv(P, X, d_kv, kt, vv, KTB, VB)
        mk.op("dve", lambda e: e.tensor_tensor(lp[:, 0:64], lam[:, 0:64], lam[:, 64:128], ALU.mult),
              reads=[TAB], writes=[LP])
        mk.op("dve", lambda e: e.tensor_tensor(lp[:, 64:128], lam[:, 128:192], lam[:, 192:256], ALU.mult),
              reads=[TAB, LP], writes=[LP])
        mk.op("dve", lambda e: e.reduce_sum(lsum[:, 0:1], lp[:, 0:64], mybir.AxisListType.X), reads=[LP], writes=[LS])
        mk.op("dve", lambda e: e.reduce_sum(lsum[:, 1:2], lp[:, 64:128], mybir.AxisListType.X), reads=[LP, LS], writes=[LS])
        mk.op("act", lambda e: e.activation(out=lexp[:], in_=lsum[:], func=AF.Exp), reads=[LS], writes=[LE])
        mk.op("dve", lambda e: e.tensor_tensor(nlam[:], lexp[:, 1:2], lexp[:, 0:1], ALU.subtract), reads=[LE], writes=[NL])
        mk.op("dve", lambda e: e.tensor_scalar(nlam[:], nlam[:], -float(lambda_init), None, ALU.add), reads=[NL], writes=[NL])

        it = 0
        kk = 0
        otok = {}
        pipe = Pipe(3, 4)
        for hl in range(2):
            for qn, qt in enumerate([rd_ * 4 + t_ for t_ in range(4) for rd_ in range(4)]):
                qi = it % 2
                it += 1
                mk.dma("pool", d_q[qi], [(qs[qi][:], X.gath_q[qt % 4], P.idx[:, (qt // 4) * 2 + hl:(qt // 4) * 2 + hl + 1])],
                       reads=[P.IDX], writes=[QS[qi]], extra=[X.last])
                nkb = 4 * qt + 4
                for kb in range(nkb):
                    diag = kb >= 4 * qt
                    first, last = (kb == 0), (kb == nkb - 1)
                    for m in range(2):
                        b = kk % 8
                        sb_ = kk % 4
                        kk += 1
                        prow = slice(m * 64, (m + 1) * 64)
                        mk.op("pe", lambda e: e.matmul(s_ps[sb_][:], kt[prow, hl, kb * 128:(kb + 1) * 128], qs[qi][prow, :],
                                                       start=True, stop=True),
                              reads=[KTB, QS[qi]], writes=[SPS[sb_]])
                        if diag:
                            j = kb - 4 * qt
                            mk.op("dve", lambda e: e.tensor_tensor(tt_[b][:], s_ps[sb_][:], tD[:, hl, j * TT:(j + 1) * TT], ALU.add),
                                  reads=[SPS[sb_], TAB], writes=[TTB[b]])
                            mk.op("act", lambda e: e.activation(out=pp[b][:], in_=tt_[b][:], func=AF.Exp),
                                  reads=[TTB[b]], writes=[PP[b]])
                        else:
                            n = 4 * qt - kb
                            mk.op("dve", lambda e: e.tensor_tensor(tt_[b][:], s_ps[sb_][:], tT[:, hl, :], ALU.add),
                                  reads=[SPS[sb_], TAB], writes=[TTB[b]])
                            mk.op("act", lambda e: e.activation(out=pp[b][:], in_=tt_[b][:], func=AF.Exp,
                                                                bias=tO[:, hl * 64 + n:hl * 64 + n + 1]),
                                  reads=[TTB[b], TAB], writes=[PP[b]])

                        def pv_o(b=b, kb=kb, hl=hl, m=m, first=first, last=last):
                            mk.op("pe", lambda e: e.matmul(o_ps[m][:], vv[:, kb, hl * 128:(hl + 1) * 128], pp[b][:],
                                                           start=first, stop=last),
                                  reads=[VB, PP[b]], writes=[OPS[m]])

                        def pv_l(b=b, m=m, first=first, last=last):
                            mk.op("pe", lambda e: e.matmul(l_ps[m][:], P.ones[:], pp[b][:], start=first, stop=last),
                                  reads=[P.ONES, PP[b]], writes=[LPS[m]])
                        pipe.push((m, pv_o, pv_l))

                def epi(qi=qi, qt=qt, hl=hl):
                    for m in range(2):
                        mk.op("dve", lambda e: e.reciprocal(rl[m][:], l_ps[m][:]), reads=[LPS[m]], writes=[RL[m]])
                        mk.op("act", lambda e: e.activation(out=osb[m][:], in_=o_ps[m][:], func=AF.Copy),
                              reads=[OPS[m]], writes=[OSB[m]])
                        mk.op("dve", lambda e: e.tensor_tensor(am[m][:], osb[m][:], rl[m][:], ALU.mult),
                              reads=[OSB[m], RL[m]], writes=[AM[m]])
                    mk.op("dve", lambda e: e.scalar_tensor_tensor(on[qi][:], am[1][:], nlam[:, 0:1], am[0][:], ALU.mult, ALU.add),
                          reads=[AM[0], AM[1], NL], writes=[ON[qi]])
                    rd = qt // 4
                    otok[qi] = mk.dma("sp", d_o[qi], [(X.snd_o[qt % 4][hl][rd * 128:(rd + 1) * 128, :], on[qi][:])],
                                      reads=[ON[qi]])
                pipe.push(epi, 0)
                if qn % 4 == 3:
                    def xchg(t_=qn // 4, hl=hl):
                        X.last_o = mk.allgather(P.csem, X.snd_o[t_][hl], X.gath_o[t_][hl], extra=list(otok.values()))
                    pipe.push(xchg, 0)
        pipe.flush()


def emit_post(P, l, X, IN, w_o_ap, tag, sublnT=None, lambda_init=None):
    mk = P.mk
    mk.stage_boundary()
    with ExitStack() as st:
        d_w = mk.dsem(f"wo_{tag}", "pool")
        wo, WO = load_wres(P, st, "wo", w_o_ap, DC, d_w)
        ws = NormWS(P, st, tag)
        pw = ProjWS(P, st, tag)
        ob = [P.sb(f"ob{i}", [128, DC, TT], F32, st) for i in range(2)]
        OB = [Buf(f"ob{i}") for i in range(2)]
        d_ob = [mk.dsem(f"ob{i}_{tag}", "pool") for i in range(2)]
        onb = P.sb("onb", [128, DC, TT], BF16, st)
        ONB = [Buf(f"onb{dc}") for dc in range(DC)]
        if sublnT is not None:
            sgc = P.sb("sgc", [128, 1], F32, st)
            SGC = Buf("sgc")
            d_sg = mk.dsem(f"sg_{tag}")
            mk.dma("sp", d_sg, [(sgc[:], sublnT)], reads=[IN], writes=[SGC])
            mk.op("dve", lambda e: e.tensor_scalar(sgc[:], sgc[:], float(1.0 - lambda_init), None, ALU.mult),
                  reads=[SGC], writes=[SGC])
        for t in range(NT):
            i = t % 2
            mk.dma("pool", d_ob[i], [(ob[i][:, 2 * r + c, :], X.gath_o[t][c], P.idx[:, 24 + r:25 + r])
                                     for r in range(4) for c in range(2)],
                   reads=[P.IDX], writes=[OB[i]], extra=[X.last_o])
            for fc in range(DC):
                if sublnT is None:
                    if fc % 2 == 0:
                        mk.op("dve", lambda e: e.tensor_copy(onb[:, fc, :], ob[i][:, fc, :]), reads=[OB[i]], writes=[ONB[fc]])
                    else:
                        mk.op("act", lambda e: e.activation(out=onb[:, fc, :], in_=ob[i][:, fc, :], func=AF.Copy),
                              reads=[OB[i]], writes=[ONB[fc]])
                else:
                    si = ws.square(ob[i][:, fc, :], [OB[i]])
                    ws.accum(si, True, True)
                    ws.finish(128)
                    mk.op("dve", lambda e: e.scalar_tensor_tensor(onb[:, fc, :], ob[i][:, fc, :], sgc[:, 0:1],
                                                                  ws.rstd[:], ALU.mult, ALU.mult),
                          reads=[OB[i], SGC, ws.RSTD], writes=[ONB[fc]])
            emit_proj_norm_res(P, ws, pw, t, onb, ONB, DC, wo, WO, l, 3, False)


def emit_mixer_B(P, l, w_in_ap, lngT, lnbT, wsT, triu, bsR, w_o_ap, IN, tag):
    mk = P.mk
    mk.stage_boundary()
    with ExitStack() as st:
        d_w = mk.dsem(f"wiB_{tag}", "pool")
        wi = P.sb("wiB", [128, DC, 2 * D], BF16, st)
        WI = Buf("wiB")
        wv = w_in_ap.rearrange("(kc p) c -> p kc c", p=128)
        mk.dma("pool", d_w, [(wi[:, :, i * 512:(i + 1) * 512], wv[:, :, i * 512:(i + 1) * 512]) for i in range(4)],
               writes=[WI])
        d_wo = mk.dsem(f"woB_{tag}", "pool")
        wo, WO = load_wres(P, st, "woB", w_o_ap, DC, d_wo)
        d_c = mk.dsem(f"cB_{tag}")
        lng = P.sb("lng", [128, 8], F32, st)
        lnb = P.sb("lnb", [128, 8], F32, st)
        addt = P.sb("addt", [128, 8 * 128], F32, st)
        ADDT = Buf("addt")
        wsb = P.sb("wsb", [128, 8 * 128], BF16, st)
        WSB = Buf("wsb")
        CB = Buf("cB")
        ps = [P.ps(f"B_ps{i}", [128, TT], F32, st) for i in range(4)]
        PS = [Buf(f"Bps{i}") for i in range(4)]
        with ExitStack() as st2:
            wsf = P.sb("wsf", [128, 8 * 128], F32, st2)
            tri = P.sb("tri", [128, 128], F32, st2)
            bsr = P.sb("bsr", [128, 8 * 128], F32, st2)
            mk.dma("sp", d_c, [(lng[:], lngT), (lnb[:], lnbT), (wsf[:], wsT), (tri[:], triu), (bsr[:], bsR)],
                   reads=[IN], writes=[CB])
            for g in range(8):
                gs = slice(g * 128, (g + 1) * 128)
                mk.op("dve", lambda e: e.tensor_tensor(wsb[:, gs], wsf[:, gs], tri[:], ALU.mult),
                      reads=[CB, WSB], writes=[WSB])
            for g in range(8):
                gs = slice(g * 128, (g + 1) * 128)
                b = g % 4
                mk.op("pe", lambda e: e.matmul(ps[b][:, 0:128], P.ones[:], wsb[:, gs], start=True, stop=True),
                      reads=[P.ONES, WSB], writes=[PS[b]])
                mk.op("dve", lambda e: e.scalar_tensor_tensor(addt[:, gs], ps[b][:, 0:128], lnb[:, g:g + 1], bsr[:, gs],
                                                              ALU.mult, ALU.add),
                      reads=[PS[b], CB, ADDT], writes=[ADDT])
        mk.stage_boundary()
        ws = NormWS(P, st, tag)
        pw = ProjWS(P, st, tag)
        hn = P.sb("hnB", [128, DC, TT], BF16, st)
        HN = [Buf(f"hn{dc}") for dc in range(DC)]
        u = P.sb("uB", [128, DC, TT], F32, st)
        U = [Buf(f"u{dc}") for dc in range(DC)]
        vf = P.sb("vfB", [128, D], F32, st)
        VF = Buf("vf")
        vn = P.sb("vnB", [128, 4, D], BF16, st)
        VN = [Buf(f"vn{tb}") for tb in range(4)]
        st4 = P.sb("st4", [128, 8], F32, st)
        ST4 = Buf("st4")
        junk = P.sb("junkB", [128, D], F32, st)
        JK = Buf("junk")
        t1 = [P.sb(f"t1B{i}", [128, TT], F32, st) for i in range(2)]
        T1 = [Buf(f"t1{i}") for i in range(2)]
        k = 0
        for t in range(NT):
            emit_norm_in(P, ws, t, l, 2, hn, HN)
            for fc in range(DC):
                b = k % 4
                k += 1
                for kc in range(DC):
                    mk.op("pe", lambda e: e.matmul(ps[b][:], wi[:, kc, fc * 128:(fc + 1) * 128], hn[:, kc, :],
                                                   start=(kc == 0), stop=(kc == DC - 1)),
                          reads=[WI, HN[kc]], writes=[PS[b]])
                mk.op("act", lambda e: e.activation(out=u[:, fc, :], in_=ps[b][:], func=AF.Gelu),
                      reads=[PS[b]], writes=[U[fc]])
            for tb in range(4):
                for hf in range(2):
                    b = k % 4
                    k += 1
                    for kc in range(DC):
                        mk.op("pe", lambda e: e.matmul(ps[b][:], hn[:, kc, tb * 128:(tb + 1) * 128],
                                                       wi[:, kc, D + hf * 512:D + (hf + 1) * 512],
                                                       start=(kc == 0), stop=(kc == DC - 1)),
                              reads=[WI, HN[kc]], writes=[PS[b]])
                    mk.op("act", lambda e: e.activation(out=vf[:, hf * 512:(hf + 1) * 512], in_=ps[b][:], func=AF.Gelu),
                          reads=[PS[b], VF], writes=[VF])
                mk.op("dve", lambda e: e.reduce_sum(st4[:, 0:1], vf[:], mybir.AxisListType.X),
                      reads=[VF, ST4], writes=[ST4])
                mk.op("act", lambda e: e.activation(out=junk[:], in_=vf[:], func=AF.Square),
                      reads=[VF, JK], writes=[JK])
                mk.op("dve", lambda e: e.reduce_sum(st4[:, 1:2], junk[:], mybir.AxisListType.X),
                      reads=[JK, ST4], writes=[ST4])
                mk.op("dve", lambda e: e.tensor_scalar(st4[:, 2:3], st4[:, 0:1], 1.0 / D, None, ALU.mult),
                      reads=[ST4], writes=[ST4])
                mk.op("dve", lambda e: e.tensor_tensor(st4[:, 5:6], st4[:, 2:3], st4[:, 2:3], ALU.mult),
                      reads=[ST4], writes=[ST4])
                mk.op("dve", lambda e: e.scalar_tensor_tensor(st4[:, 6:7], st4[:, 1:2], 1.0 / D, st4[:, 5:6],
                                                              ALU.mult, ALU.subtract),
                      reads=[ST4], writes=[ST4])
                mk.op("act", lambda e: e.activation(out=st4[:, 7:8], in_=st4[:, 6:7], func=AF.Sqrt, bias=P.epsb[:], scale=1.0),
                      reads=[ST4, P.EPSB], writes=[ST4])
                mk.op("dve", lambda e: e.reciprocal(st4[:, 3:4], st4[:, 7:8]), reads=[ST4], writes=[ST4])
                mk.op("dve", lambda e: e.scalar_tensor_tensor(st4[:, 4:5], st4[:, 2:3], -1.0, st4[:, 3:4], ALU.mult, ALU.mult),
                      reads=[ST4], writes=[ST4])
                mk.op("act", lambda e: e.activation(out=vn[:, tb, :], in_=vf[:], func=AF.Identity,
                                                    bias=st4[:, 4:5], scale=st4[:, 3:4]),
                      reads=[VF, ST4], writes=[VN[tb]])
            for g in range(8):
                gs = slice(g * 128, (g + 1) * 128)
                b = k % 4
                k += 1
                i = g % 2
                for tb in range(4):
                    mk.op("pe", lambda e: e.matmul(ps[b][:, tb * 128:(tb + 1) * 128], vn[:, tb, gs],
                                                   wsb[:, gs], start=True, stop=True),
                          reads=[VN[tb], WSB, PS[b]], writes=[PS[b]])
                for tb in range(4):
                    tbs = slice(tb * 128, (tb + 1) * 128)
                    mk.op("dve", lambda e: e.scalar_tensor_tensor(t1[i][:, tbs], ps[b][:, tbs], lng[:, g:g + 1],
                                                                  addt[:, gs], ALU.mult, ALU.add),
                          reads=[PS[b], CB, ADDT, T1[i]], writes=[T1[i]])
                mk.op("dve", lambda e: e.tensor_tensor(hn[:, g, :], t1[i][:], u[:, g, :], ALU.mult),
                      reads=[T1[i], U[g]], writes=[HN[g]])
            emit_proj_norm_res(P, ws, pw, t, hn, HN, DC, wo, WO, l, 3, False)


def load_x(P, xT_ap):
    mk = P.mk
    P.d_x = mk.dsem("xin")
    v = xT_ap.rearrange("(dc p) t -> p dc t", p=128)
    mk.dma("sp", P.d_x, [(P.xs[:, dc, :], v[:, dc, :]) for dc in range(DC)],
           writes=[b for dc in range(DC) for b in P.X[dc]])


def store_x(P, oT_ap):
    mk = P.mk
    P.d_xo = mk.dsem("xout")
    v = oT_ap.rearrange("(dc p) t -> p dc t", p=128)
    return mk.dma("sp", P.d_xo, [(v[:, dc, :], P.xs[:, dc, :]) for dc in range(DC)],
                  reads=[b for dc in range(DC) for b in P.X[dc]])


I32 = mybir.dt.int32
WNAMES = ["ff1_w_in", "ff1_w_out", "ff2_w_in", "ff2_w_out", "a_w_qkv", "a_w_o", "b_w_in", "b_w_o", "c_w_qkv", "c_w_o"]


class XChg:
    def __init__(self, nc, l):
        def dt_(name, shape, dt):
            return nc.dram_tensor(f"{name}_L{l}", list(shape), dt).ap()
        self.snd_q = [dt_(f"sq{t}", [D, TT], BF16) for t in range(NT)]
        self.snd_k = [dt_(f"sk{t}", [D, TT], BF16) for t in range(NT)]
        self.snd_v = [dt_(f"sv{t}", [4 * TT, 256], BF16) for t in range(NT)]
        self.gath_q = [dt_(f"gq{t}", [4 * D, TT], BF16) for t in range(NT)]
        self.gath_k = [dt_(f"gk{t}", [4 * D, TT], BF16) for t in range(NT)]
        self.gath_v = [dt_(f"gv{t}", [16 * TT, 256], BF16) for t in range(NT)]
        self.snd_o = [[dt_(f"so{t}_{c}", [4 * 128, TT], F32) for c in range(2)] for t in range(NT)]
        self.gath_o = [[dt_(f"go{t}_{c}", [16 * 128, TT], F32) for c in range(2)] for t in range(NT)]
        self.last = None
        self.last_o = None


def build_fused(layers=(0, 1, 2, 3), ffn=True):
    nc = bass.Bass("TRN2", target_bir_lowering=False)

    def ext(name, shape, dt=F32):
        return nc.dram_tensor(name, list(shape), dt, kind="ExternalInput").ap()
    xT = ext("xT", [D, NTOK])
    gT = ext("gT", [128, DEPTH * 6 * DC])
    idx = ext("idx", [128, 28], I32)
    W = {"ff1_w_in": ext("ff1_w_in", [DEPTH, D, 2 * DFF]), "ff1_w_out": ext("ff1_w_out", [DEPTH, DFF, D]),
         "ff2_w_in": ext("ff2_w_in", [DEPTH, D, 2 * DFF]), "ff2_w_out": ext("ff2_w_out", [DEPTH, DFF, D]),
         "a_w_qkv": ext("a_w_qkv", [2, D, 3 * D]), "a_w_o": ext("a_w_o", [2, D, D]),
         "b_w_in": ext("b_w_in", [1, D, 2 * D]), "b_w_o": ext("b_w_o", [1, D, D]),
         "c_w_qkv": ext("c_w_qkv", [1, D, 3 * D]), "c_w_o": ext("c_w_o", [1, D, D])}
    biasT = [ext("biasT0", [4, 128, 8 * TT]), ext("biasT1", [4, 128, 8 * TT])]
    tabT = ext("tabT", [2, 128, TT])
    tabOff = ext("tabOff", [128, 128])
    tabD = ext("tabD", [2, 128, 4 * TT])
    lamB = ext("lamB", [128, 256])
    sublnT = ext("sublnT", [128, 1])
    lngT = ext("lngT", [128, 8])
    lnbT = ext("lnbT", [128, 8])
    wsT = ext("wsT", [128, 8 * 128])
    triu = ext("triu", [128, 128])
    bsR = ext("bsR", [128, 8 * 128])
    xTo = nc.dram_tensor("xTo", [D, NTOK], F32, kind="ExternalOutput").ap()
    with ExitStack() as stack:
        mk = MK(nc, stack)
        P = Prog(nc, stack, mk)
        setup_common(P, gT)
        P.idx = P.sb("idx", [128, 28], I32)
        P.IDX = Buf("idx")
        mk.dma("sp", mk.dsem("idx"), [(P.idx[:], idx)], writes=[P.IDX])
        csem_h = stack.enter_context(nc.semaphore("csem"))
        P.csem = DSem("csem", csem_h)
        load_x(P, xT)
        IN = Buf("in")
        for l in layers:
            kind, jm = l % 3, l // 3
            if ffn:
                with nc.named_scope(f"L{l}_ffn1"):
                    emit_ffn(P, l, 0, W["ff1_w_in"][l], W["ff1_w_out"][l])
            if kind == 1:
                with nc.named_scope(f"L{l}_mixB"):
                    emit_mixer_B(P, l, W["b_w_in"][jm], lngT, lnbT, wsT, triu, bsR, W["b_w_o"][jm], IN, f"mB{l}")
            else:
                X = XChg(nc, l)
                if kind == 0:
                    with nc.named_scope(f"L{l}_pre"):
                        emit_pre(P, l, W["a_w_qkv"][jm], X, f"pre{l}")
                    with nc.named_scope(f"L{l}_core"):
                        emit_core_A(P, X, IN, biasT[jm], f"cA{l}")
                    with nc.named_scope(f"L{l}_post"):
                        emit_post(P, l, X, IN, W["a_w_o"][jm], f"post{l}")
                else:
                    li = 0.8 - 0.6 * float(np.exp(-0.3 * l))
                    with nc.named_scope(f"L{l}_pre"):
                        emit_pre(P, l, W["c_w_qkv"][jm], X, f"pre{l}")
                    with nc.named_scope(f"L{l}_core"):
                        emit_core_C(P, X, IN, tabT, tabOff, tabD, lamB, f"cC{l}", li)
                    with nc.named_scope(f"L{l}_post"):
                        emit_post(P, l, X, IN, W["c_w_o"][jm], f"post{l}", sublnT, li)
            if ffn:
                with nc.named_scope(f"L{l}_ffn2"):
                    emit_ffn(P, l, 1, W["ff2_w_in"][l], W["ff2_w_out"][l])
        mk.wait_all("sp", [store_x(P, xTo)])
    return nc


def gains_T(norm_g_l):
    out = np.zeros((128, DEPTH * 6 * DC), np.float32)
    out[:, :6 * DC] = norm_g_l.reshape(6, DC, 128).transpose(2, 0, 1).reshape(128, 6 * DC)
    return out


def feat_cols(v, n):
    return np.ascontiguousarray(v.reshape(n, 128).T)


def bias_tables_A(rel_bias, j):
    k_in = np.arange(128)[:, None, None]
    kb = np.arange(8)[None, :, None]
    q_rel = np.arange(512)[None, None, :]
    krel = kb * 128 + k_in
    valid = (krel // 64 >= q_rel // 64) & (krel // 64 <= q_rel // 64 + 8)
    idx = np.clip(q_rel + 512 - krel, -128, 128) + 128
    out = np.empty((4, 128, 8, 512), np.float32)
    for hl in range(4):
        g = rel_bias[4 * j + hl][idx]
        out[hl] = np.where(valid, g, np.float32(NEG))
    return out.reshape(4, 128, 8 * 512)


def tables_C(j):
    tabT = np.empty((2, 128, 512), np.float32)
    tabOff = np.empty((128, 128), np.float32)
    tabD = np.empty((2, 128, 4, 512), np.float32)
    k_in = np.arange(128)[:, None]
    q_rel = np.arange(512)[None, :]
    for hl in range(2):
        h = 2 * j + hl
        slope = np.float32(2.0 ** (-(h + 1)))
        tabT[hl] = -slope * (q_rel - k_in).astype(np.float32)
        tabOff[:, hl * 64:(hl + 1) * 64] = (-slope * 128.0 * np.arange(64, dtype=np.float32))[None, :]
        for jj in range(4):
            k_rel = jj * 128 + k_in
            allowed = k_rel < (q_rel // 64 + 1) * 64
            tabD[hl, :, jj, :] = np.where(allowed, -slope * np.abs(q_rel - k_rel).astype(np.float32), np.float32(NEG))
    return tabT, tabOff, tabD.reshape(2, 128, 4 * 512)


def gains_all(norm_g):
    return np.ascontiguousarray(norm_g.reshape(DEPTH, 6, DC, 128).transpose(3, 0, 1, 2).reshape(128, DEPTH * 6 * DC))


def idx_table(j):
    p = np.arange(128, dtype=np.int32)
    t = np.zeros((128, 28), np.int32)
    for r in range(4):
        for c in range(2):
            t[:, r * 2 + c] = r * 1024 + 256 * j + c * 128 + p
        for tb in range(4):
            t[:, 8 + r * 4 + tb] = r * 2048 + j * 512 + tb * 128 + p
        t[:, 24 + r] = r * 512 + j * 128 + p
    return t


def make_in_maps(inp):
    x = inp["x"]
    shared = {k: inp[k] for k in WNAMES}
    ws = inp["b_w_s"][0]
    shared.update({
        "gT": gains_all(inp["norm_g"]),
        "lamB": np.ascontiguousarray(np.broadcast_to(inp["c_lambda"][0].reshape(1, 256), (128, 256))),
        "sublnT": np.ascontiguousarray(inp["c_subln_g"][0].reshape(128, 1)),
        "lngT": feat_cols(inp["b_ln_g"][0], 8), "lnbT": feat_cols(inp["b_ln_b"][0], 8),
        "wsT": np.ascontiguousarray(ws.transpose(2, 0, 1).reshape(128, 8 * 128)),
        "triu": np.triu(np.ones((128, 128), np.float32)),
        "bsR": np.ascontiguousarray(np.broadcast_to(inp["b_b_s"][0].reshape(1, 8 * 128), (128, 8 * 128))),
    })
    in_maps = []
    for c in range(NCORES):
        b, j = c // 4, c % 4
        tabT, tabOff, tabD = tables_C(j)
        m = dict(shared)
        m.update({"xT": np.ascontiguousarray(x[b, NTOK * j:NTOK * (j + 1), :].T), "idx": idx_table(j),
                  "biasT0": bias_tables_A(inp["a_rel_bias"][0], j), "biasT1": bias_tables_A(inp["a_rel_bias"][1], j),
                  "tabT": tabT, "tabOff": tabOff, "tabD": tabD})
        in_maps.append(m)
    return in_maps


def kernel(**inputs):
    inp = {k: np.ascontiguousarray(np.asarray(v)) for k, v in inputs.items()}
    x = inp["x"]
    nc = build_fused()
    in_maps = make_in_maps(inp)
    res = run_bass_kernel_spmd(nc, in_maps, core_ids=list(range(NCORES)))
    out = np.empty_like(x)
    for c in range(NCORES):
        b, j = c // 4, c % 4
        out[b, NTOK * j:NTOK * (j + 1), :] = res.results[c]["xTo"].T
    return out
```

```python
import numpy as np
import concourse.bass as bass
import concourse.mybir as mybir
from concourse.bass_utils import run_bass_kernel_spmd

F32 = mybir.dt.float32
BF16 = mybir.dt.bfloat16
AF = mybir.ActivationFunctionType
ALU = mybir.AluOpType

D = 1024
DC = 8
DFF = 2816
FC = 22
NTOK = 2048
TT = 512
NT = NTOK // TT
EPS = 1e-6
DEPTH = 4
NCORES = 8


_FENCE = {}


class Buf:
    __slots__ = ("name", "w", "r")

    def __init__(self, name):
        self.name = name
        self.w = None
        self.r = dict(_FENCE)


class DSem:
    def __init__(self, key, sem):
        self.key = key
        self.sem = sem
        self.cnt = 0


class EngState:
    def __init__(self, name, eng, sem):
        self.name = name
        self.eng = eng
        self.sem = sem
        self.cnt = 0
        self.waited = {}


class _PEProxy:
    def __init__(self, eng):
        self.eng = eng
        self.last_stop = True

    def matmul(self, *a, **kw):
        self.last_stop = bool(kw.get("stop", True))
        return self.eng.matmul(*a, **kw)


class MK:
    def __init__(self, nc, stack):
        self.nc = nc
        self.stack = stack
        self.engs = {}
        for name, eng in (("pe", nc.tensor), ("act", nc.scalar), ("dve", nc.vector),
                          ("pool", nc.gpsimd), ("sp", nc.sync)):
            sem = stack.enter_context(nc.semaphore("s_" + name))
            self.engs[name] = EngState(name, eng, sem)
        self.dsems = []
        self.free_dsems = {"sp": [], "pool": []}
        self.stage_dsems = []
        _FENCE.clear()

    def dsem(self, name, q="sp"):
        if self.free_dsems[q]:
            d = self.free_dsems[q].pop()
        else:
            sem = self.stack.enter_context(self.nc.semaphore("d_%d" % len(self.dsems)))
            d = DSem("d_%d" % len(self.dsems), sem)
            d.q = q
            self.dsems.append(d)
        self.stage_dsems.append(d)
        return d

    def stage_boundary(self):
        for d in self.stage_dsems:
            self.free_dsems[d.q].append(d)
        self.stage_dsems = []
        _FENCE.clear()
        for E in self.engs.values():
            if E.cnt > 0:
                _FENCE[E.name] = (E.name, E.sem, E.cnt)
        for d in self.dsems:
            if d.cnt > 0:
                _FENCE[d.key] = (d.key, d.sem, d.cnt)

    def _wait(self, E, toks):
        need = {}
        for tok in toks:
            if tok is None:
                continue
            key, sem, val = tok
            if key == "pe" and E.name == "pe":
                continue
            if key not in need or need[key][1] < val:
                need[key] = (sem, val)
        for key, (sem, val) in need.items():
            if E.waited.get(key, 0) < val:
                E.eng.wait_ge(sem, val)
                E.waited[key] = val

    def _deps(self, reads, writes):
        toks = []
        for b in reads:
            toks.append(b.w)
        for b in writes:
            toks.append(b.w)
            toks.extend(b.r.values())
        return toks

    def op(self, ename, fn, reads=(), writes=(), force_inc=False):
        E = self.engs[ename]
        self._wait(E, self._deps(reads, writes))
        if ename == "pe":
            prox = _PEProxy(E.eng)
            ins = fn(prox)
            if not prox.last_stop and not force_inc:
                tok = (ename, E.sem, E.cnt + 1)
                for b in reads:
                    b.r[ename] = tok
                for b in writes:
                    b.w = tok
                    b.r = {}
                return tok
        else:
            ins = fn(E.eng)
        E.cnt += 1
        ins.then_inc(E.sem, 1)
        tok = (ename, E.sem, E.cnt)
        for b in reads:
            b.r[ename] = tok
        for b in writes:
            b.w = tok
            b.r = {}
        return tok

    def dma(self, qname, dsem, pairs, reads=(), writes=(), extra=(), **kw):
        E = self.engs[qname]
        assert dsem.q == qname, (dsem.key, dsem.q, qname)
        self._wait(E, self._deps(reads, writes) + list(extra))
        for pr in pairs:
            if len(pr) == 3:
                E.eng.indirect_dma_start(out=pr[0], out_offset=None, in_=pr[1],
                                         in_offset=bass.IndirectOffsetOnAxis(ap=pr[2], axis=0)).then_inc(dsem.sem, 16)
            else:
                E.eng.dma_start(out=pr[0], in_=pr[1], **kw).then_inc(dsem.sem, 16)
            dsem.cnt += 16
        tok = (dsem.key, dsem.sem, dsem.cnt)
        for b in reads:
            b.r[dsem.key] = tok
        for b in writes:
            b.w = tok
            b.r = {}
        return tok

    def allgather(self, csem, src_ap, dst_ap, extra=(), writes=()):
        E = self.engs["pool"]
        self._wait(E, self._deps((), writes) + list(extra))
        E.eng.collective_compute("AllGather", ALU.bypass, replica_groups=[[0, 1, 2, 3], [4, 5, 6, 7]],
                                 ins=[src_ap], outs=[dst_ap]).then_inc(csem.sem, 1)
        csem.cnt += 1
        tok = (csem.key, csem.sem, csem.cnt)
        for b in writes:
            b.w = tok
            b.r = {}
        return tok

    def wait_all(self, ename, toks):
        self._wait(self.engs[ename], toks)


from contextlib import ExitStack

SEQ = 8192
NKB = SEQ // 128
NEG = -30000.0
LAMBDA_INIT_L2 = 0.8 - 0.6 * float(np.exp(-0.3 * 2))


class Prog:
    def __init__(self, nc, stack, mk):
        self.nc = nc
        self.stack = stack
        self.mk = mk

    def sb(self, name, shape, dt, stack=None):
        self.uid = getattr(self, "uid", 0) + 1
        return (stack or self.stack).enter_context(self.nc.sbuf_tensor(f"{name}_{self.uid}", shape, dt))

    def ps(self, name, shape, dt=F32, stack=None):
        self.uid = getattr(self, "uid", 0) + 1
        return (stack or self.stack).enter_context(self.nc.psum_tensor(f"{name}_{self.uid}", shape, dt))


def gcol(l, n, dc):
    return (l * 6 + n) * DC + dc


def setup_common(P, gT_ap):
    nc, mk = P.nc, P.mk
    P.xs = P.sb("xs", [128, DC, NTOK], F32)
    P.X = [[Buf(f"x{dc}_{t}") for t in range(NT)] for dc in range(DC)]
    P.ones = P.sb("ones", [128, 128], BF16)
    P.ONES = Buf("ones")
    P.epsb = P.sb("epsb", [128, 1], F32)
    P.EPSB = Buf("epsb")
    P.g = P.sb("g", [128, DEPTH * 6 * DC], F32)
    P.hg = P.sb("hg", [128, DEPTH * 6 * DC], F32)
    P.G = Buf("g")
    P.HG = Buf("hg")
    P.d_const = mk.dsem("const")
    mk.op("dve", lambda e: e.memset(P.ones[:], 1.0), writes=[P.ONES])
    mk.op("dve", lambda e: e.memset(P.epsb[:], EPS), writes=[P.EPSB])
    mk.dma("sp", P.d_const, [(P.g[:], gT_ap)], writes=[P.G])
    mk.op("dve", lambda e: e.tensor_scalar(P.hg[:], P.g[:], 0.5, None, ALU.mult),
          reads=[P.G], writes=[P.HG])


class NormWS:
    def __init__(self, P, st, tag):
        self.P = P
        self.sq = [P.sb(f"sq{i}_{tag}", [128, TT], BF16, st) for i in range(2)]
        self.SQ = [Buf(f"sq{i}") for i in range(2)]
        self.rt = P.sb(f"rt_{tag}", [128, TT], F32, st)
        self.RT = Buf("rt")
        self.rstd = P.sb(f"rstd_{tag}", [128, TT], F32, st)
        self.RSTD = Buf("rstd")
        self.ss_ps = P.ps(f"ss_ps_{tag}", [128, TT], F32, st)
        self.SSPS = Buf("ssps")
        self.k = 0

    def square(self, src_ap, SRC):
        i = self.k % 2
        self.k += 1
        self.P.mk.op("act", lambda e: e.activation(out=self.sq[i][:], in_=src_ap, func=AF.Square),
                     reads=SRC, writes=[self.SQ[i]])
        return i

    def accum(self, i, first, last):
        P = self.P
        P.mk.op("pe", lambda e: e.matmul(self.ss_ps[:], P.ones[:], self.sq[i][:], start=first, stop=last),
                reads=[P.ONES, self.SQ[i]], writes=[self.SSPS], force_inc=True)

    def finish(self, n):
        P = self.P
        P.mk.op("act", lambda e: e.activation(out=self.rt[:], in_=self.ss_ps[:], func=AF.Sqrt,
                                              bias=P.epsb[:], scale=1.0 / n),
                reads=[self.SSPS, P.EPSB], writes=[self.RT])
        P.mk.op("dve", lambda e: e.reciprocal(self.rstd[:], self.rt[:]), reads=[self.RT], writes=[self.RSTD])


def emit_norm_in(P, ws, t, l, n, xn, XN):
    mk = P.mk
    tsl = slice(t * TT, (t + 1) * TT)
    for dc in range(DC):
        i = ws.square(P.xs[:, dc, tsl], [P.X[dc][t]])
        ws.accum(i, dc == 0, dc == DC - 1)
    ws.finish(D)
    for dc in range(DC):
        c = gcol(l, n, dc)
        mk.op("dve", lambda e: e.scalar_tensor_tensor(xn[:, dc, :], P.xs[:, dc, tsl], P.g[:, c:c + 1],
                                                      ws.rstd[:], ALU.mult, ALU.mult),
              reads=[P.X[dc][t], P.G, ws.RSTD], writes=[XN[dc]])


class ProjWS:
    def __init__(self, P, st, tag):
        self.y = P.sb(f"y_{tag}", [128, DC, TT], F32, st)
        self.Y = [Buf(f"y{dc}") for dc in range(DC)]
        self.y_ps = [P.ps(f"y_ps{i}_{tag}", [128, TT], F32, st) for i in range(2)]
        self.YPS = [Buf(f"yps{i}") for i in range(2)]
        self.tmp = [P.sb(f"tmp{i}_{tag}", [128, TT], F32, st) for i in range(2)]
        self.TMP = [Buf(f"tmp{i}") for i in range(2)]


def emit_proj_norm_res(P, ws, pw, t, rhs, RHS, nfc, wres, WRES, l, n, half):
    mk = P.mk
    tsl = slice(t * TT, (t + 1) * TT)
    gt = P.hg if half else P.g
    GT = P.HG if half else P.G
    pend = None
    for dc in range(DC):
        b = dc % 2
        for fc in range(nfc):
            mk.op("pe", lambda e: e.matmul(pw.y_ps[b][:], wres[:, fc, dc * 128:(dc + 1) * 128], rhs[:, fc, :],
                                           start=(fc == 0), stop=(fc == nfc - 1)),
                  reads=[WRES, RHS[fc]], writes=[pw.YPS[b]])
        if pend is not None:
            ws.accum(pend, dc - 1 == 0, False)
        mk.op("dve", lambda e: e.tensor_copy(pw.y[:, dc, :], pw.y_ps[b][:]), reads=[pw.YPS[b]], writes=[pw.Y[dc]])
        pend = ws.square(pw.y[:, dc, :], [pw.Y[dc]])
    ws.accum(pend, False, True)
    ws.finish(D)
    for dc in range(DC):
        i = dc % 2
        c = gcol(l, n, dc)
        mk.op("dve", lambda e: e.tensor_tensor(pw.tmp[i][:], pw.y[:, dc, :], ws.rstd[:], ALU.mult),
              reads=[pw.Y[dc], ws.RSTD], writes=[pw.TMP[i]])
        mk.op("dve", lambda e: e.scalar_tensor_tensor(P.xs[:, dc, tsl], pw.tmp[i][:], gt[:, c:c + 1],
                                                      P.xs[:, dc, tsl], ALU.mult, ALU.add),
              reads=[pw.TMP[i], GT, P.X[dc][t]], writes=[P.X[dc][t]])


def emit_ffn(P, l, which, w_in_ap, w_out_ap):
    nc, mk = P.nc, P.mk
    na, nb = (0, 1) if which == 0 else (4, 5)
    GF = 2
    NG = FC // GF
    NS = 2
    tag = f"f{l}{which}"
    mk.stage_boundary()
    with ExitStack() as st:
        xn2 = [P.sb(f"xn{i}", [128, DC, TT], BF16, st) for i in range(2)]
        XN2 = [[Buf(f"xn{i}_{dc}") for dc in range(DC)] for i in range(2)]
        a = P.sb("a", [128, FC, TT], BF16, st)
        A = [Buf(f"a{fc}") for fc in range(FC)]
        win = [P.sb(f"win{s}", [128, DC, 2 * GF * 128], BF16, st) for s in range(NS)]
        WIN = [Buf(f"win{s}") for s in range(NS)]
        d_win = [mk.dsem(f"win{s}_{tag}", "pool") for s in range(NS)]
        wout = P.sb("wout", [128, FC, D], BF16, st)
        WOUT = Buf("wout")
        d_wout = mk.dsem(f"wout_{tag}", "pool")
        sg = [P.sb(f"sg{i}", [128, TT], F32, st) for i in range(2)]
        SG = [Buf(f"sg{i}") for i in range(2)]
        gate_ps = [P.ps(f"gate_ps{i}", [128, TT], F32, st) for i in range(2)]
        GPS = [Buf(f"gps{i}") for i in range(2)]
        up_ps = [P.ps(f"up_ps{i}", [128, TT], F32, st) for i in range(2)]
        UPS = [Buf(f"ups{i}") for i in range(2)]
        ws = NormWS(P, st, tag)
        pw = ProjWS(P, st, tag)

        w_in_v = w_in_ap.rearrange("(kc p) c -> p kc c", p=128)
        w_out_v = w_out_ap.rearrange("(fc p) d -> p fc d", p=128)

        def load_win(t, g):
            s = (t * NG + g) % NS
            c0 = g * GF * 128
            mk.dma("pool", d_win[s],
                   [(win[s][:, :, 0:GF * 128], w_in_v[:, :, c0:c0 + GF * 128]),
                    (win[s][:, :, GF * 128:2 * GF * 128], w_in_v[:, :, DFF + c0:DFF + c0 + GF * 128])],
                   writes=[WIN[s]])

        mk.dma("pool", d_wout,
               [(wout[:, 0:11, :], w_out_v[:, 0:11, :]), (wout[:, 11:22, :], w_out_v[:, 11:22, :])],
               writes=[WOUT])

        emit_norm_in(P, ws, 0, l, na, xn2[0], XN2[0])
        for t in range(NT):
            xn, XN = xn2[t % 2], XN2[t % 2]
            load_win(t, 0)
            k = 0
            for g in range(NG):
                s = (t * NG + g) % NS
                if g + 1 < NG:
                    load_win(t, g + 1)
                if g == NG - 3 and t + 1 < NT:
                    emit_norm_in(P, ws, t + 1, l, na, xn2[(t + 1) % 2], XN2[(t + 1) % 2])
                for j in range(GF):
                    fc = g * GF + j
                    b = k % 2
                    k += 1
                    for kc in range(DC):
                        mk.op("pe", lambda e: e.matmul(gate_ps[b][:], win[s][:, kc, j * 128:(j + 1) * 128],
                                                       xn[:, kc, :], start=(kc == 0), stop=(kc == DC - 1)),
                              reads=[WIN[s], XN[kc]], writes=[GPS[b]])
                    for kc in range(DC):
                        mk.op("pe", lambda e: e.matmul(up_ps[b][:],
                                                       win[s][:, kc, GF * 128 + j * 128:GF * 128 + (j + 1) * 128],
                                                       xn[:, kc, :], start=(kc == 0), stop=(kc == DC - 1)),
                              reads=[WIN[s], XN[kc]], writes=[UPS[b]])
                    mk.op("act", lambda e: e.activation(out=sg[b][:], in_=gate_ps[b][:], func=AF.Silu),
                          reads=[GPS[b]], writes=[SG[b]])
                    mk.op("dve", lambda e: e.tensor_tensor(a[:, fc, :], sg[b][:], up_ps[b][:], ALU.mult),
                          reads=[SG[b], UPS[b]], writes=[A[fc]])
            emit_proj_norm_res(P, ws, pw, t, a, A, FC, wout, WOUT, l, nb, True)


def load_wres(P, st, name, w_ap, nfc, dsem):
    w = P.sb(name, [128, nfc, D], BF16, st)
    W = Buf(name)
    v = w_ap.rearrange("(fc p) d -> p fc d", p=128)
    h = nfc // 2
    P.mk.dma("pool", dsem, [(w[:, 0:h, :], v[:, 0:h, :]), (w[:, h:nfc, :], v[:, h:nfc, :])], writes=[W])
    return w, W


def emit_pre(P, l, wqkv_ap, X, tag):
    mk = P.mk
    mk.stage_boundary()
    with ExitStack() as st:
        hn = P.sb("hn", [128, DC, TT], BF16, st)
        HN = [Buf(f"hn{dc}") for dc in range(DC)]
        wq = P.sb("wqkv", [128, DC, 3 * D], BF16, st)
        WQ = Buf("wqkv")
        d_w = mk.dsem(f"wqkv_{tag}", "pool")
        ws = NormWS(P, st, tag)
        ps = [P.ps(f"pre_ps{i}", [128, TT], F32, st) for i in range(2)]
        PS = [Buf(f"preps{i}") for i in range(2)]
        stg = [P.sb(f"stg{i}", [128, TT], BF16, st) for i in range(4)]
        STG = [Buf(f"stg{i}") for i in range(4)]
        d_stg = [mk.dsem(f"stg{i}_{tag}") for i in range(4)]
        wv = wqkv_ap.rearrange("(kc p) c -> p kc c", p=128)
        mk.dma("pool", d_w, [(wq[:, :, i * 768:(i + 1) * 768], wv[:, :, i * 768:(i + 1) * 768]) for i in range(4)],
               writes=[WQ])
        k = 0
        otok = {}
        for t in range(NT):
            tsl = slice(t * TT, (t + 1) * TT)
            emit_norm_in(P, ws, t, l, 2, hn, HN)
            for fcg in range(16):
                b = k % 2
                si = k % 4
                k += 1
                for kc in range(DC):
                    mk.op("pe", lambda e: e.matmul(ps[b][:], wq[:, kc, fcg * 128:(fcg + 1) * 128], hn[:, kc, :],
                                                   start=(kc == 0), stop=(kc == DC - 1)),
                          reads=[WQ, HN[kc]], writes=[PS[b]])
                if fcg < 8:
                    mk.op("act", lambda e: e.activation(out=stg[si][:], in_=ps[b][:], func=AF.Copy, scale=0.125),
                          reads=[PS[b]], writes=[STG[si]])
                    dst = X.snd_q[t][fcg * 128:(fcg + 1) * 128, :]
                else:
                    mk.op("dve", lambda e: e.tensor_copy(stg[si][:], ps[b][:]), reads=[PS[b]], writes=[STG[si]])
                    dst = X.snd_k[t][(fcg - 8) * 128:(fcg - 7) * 128, :]
                otok[si] = mk.dma("sp", d_stg[si], [(dst, stg[si][:])], reads=[STG[si]])
                if fcg == 7:
                    X.last = mk.allgather(P.csem, X.snd_q[t], X.gath_q[t], extra=list(otok.values()))
                elif fcg == 15:
                    X.last = mk.allgather(P.csem, X.snd_k[t], X.gath_k[t], extra=list(otok.values()))
            for tb in range(TT // 128):
                for hf in range(2):
                    b = k % 2
                    si = k % 4
                    k += 1
                    for kc in range(DC):
                        mk.op("pe", lambda e: e.matmul(ps[b][:], hn[:, kc, tb * 128:(tb + 1) * 128],
                                                       wq[:, kc, 2 * D + hf * 512:2 * D + (hf + 1) * 512],
                                                       start=(kc == 0), stop=(kc == DC - 1)),
                              reads=[WQ, HN[kc]], writes=[PS[b]])
                    if hf == 0:
                        mk.op("act", lambda e: e.activation(out=stg[si][:], in_=ps[b][:], func=AF.Copy),
                              reads=[PS[b]], writes=[STG[si]])
                    else:
                        mk.op("dve", lambda e: e.tensor_copy(stg[si][:], ps[b][:]), reads=[PS[b]], writes=[STG[si]])
                    pairs = []
                    for jj in range(2):
                        jb = hf * 2 + jj
                        pairs.append((X.snd_v[t][jb * TT + tb * 128:jb * TT + tb * 128 + 128, :],
                                      stg[si][:, jj * 256:(jj + 1) * 256]))
                    otok[si] = mk.dma("sp", d_stg[si], pairs, reads=[STG[si]])
            X.last = mk.allgather(P.csem, X.snd_v[t], X.gath_v[t], extra=list(otok.values()))


class Pipe:
    def __init__(self, depth, batch):
        self.q = []
        self.depth = depth
        self.batch = batch

    def _n(self):
        return sum(1 for w, _ in self.q if w)

    def push(self, fn, weight=1):
        self.q.append((weight, fn))
        while self._n() >= self.depth + self.batch:
            self._pop()

    def _pop(self):
        items = []
        while self.q and self.q[0][0] == 1 and len(items) < self.batch:
            items.append(self.q.pop(0)[1])
        order = sorted(range(len(items)), key=lambda i: items[i][0])
        for i in order:
            items[i][1]()
        for i in order:
            items[i][2]()
        while self.q and self.q[0][0] == 0:
            self.q.pop(0)[1]()

    def flush(self):
        while self.q:
            self._pop()


def load_kv(P, X, d_kv, kt, vv, KTB, VB):
    pairs = []
    for r in range(4):
        for t in range(4):
            for c in range(2):
                col0 = r * NTOK + t * TT
                pairs.append((kt[:, c, col0:col0 + TT], X.gath_k[t], P.idx[:, r * 2 + c:r * 2 + c + 1]))
            for tb in range(4):
                kbi = r * 16 + t * 4 + tb
                pairs.append((vv[:, kbi, :], X.gath_v[t], P.idx[:, 8 + r * 4 + tb:8 + r * 4 + tb + 1]))
    P.mk.dma("pool", d_kv, pairs, reads=[P.IDX], writes=[KTB, VB], extra=[X.last])


def emit_core_A(P, X, IN, biasT, tag):
    mk = P.mk
    NQG = SEQ // TT
    mk.stage_boundary()
    with ExitStack() as st:
        kt = P.sb("kt", [128, 2, SEQ], BF16, st)
        KTB = Buf("kt")
        vv = P.sb("vv", [128, NKB, 256], BF16, st)
        VB = Buf("vv")
        d_kv = mk.dsem(f"kv_{tag}", "pool")
        bias = P.sb("biasA", [128, 8 * TT], F32, st)
        BIAS = Buf("biasA")
        d_bias = mk.dsem(f"bias_{tag}")
        qs = [P.sb(f"qs{i}", [128, TT], BF16, st) for i in range(2)]
        QS = [Buf(f"qs{i}") for i in range(2)]
        d_q = [mk.dsem(f"q{i}_{tag}", "pool") for i in range(2)]
        tt_ = [P.sb(f"tA{i}", [128, TT], F32, st) for i in range(8)]
        TTB = [Buf(f"tA{i}") for i in range(8)]
        pp = [P.sb(f"pA{i}", [128, TT], BF16, st) for i in range(8)]
        PP = [Buf(f"pA{i}") for i in range(8)]
        s_ps = [P.ps(f"sA_ps{i}", [128, TT], F32, st) for i in range(4)]
        SPS = [Buf(f"sps{i}") for i in range(4)]
        o_ps = [P.ps(f"oA_ps{i}", [64, TT], F32, st) for i in range(2)]
        OPS = [Buf(f"ops{i}") for i in range(2)]
        l_ps = [P.ps(f"lA_ps{i}", [64, TT], F32, st) for i in range(2)]
        LPS = [Buf(f"lps{i}") for i in range(2)]
        rl = P.sb("rlA", [64, TT], F32, st)
        RL = Buf("rl")
        osb = P.sb("osbA", [64, TT], F32, st)
        OSB = Buf("osb")
        on = [P.sb(f"onA{i}", [64, TT], F32, st) for i in range(2)]
        ON = [Buf(f"on{i}") for i in range(2)]
        d_o = [mk.dsem(f"o{i}_{tag}") for i in range(2)]

        load_kv(P, X, d_kv, kt, vv, KTB, VB)
        it = 0
        kk = 0
        otok = {}
        pipe = Pipe(3, 4)
        for hl in range(4):
            c, u = hl // 2, hl % 2
            prow = slice(u * 64, (u + 1) * 64)
            mk.dma("sp", d_bias, [(bias[:, 0:2048], biasT[hl, :, 0:2048]), (bias[:, 2048:4096], biasT[hl, :, 2048:4096])],
                   reads=[IN], writes=[BIAS])
            for qn, qg in enumerate([rd_ * 4 + t_ for t_ in range(4) for rd_ in range(4)]):
                qi = it % 2
                it += 1
                mk.dma("pool", d_q[qi], [(qs[qi][:], X.gath_q[qg % 4], P.idx[:, (qg // 4) * 2 + c:(qg // 4) * 2 + c + 1])],
                       reads=[P.IDX], writes=[QS[qi]], extra=[X.last])
                kbs = [kb for kb in (3, 4, 0, 1, 2, 5, 6, 7) if qg * 4 - 4 + kb >= 0]
                for n_, kb in enumerate(kbs):
                    kbi = qg * 4 - 4 + kb
                    b = kk % 8
                    sb_ = kk % 4
                    kk += 1
                    first, last = (n_ == 0), (n_ == len(kbs) - 1)
                    cs = slice(64 * max(0, 2 * kb - 8), 64 * (min(7, 2 * kb + 1) + 1))
                    bs = slice(kb * TT + cs.start, kb * TT + cs.stop)
                    mk.op("pe", lambda e: e.matmul(s_ps[sb_][:, cs], kt[prow, c, kbi * 128:(kbi + 1) * 128], qs[qi][prow, cs],
                                                   start=True, stop=True),
                          reads=[KTB, QS[qi]], writes=[SPS[sb_]])
                    mk.op("dve", lambda e: e.tensor_tensor(tt_[b][:, cs], s_ps[sb_][:, cs], bias[:, bs], ALU.add),
                          reads=[SPS[sb_], BIAS], writes=[TTB[b]])
                    mk.op("act", lambda e: e.activation(out=pp[b][:, cs], in_=tt_[b][:, cs], func=AF.Exp),
                          reads=[TTB[b]], writes=[PP[b]])

                    def pv_o(b=b, kbi=kbi, hl=hl, qi=qi, first=first, last=last, cs=cs):
                        mk.op("pe", lambda e: e.matmul(o_ps[qi][:, cs], vv[:, kbi, hl * 64:(hl + 1) * 64], pp[b][:, cs],
                                                       start=first, stop=last),
                              reads=[VB, PP[b]], writes=[OPS[qi]])

                    def pv_l(b=b, qi=qi, first=first, last=last, cs=cs):
                        mk.op("pe", lambda e: e.matmul(l_ps[qi][:, cs], P.ones[:, 0:64], pp[b][:, cs], start=first, stop=last),
                              reads=[P.ONES, PP[b]], writes=[LPS[qi]])
                    pipe.push((0, pv_o, pv_l))

                def epi(qi=qi, qg=qg, c=c, u=u):
                    mk.op("dve", lambda e: e.reciprocal(rl[:], l_ps[qi][:]), reads=[LPS[qi]], writes=[RL])
                    mk.op("act", lambda e: e.activation(out=osb[:], in_=o_ps[qi][:], func=AF.Copy),
                          reads=[OPS[qi]], writes=[OSB])
                    mk.op("dve", lambda e: e.tensor_tensor(on[qi][:], osb[:], rl[:], ALU.mult),
                          reads=[OSB, RL], writes=[ON[qi]])
                    rd = qg // 4
                    otok[qi] = mk.dma("sp", d_o[qi], [(X.snd_o[qg % 4][c][rd * 128 + u * 64:rd * 128 + (u + 1) * 64, :], on[qi][:])],
                                      reads=[ON[qi]])
                pipe.push(epi, 0)
                if u == 1 and qn % 4 == 3:
                    def xchg(t_=qn // 4, c=c):
                        X.last_o = mk.allgather(P.csem, X.snd_o[t_][c], X.gath_o[t_][c], extra=list(otok.values()))
                    pipe.push(xchg, 0)
        pipe.flush()


def emit_core_C(P, X, IN, tabT, tabOff, tabD, lamB, tag, lambda_init):
    mk = P.mk
    NQT = SEQ // TT
    mk.stage_boundary()
    with ExitStack() as st:
        kt = P.sb("ktC", [128, 2, SEQ], BF16, st)
        KTB = Buf("kt")
        vv = P.sb("vvC", [128, NKB, 256], BF16, st)
        VB = Buf("vv")
        d_kv = mk.dsem(f"kv_{tag}", "pool")
        tT = P.sb("tT", [128, 2, TT], F32, st)
        tO = P.sb("tO", [128, 2 * 64], F32, st)
        tD = P.sb("tD", [128, 2, 4 * TT], F32, st)
        lam = P.sb("lam", [128, 256], F32, st)
        TAB = Buf("tabs")
        d_tab = mk.dsem(f"tab_{tag}")
        lp = P.sb("lamp", [128, 128], F32, st)
        LP = Buf("lp")
        lsum = P.sb("lsum", [128, 2], F32, st)
        LS = Buf("ls")
        lexp = P.sb("lexp", [128, 2], F32, st)
        LE = Buf("le")
        nlam = P.sb("nlam", [128, 1], F32, st)
        NL = Buf("nl")
        qs = [P.sb(f"qsC{i}", [128, TT], BF16, st) for i in range(2)]
        QS = [Buf(f"qs{i}") for i in range(2)]
        d_q = [mk.dsem(f"q{i}_{tag}", "pool") for i in range(2)]
        tt_ = [P.sb(f"tC{i}", [128, TT], F32, st) for i in range(8)]
        TTB = [Buf(f"tC{i}") for i in range(8)]
        pp = [P.sb(f"pC{i}", [128, TT], BF16, st) for i in range(8)]
        PP = [Buf(f"pC{i}") for i in range(8)]
        s_ps = [P.ps(f"sC_ps{i}", [128, TT], F32, st) for i in range(4)]
        SPS = [Buf(f"sps{i}") for i in range(4)]
        o_ps = [P.ps(f"oC_ps{i}", [128, TT], F32, st) for i in range(2)]
        OPS = [Buf(f"ops{i}") for i in range(2)]
        l_ps = [P.ps(f"lC_ps{i}", [128, TT], F32, st) for i in range(2)]
        LPS = [Buf(f"lps{i}") for i in range(2)]
        rl = [P.sb(f"rlC{i}", [128, TT], F32, st) for i in range(2)]
        RL = [Buf(f"rl{i}") for i in range(2)]
        osb = [P.sb(f"osbC{i}", [128, TT], F32, st) for i in range(2)]
        OSB = [Buf(f"osb{i}") for i in range(2)]
        am = [P.sb(f"amC{i}", [128, TT], F32, st) for i in range(2)]
        AM = [Buf(f"am{i}") for i in range(2)]
        on = [P.sb(f"onC{i}", [128, TT], F32, st) for i in range(2)]
        ON = [Buf(f"on{i}") for i in range(2)]
        d_o = [mk.dsem(f"o{i}_{tag}") for i in range(2)]

        mk.dma("sp", d_tab, [(tT[:, 0, :], tabT[0]), (tT[:, 1, :], tabT[1]), (tO[:], tabOff),
                             (tD[:, 0, :], tabD[0]), (tD[:, 1, :], tabD[1]), (lam[:], lamB)],
               reads=[IN], writes=[TAB])
        load_kv(P, X, d_kv, kt, vv, KTB, VB)
        mk.op("dve", lambda e: e.tensor_tensor(lp[:, 0:64], lam[:, 0:64], lam[:, 64:128], ALU.mult),
              reads=[TAB], writes=[LP])
        mk.op("dve", lambda e: e.tensor_tensor(lp[:, 64:128], lam[:, 128:192], lam[:, 192:256], ALU.mult),
              reads=[TAB, LP], writes=[LP])
        mk.op("dve", lambda e: e.reduce_sum(lsum[:, 0:1], lp[:, 0:64], mybir.AxisListType.X), reads=[LP], writes=[LS])
        mk.op("dve", lambda e: e.reduce_sum(lsum[:, 1:2], lp[:, 64:128], mybir.AxisListType.X), reads=[LP, LS], writes=[LS])
        mk.op("act", lambda e: e.activation(out=lexp[:], in_=lsum[:], func=AF.Exp), reads=[LS], writes=[LE])
        mk.op("dve", lambda e: e.tensor_tensor(nlam[:], lexp[:, 1:2], lexp[:, 0:1], ALU.subtract), reads=[LE], writes=[NL])
        mk.op("dve", lambda e: e.tensor_scalar(nlam[:], nlam[:], -float(lambda_init), None, ALU.add), reads=[NL], writes=[NL])

        it = 0
        kk = 0
        otok = {}
        pipe = Pipe(3, 4)
        for hl in range(2):
            for qn, qt in enumerate([rd_ * 4 + t_ for t_ in range(4) for rd_ in range(4)]):
                qi = it % 2
                it += 1
                mk.dma("pool", d_q[qi], [(qs[qi][:], X.gath_q[qt % 4], P.idx[:, (qt // 4) * 2 + hl:(qt // 4) * 2 + hl + 1])],
                       reads=[P.IDX], writes=[QS[qi]], extra=[X.last])
                nkb = 4 * qt + 4
                for kb in range(nkb):
                    diag = kb >= 4 * qt
                    first, last = (kb == 0), (kb == nkb - 1)
                    for m in range(2):
                        b = kk % 8
                        sb_ = kk % 4
                        kk += 1
                        prow = slice(m * 64, (m + 1) * 64)
                        mk.op("pe", lambda e: e.matmul(s_ps[sb_][:], kt[prow, hl, kb * 128:(kb + 1) * 128], qs[qi][prow, :],
                                                       start=True, stop=True),
                              reads=[KTB, QS[qi]], writes=[SPS[sb_]])
                        if diag:
                            j = kb - 4 * qt
                            mk.op("dve", lambda e: e.tensor_tensor(tt_[b][:], s_ps[sb_][:], tD[:, hl, j * TT:(j + 1) * TT], ALU.add),
                                  reads=[SPS[sb_], TAB], writes=[TTB[b]])
                            mk.op("act", lambda e: e.activation(out=pp[b][:], in_=tt_[b][:], func=AF.Exp),
                                  reads=[TTB[b]], writes=[PP[b]])
                        else:
                            n = 4 * qt - kb
                            mk.op("dve", lambda e: e.tensor_tensor(tt_[b][:], s_ps[sb_][:], tT[:, hl, :], ALU.add),
                                  reads=[SPS[sb_], TAB], writes=[TTB[b]])
                            mk.op("act", lambda e: e.activation(out=pp[b][:], in_=tt_[b][:], func=AF.Exp,
                                                                bias=tO[:, hl * 64 + n:hl * 64 + n + 1]),
                                  reads=[TTB[b], TAB], writes=[PP[b]])

                        def pv_o(b=b, kb=kb, hl=hl, m=m, first=first, last=last):
                            mk.op("pe", lambda e: e.matmul(o_ps[m][:], vv[:, kb, hl * 128:(hl + 1) * 128], pp[b][:],
                                                           start=first, stop=last),
                                  reads=[VB, PP[b]], writes=[OPS[m]])

                        def pv_l(b=b, m=m, first=first, last=last):
                            mk.op("pe", lambda e: e.matmul(l_ps[m][:], P.ones[:], pp[b][:], start=first, stop=last),
                                  reads=[P.ONES, PP[b]], writes=[LPS[m]])
                        pipe.push((m, pv_o, pv_l))

                def epi(qi=qi, qt=qt, hl=hl):
                    for m in range(2):
                        mk.op("dve", lambda e: e.reciprocal(rl[m][:], l_ps[m][:]), reads=[LPS[m]], writes=[RL[m]])
                        mk.op("act", lambda e: e.activation(out=osb[m][:], in_=o_ps[m][:], func=AF.Copy),
                              reads=[OPS[m]], writes=[OSB[m]])
                        mk.op("dve", lambda e: e.tensor_tensor(am[m][:], osb[m][:], rl[m][:], ALU.mult),
                              reads=[OSB[m], RL[m]], writes=[AM[m]])
                    mk.op("dve", lambda e: e.scalar_tensor_tensor(on[qi][:], am[1][:], nlam[:, 0:1], am[0][:], ALU.mult, ALU.add),
                          reads=[AM[0], AM[1], NL], writes=[ON[qi]])
                    rd = qt // 4
                    otok[qi] = mk.dma("sp", d_o[qi], [(X.snd_o[qt % 4][hl][rd * 128:(rd + 1) * 128, :], on[qi][:])],
                                      reads=[ON[qi]])
                pipe.push(epi, 0)
                if qn % 4 == 3:
                    def xchg(t_=qn // 4, hl=hl):
                        X.last_o = mk.allgather(P.csem, X.snd_o[t_][hl], X.gath_o[t_][hl], extra=list(otok.values()))
                    pipe.push(xchg, 0)
        pipe.flush()


def emit_post(P, l, X, IN, w_o_ap, tag, sublnT=None, lambda_init=None):
    mk = P.mk
    mk.stage_boundary()
    with ExitStack() as st:
        d_w = mk.dsem(f"wo_{tag}", "pool")
        wo, WO = load_wres(P, st, "wo", w_o_ap, DC, d_w)
        ws = NormWS(P, st, tag)
        pw = ProjWS(P, st, tag)
        ob = [P.sb(f"ob{i}", [128, DC, TT], F32, st) for i in range(2)]
        OB = [Buf(f"ob{i}") for i in range(2)]
        d_ob = [mk.dsem(f"ob{i}_{tag}", "pool") for i in range(2)]
        onb = P.sb("onb", [128, DC, TT], BF16, st)
        ONB = [Buf(f"onb{dc}") for dc in range(DC)]
        if sublnT is not None:
            sgc = P.sb("sgc", [128, 1], F32, st)
            SGC = Buf("sgc")
            d_sg = mk.dsem(f"sg_{tag}")
            mk.dma("sp", d_sg, [(sgc[:], sublnT)], reads=[IN], writes=[SGC])
            mk.op("dve", lambda e: e.tensor_scalar(sgc[:], sgc[:], float(1.0 - lambda_init), None, ALU.mult),
                  reads=[SGC], writes=[SGC])
        for t in range(NT):
            i = t % 2
            mk.dma("pool", d_ob[i], [(ob[i][:, 2 * r + c, :], X.gath_o[t][c], P.idx[:, 24 + r:25 + r])
                                     for r in range(4) for c in range(2)],
                   reads=[P.IDX], writes=[OB[i]], extra=[X.last_o])
            for fc in range(DC):
                if sublnT is None:
                    if fc % 2 == 0:
                        mk.op("dve", lambda e: e.tensor_copy(onb[:, fc, :], ob[i][:, fc, :]), reads=[OB[i]], writes=[ONB[fc]])
                    else:
                        mk.op("act", lambda e: e.activation(out=onb[:, fc, :], in_=ob[i][:, fc, :], func=AF.Copy),
                              reads=[OB[i]], writes=[ONB[fc]])
                else:
                    si = ws.square(ob[i][:, fc, :], [OB[i]])
                    ws.accum(si, True, True)
                    ws.finish(128)
                    mk.op("dve", lambda e: e.scalar_tensor_tensor(onb[:, fc, :], ob[i][:, fc, :], sgc[:, 0:1],
                                                                  ws.rstd[:], ALU.mult, ALU.mult),
                          reads=[OB[i], SGC, ws.RSTD], writes=[ONB[fc]])
            emit_proj_norm_res(P, ws, pw, t, onb, ONB, DC, wo, WO, l, 3, False)


def emit_mixer_B(P, l, w_in_ap, lngT, lnbT, wsT, triu, bsR, w_o_ap, IN, tag):
    mk = P.mk
    mk.stage_boundary()
    with ExitStack() as st:
        d_w = mk.dsem(f"wiB_{tag}", "pool")
        wi = P.sb("wiB", [128, DC, 2 * D], BF16, st)
        WI = Buf("wiB")
        wv = w_in_ap.rearrange("(kc p) c -> p kc c", p=128)
        mk.dma("pool", d_w, [(wi[:, :, i * 512:(i + 1) * 512], wv[:, :, i * 512:(i + 1) * 512]) for i in range(4)],
               writes=[WI])
        d_wo = mk.dsem(f"woB_{tag}", "pool")
        wo, WO = load_wres(P, st, "woB", w_o_ap, DC, d_wo)
        d_c = mk.dsem(f"cB_{tag}")
        lng = P.sb("lng", [128, 8], F32, st)
        lnb = P.sb("lnb", [128, 8], F32, st)
        addt = P.sb("addt", [128, 8 * 128], F32, st)
        ADDT = Buf("addt")
        wsb = P.sb("wsb", [128, 8 * 128], BF16, st)
        WSB = Buf("wsb")
        CB = Buf("cB")
        ps = [P.ps(f"B_ps{i}", [128, TT], F32, st) for i in range(4)]
        PS = [Buf(f"Bps{i}") for i in range(4)]
        with ExitStack() as st2:
            wsf = P.sb("wsf", [128, 8 * 128], F32, st2)
            tri = P.sb("tri", [128, 128], F32, st2)
            bsr = P.sb("bsr", [128, 8 * 128], F32, st2)
            mk.dma("sp", d_c, [(lng[:], lngT), (lnb[:], lnbT), (wsf[:], wsT), (tri[:], triu), (bsr[:], bsR)],
                   reads=[IN], writes=[CB])
            for g in range(8):
                gs = slice(g * 128, (g + 1) * 128)
                mk.op("dve", lambda e: e.tensor_tensor(wsb[:, gs], wsf[:, gs], tri[:], ALU.mult),
                      reads=[CB, WSB], writes=[WSB])
            for g in range(8):
                gs = slice(g * 128, (g + 1) * 128)
                b = g % 4
                mk.op("pe", lambda e: e.matmul(ps[b][:, 0:128], P.ones[:], wsb[:, gs], start=True, stop=True),
                      reads=[P.ONES, WSB], writes=[PS[b]])
                mk.op("dve", lambda e: e.scalar_tensor_tensor(addt[:, gs], ps[b][:, 0:128], lnb[:, g:g + 1], bsr[:, gs],
                                                              ALU.mult, ALU.add),
                      reads=[PS[b], CB, ADDT], writes=[ADDT])
        mk.stage_boundary()
        ws = NormWS(P, st, tag)
        pw = ProjWS(P, st, tag)
        hn = P.sb("hnB", [128, DC, TT], BF16, st)
        HN = [Buf(f"hn{dc}") for dc in range(DC)]
        u = P.sb("uB", [128, DC, TT], F32, st)
        U = [Buf(f"u{dc}") for dc in range(DC)]
        vf = P.sb("vfB", [128, D], F32, st)
        VF = Buf("vf")
        vn = P.sb("vnB", [128, 4, D], BF16, st)
        VN = [Buf(f"vn{tb}") for tb in range(4)]
        st4 = P.sb("st4", [128, 8], F32, st)
        ST4 = Buf("st4")
        junk = P.sb("junkB", [128, D], F32, st)
        JK = Buf("junk")
        t1 = [P.sb(f"t1B{i}", [128, TT], F32, st) for i in range(2)]
        T1 = [Buf(f"t1{i}") for i in range(2)]
        k = 0
        for t in range(NT):
            emit_norm_in(P, ws, t, l, 2, hn, HN)
            for fc in range(DC):
                b = k % 4
                k += 1
                for kc in range(DC):
                    mk.op("pe", lambda e: e.matmul(ps[b][:], wi[:, kc, fc * 128:(fc + 1) * 128], hn[:, kc, :],
                                                   start=(kc == 0), stop=(kc == DC - 1)),
                          reads=[WI, HN[kc]], writes=[PS[b]])
                mk.op("act", lambda e: e.activation(out=u[:, fc, :], in_=ps[b][:], func=AF.Gelu),
                      reads=[PS[b]], writes=[U[fc]])
            for tb in range(4):
                for hf in range(2):
                    b = k % 4
                    k += 1
                    for kc in range(DC):
                        mk.op("pe", lambda e: e.matmul(ps[b][:], hn[:, kc, tb * 128:(tb + 1) * 128],
                                                       wi[:, kc, D + hf * 512:D + (hf + 1) * 512],
                                                       start=(kc == 0), stop=(kc == DC - 1)),
                              reads=[WI, HN[kc]], writes=[PS[b]])
                    mk.op("act", lambda e: e.activation(out=vf[:, hf * 512:(hf + 1) * 512], in_=ps[b][:], func=AF.Gelu),
                          reads=[PS[b], VF], writes=[VF])
                mk.op("dve", lambda e: e.reduce_sum(st4[:, 0:1], vf[:], mybir.AxisListType.X),
                      reads=[VF, ST4], writes=[ST4])
                mk.op("act", lambda e: e.activation(out=junk[:], in_=vf[:], func=AF.Square),
                      reads=[VF, JK], writes=[JK])
                mk.op("dve", lambda e: e.reduce_sum(st4[:, 1:2], junk[:], mybir.AxisListType.X),
                      reads=[JK, ST4], writes=[ST4])
                mk.op("dve", lambda e: e.tensor_scalar(st4[:, 2:3], st4[:, 0:1], 1.0 / D, None, ALU.mult),
                      reads=[ST4], writes=[ST4])
                mk.op("dve", lambda e: e.tensor_tensor(st4[:, 5:6], st4[:, 2:3], st4[:, 2:3], ALU.mult),
                      reads=[ST4], writes=[ST4])
                mk.op("dve", lambda e: e.scalar_tensor_tensor(st4[:, 6:7], st4[:, 1:2], 1.0 / D, st4[:, 5:6],
                                                              ALU.mult, ALU.subtract),
                      reads=[ST4], writes=[ST4])
                mk.op("act", lambda e: e.activation(out=st4[:, 7:8], in_=st4[:, 6:7], func=AF.Sqrt, bias=P.epsb[:], scale=1.0),
                      reads=[ST4, P.EPSB], writes=[ST4])
                mk.op("dve", lambda e: e.reciprocal(st4[:, 3:4], st4[:, 7:8]), reads=[ST4], writes=[ST4])
                mk.op("dve", lambda e: e.scalar_tensor_tensor(st4[:, 4:5], st4[:, 2:3], -1.0, st4[:, 3:4], ALU.mult, ALU.mult),
                      reads=[ST4], writes=[ST4])
                mk.op("act", lambda e: e.activation(out=vn[:, tb, :], in_=vf[:], func=AF.Identity,
                                                    bias=st4[:, 4:5], scale=st4[:, 3:4]),
                      reads=[VF, ST4], writes=[VN[tb]])
            for g in range(8):
                gs = slice(g * 128, (g + 1) * 128)
                b = k % 4
                k += 1
                i = g % 2
                for tb in range(4):
                    mk.op("pe", lambda e: e.matmul(ps[b][:, tb * 128:(tb + 1) * 128], vn[:, tb, gs],
                                                   wsb[:, gs], start=True, stop=True),
                          reads=[VN[tb], WSB, PS[b]], writes=[PS[b]])
                for tb in range(4):
                    tbs = slice(tb * 128, (tb + 1) * 128)
                    mk.op("dve", lambda e: e.scalar_tensor_tensor(t1[i][:, tbs], ps[b][:, tbs], lng[:, g:g + 1],
                                                                  addt[:, gs], ALU.mult, ALU.add),
                          reads=[PS[b], CB, ADDT, T1[i]], writes=[T1[i]])
                mk.op("dve", lambda e: e.tensor_tensor(hn[:, g, :], t1[i][:], u[:, g, :], ALU.mult),
                      reads=[T1[i], U[g]], writes=[HN[g]])
            emit_proj_norm_res(P, ws, pw, t, hn, HN, DC, wo, WO, l, 3, False)


def load_x(P, xT_ap):
    mk = P.mk
    P.d_x = mk.dsem("xin")
    v = xT_ap.rearrange("(dc p) t -> p dc t", p=128)
    mk.dma("sp", P.d_x, [(P.xs[:, dc, :], v[:, dc, :]) for dc in range(DC)],
           writes=[b for dc in range(DC) for b in P.X[dc]])


def store_x(P, oT_ap):
    mk = P.mk
    P.d_xo = mk.dsem("xout")
    v = oT_ap.rearrange("(dc p) t -> p dc t", p=128)
    return mk.dma("sp", P.d_xo, [(v[:, dc, :], P.xs[:, dc, :]) for dc in range(DC)],
                  reads=[b for dc in range(DC) for b in P.X[dc]])


I32 = mybir.dt.int32
WNAMES = ["ff1_w_in", "ff1_w_out", "ff2_w_in", "ff2_w_out", "a_w_qkv", "a_w_o", "b_w_in", "b_w_o", "c_w_qkv", "c_w_o"]


class XChg:
    def __init__(self, nc, l):
        def dt_(name, shape, dt):
            return nc.dram_tensor(f"{name}_L{l}", list(shape), dt).ap()
        self.snd_q = [dt_(f"sq{t}", [D, TT], BF16) for t in range(NT)]
        self.snd_k = [dt_(f"sk{t}", [D, TT], BF16) for t in range(NT)]
        self.snd_v = [dt_(f"sv{t}", [4 * TT, 256], BF16) for t in range(NT)]
        self.gath_q = [dt_(f"gq{t}", [4 * D, TT], BF16) for t in range(NT)]
        self.gath_k = [dt_(f"gk{t}", [4 * D, TT], BF16) for t in range(NT)]
        self.gath_v = [dt_(f"gv{t}", [16 * TT, 256], BF16) for t in range(NT)]
        self.snd_o = [[dt_(f"so{t}_{c}", [4 * 128, TT], F32) for c in range(2)] for t in range(NT)]
        self.gath_o = [[dt_(f"go{t}_{c}", [16 * 128, TT], F32) for c in range(2)] for t in range(NT)]
        self.last = None
        self.last_o = None


def build_fused(layers=(0, 1, 2, 3), ffn=True):
    nc = bass.Bass("TRN2", target_bir_lowering=False)

    def ext(name, shape, dt=F32):
        return nc.dram_tensor(name, list(shape), dt, kind="ExternalInput").ap()
    xT = ext("xT", [D, NTOK])
    gT = ext("gT", [128, DEPTH * 6 * DC])
    idx = ext("idx", [128, 28], I32)
    W = {"ff1_w_in": ext("ff1_w_in", [DEPTH, D, 2 * DFF]), "ff1_w_out": ext("ff1_w_out", [DEPTH, DFF, D]),
         "ff2_w_in": ext("ff2_w_in", [DEPTH, D, 2 * DFF]), "ff2_w_out": ext("ff2_w_out", [DEPTH, DFF, D]),
         "a_w_qkv": ext("a_w_qkv", [2, D, 3 * D]), "a_w_o": ext("a_w_o", [2, D, D]),
         "b_w_in": ext("b_w_in", [1, D, 2 * D]), "b_w_o": ext("b_w_o", [1, D, D]),
         "c_w_qkv": ext("c_w_qkv", [1, D, 3 * D]), "c_w_o": ext("c_w_o", [1, D, D])}
    biasT = [ext("biasT0", [4, 128, 8 * TT]), ext("biasT1", [4, 128, 8 * TT])]
    tabT = ext("tabT", [2, 128, TT])
    tabOff = ext("tabOff", [128, 128])
    tabD = ext("tabD", [2, 128, 4 * TT])
    lamB = ext("lamB", [128, 256])
    sublnT = ext("sublnT", [128, 1])
    lngT = ext("lngT", [128, 8])
    lnbT = ext("lnbT", [128, 8])
    wsT = ext("wsT", [128, 8 * 128])
    triu = ext("triu", [128, 128])
    bsR = ext("bsR", [128, 8 * 128])
    xTo = nc.dram_tensor("xTo", [D, NTOK], F32, kind="ExternalOutput").ap()
    with ExitStack() as stack:
        mk = MK(nc, stack)
        P = Prog(nc, stack, mk)
        setup_common(P, gT)
        P.idx = P.sb("idx", [128, 28], I32)
        P.IDX = Buf("idx")
        mk.dma("sp", mk.dsem("idx"), [(P.idx[:], idx)], writes=[P.IDX])
        csem_h = stack.enter_context(nc.semaphore("csem"))
        P.csem = DSem("csem", csem_h)
        load_x(P, xT)
        IN = Buf("in")
        for l in layers:
            kind, jm = l % 3, l // 3
            if ffn:
                with nc.named_scope(f"L{l}_ffn1"):
                    emit_ffn(P, l, 0, W["ff1_w_in"][l], W["ff1_w_out"][l])
            if kind == 1:
                with nc.named_scope(f"L{l}_mixB"):
                    emit_mixer_B(P, l, W["b_w_in"][jm], lngT, lnbT, wsT, triu, bsR, W["b_w_o"][jm], IN, f"mB{l}")
            else:
                X = XChg(nc, l)
                if kind == 0:
                    with nc.named_scope(f"L{l}_pre"):
                        emit_pre(P, l, W["a_w_qkv"][jm], X, f"pre{l}")
                    with nc.named_scope(f"L{l}_core"):
                        emit_core_A(P, X, IN, biasT[jm], f"cA{l}")
                    with nc.named_scope(f"L{l}_post"):
                        emit_post(P, l, X, IN, W["a_w_o"][jm], f"post{l}")
                else:
                    li = 0.8 - 0.6 * float(np.exp(-0.3 * l))
                    with nc.named_scope(f"L{l}_pre"):
                        emit_pre(P, l, W["c_w_qkv"][jm], X, f"pre{l}")
                    with nc.named_scope(f"L{l}_core"):
                        emit_core_C(P, X, IN, tabT, tabOff, tabD, lamB, f"cC{l}", li)
                    with nc.named_scope(f"L{l}_post"):
                        emit_post(P, l, X, IN, W["c_w_o"][jm], f"post{l}", sublnT, li)
            if ffn:
                with nc.named_scope(f"L{l}_ffn2"):
                    emit_ffn(P, l, 1, W["ff2_w_in"][l], W["ff2_w_out"][l])
        mk.wait_all("sp", [store_x(P, xTo)])
    return nc


def gains_T(norm_g_l):
    out = np.zeros((128, DEPTH * 6 * DC), np.float32)
    out[:, :6 * DC] = norm_g_l.reshape(6, DC, 128).transpose(2, 0, 1).reshape(128, 6 * DC)
    return out


def feat_cols(v, n):
    return np.ascontiguousarray(v.reshape(n, 128).T)


def bias_tables_A(rel_bias, j):
    k_in = np.arange(128)[:, None, None]
    kb = np.arange(8)[None, :, None]
    q_rel = np.arange(512)[None, None, :]
    krel = kb * 128 + k_in
    valid = (krel // 64 >= q_rel // 64) & (krel // 64 <= q_rel // 64 + 8)
    idx = np.clip(q_rel + 512 - krel, -128, 128) + 128
    out = np.empty((4, 128, 8, 512), np.float32)
    for hl in range(4):
        g = rel_bias[4 * j + hl][idx]
        out[hl] = np.where(valid, g, np.float32(NEG))
    return out.reshape(4, 128, 8 * 512)


def tables_C(j):
    tabT = np.empty((2, 128, 512), np.float32)
    tabOff = np.empty((128, 128), np.float32)
    tabD = np.empty((2, 128, 4, 512), np.float32)
    k_in = np.arange(128)[:, None]
    q_rel = np.arange(512)[None, :]
    for hl in range(2):
        h = 2 * j + hl
        slope = np.float32(2.0 ** (-(h + 1)))
        tabT[hl] = -slope * (q_rel - k_in).astype(np.float32)
        tabOff[:, hl * 64:(hl + 1) * 64] = (-slope * 128.0 * np.arange(64, dtype=np.float32))[None, :]
        for jj in range(4):
            k_rel = jj * 128 + k_in
            allowed = k_rel < (q_rel // 64 + 1) * 64
            tabD[hl, :, jj, :] = np.where(allowed, -slope * np.abs(q_rel - k_rel).astype(np.float32), np.float32(NEG))
    return tabT, tabOff, tabD.reshape(2, 128, 4 * 512)


def gains_all(norm_g):
    return np.ascontiguousarray(norm_g.reshape(DEPTH, 6, DC, 128).transpose(3, 0, 1, 2).reshape(128, DEPTH * 6 * DC))


def idx_table(j):
    p = np.arange(128, dtype=np.int32)
    t = np.zeros((128, 28), np.int32)
    for r in range(4):
        for c in range(2):
            t[:, r * 2 + c] = r * 1024 + 256 * j + c * 128 + p
        for tb in range(4):
            t[:, 8 + r * 4 + tb] = r * 2048 + j * 512 + tb * 128 + p
        t[:, 24 + r] = r * 512 + j * 128 + p
    return t


def make_in_maps(inp):
    x = inp["x"]
    shared = {k: inp[k] for k in WNAMES}
    ws = inp["b_w_s"][0]
    shared.update({
        "gT": gains_all(inp["norm_g"]),
        "lamB": np.ascontiguousarray(np.broadcast_to(inp["c_lambda"][0].reshape(1, 256), (128, 256))),
        "sublnT": np.ascontiguousarray(inp["c_subln_g"][0].reshape(128, 1)),
        "lngT": feat_cols(inp["b_ln_g"][0], 8), "lnbT": feat_cols(inp["b_ln_b"][0], 8),
        "wsT": np.ascontiguousarray(ws.transpose(2, 0, 1).reshape(128, 8 * 128)),
        "triu": np.triu(np.ones((128, 128), np.float32)),
        "bsR": np.ascontiguousarray(np.broadcast_to(inp["b_b_s"][0].reshape(1, 8 * 128), (128, 8 * 128))),
    })
    in_maps = []
    for c in range(NCORES):
        b, j = c // 4, c % 4
        tabT, tabOff, tabD = tables_C(j)
        m = dict(shared)
        m.update({"xT": np.ascontiguousarray(x[b, NTOK * j:NTOK * (j + 1), :].T), "idx": idx_table(j),
                  "biasT0": bias_tables_A(inp["a_rel_bias"][0], j), "biasT1": bias_tables_A(inp["a_rel_bias"][1], j),
                  "tabT": tabT, "tabOff": tabOff, "tabD": tabD})
        in_maps.append(m)
    return in_maps


def kernel(**inputs):
    inp = {k: np.ascontiguousarray(np.asarray(v)) for k, v in inputs.items()}
    x = inp["x"]
    nc = build_fused()
    in_maps = make_in_maps(inp)
    res = run_bass_kernel_spmd(nc, in_maps, core_ids=list(range(NCORES)))
    out = np.empty_like(x)
    for c in range(NCORES):
        b, j = c // 4, c % 4
        out[b, NTOK * j:NTOK * (j + 1), :] = res.results[c]["xTo"].T
    return out
```

```python
import numpy as np
import concourse.bass as bass
import concourse.mybir as mybir
from concourse.bass_utils import run_bass_kernel_spmd

F32 = mybir.dt.float32
BF16 = mybir.dt.bfloat16
AF = mybir.ActivationFunctionType
ALU = mybir.AluOpType

D = 1024
DC = 8
DFF = 2816
FC = 22
NTOK = 2048
TT = 512
NT = NTOK // TT
EPS = 1e-6
DEPTH = 4
NCORES = 8


_FENCE = {}


class Buf:
    __slots__ = ("name", "w", "r")

    def __init__(self, name):
        self.name = name
        self.w = None
        self.r = dict(_FENCE)


class DSem:
    def __init__(self, key, sem):
        self.key = key
        self.sem = sem
        self.cnt = 0


class EngState:
    def __init__(self, name, eng, sem):
        self.name = name
        self.eng = eng
        self.sem = sem
        self.cnt = 0
        self.waited = {}


class _PEProxy:
    def __init__(self, eng):
        self.eng = eng
        self.last_stop = True

    def matmul(self, *a, **kw):
        self.last_stop = bool(kw.get("stop", True))
        return self.eng.matmul(*a, **kw)


class MK:
    def __init__(self, nc, stack):
        self.nc = nc
        self.stack = stack
        self.engs = {}
        for name, eng in (("pe", nc.tensor), ("act", nc.scalar), ("dve", nc.vector),
                          ("pool", nc.gpsimd), ("sp", nc.sync)):
            sem = stack.enter_context(nc.semaphore("s_" + name))
            self.engs[name] = EngState(name, eng, sem)
        self.dsems = []
        self.free_dsems = {"sp": [], "pool": []}
        self.stage_dsems = []
        _FENCE.clear()

    def dsem(self, name, q="sp"):
        if self.free_dsems[q]:
            d = self.free_dsems[q].pop()
        else:
            sem = self.stack.enter_context(self.nc.semaphore("d_%d" % len(self.dsems)))
            d = DSem("d_%d" % len(self.dsems), sem)
            d.q = q
            self.dsems.append(d)
        self.stage_dsems.append(d)
        return d

    def stage_boundary(self):
        for d in self.stage_dsems:
            self.free_dsems[d.q].append(d)
        self.stage_dsems = []
        _FENCE.clear()
        for E in self.engs.values():
            if E.cnt > 0:
                _FENCE[E.name] = (E.name, E.sem, E.cnt)
        for d in self.dsems:
            if d.cnt > 0:
                _FENCE[d.key] = (d.key, d.sem, d.cnt)

    def _wait(self, E, toks):
        need = {}
        for tok in toks:
            if tok is None:
                continue
            key, sem, val = tok
            if key == "pe" and E.name == "pe":
                continue
            if key not in need or need[key][1] < val:
                need[key] = (sem, val)
        for key, (sem, val) in need.items():
            if E.waited.get(key, 0) < val:
                E.eng.wait_ge(sem, val)
                E.waited[key] = val

    def _deps(self, reads, writes):
        toks = []
        for b in reads:
            toks.append(b.w)
        for b in writes:
            toks.append(b.w)
            toks.extend(b.r.values())
        return toks

    def op(self, ename, fn, reads=(), writes=(), force_inc=False):
        E = self.engs[ename]
        self._wait(E, self._deps(reads, writes))
        if ename == "pe":
            prox = _PEProxy(E.eng)
            ins = fn(prox)
            if not prox.last_stop and not force_inc:
                tok = (ename, E.sem, E.cnt + 1)
                for b in reads:
                    b.r[ename] = tok
                for b in writes:
                    b.w = tok
                    b.r = {}
                return tok
        else:
            ins = fn(E.eng)
        E.cnt += 1
        ins.then_inc(E.sem, 1)
        tok = (ename, E.sem, E.cnt)
        for b in reads:
            b.r[ename] = tok
        for b in writes:
            b.w = tok
            b.r = {}
        return tok

    def dma(self, qname, dsem, pairs, reads=(), writes=(), extra=(), **kw):
        E = self.engs[qname]
        assert dsem.q == qname, (dsem.key, dsem.q, qname)
        self._wait(E, self._deps(reads, writes) + list(extra))
        for pr in pairs:
            if len(pr) == 3:
                E.eng.indirect_dma_start(out=pr[0], out_offset=None, in_=pr[1],
                                         in_offset=bass.IndirectOffsetOnAxis(ap=pr[2], axis=0)).then_inc(dsem.sem, 16)
            else:
                E.eng.dma_start(out=pr[0], in_=pr[1], **kw).then_inc(dsem.sem, 16)
            dsem.cnt += 16
        tok = (dsem.key, dsem.sem, dsem.cnt)
        for b in reads:
            b.r[dsem.key] = tok
        for b in writes:
            b.w = tok
            b.r = {}
        return tok

    def allgather(self, csem, src_ap, dst_ap, extra=(), writes=()):
        E = self.engs["pool"]
        self._wait(E, self._deps((), writes) + list(extra))
        E.eng.collective_compute("AllGather", ALU.bypass, replica_groups=[[0, 1, 2, 3], [4, 5, 6, 7]],
                                 ins=[src_ap], outs=[dst_ap]).then_inc(csem.sem, 1)
        csem.cnt += 1
        tok = (csem.key, csem.sem, csem.cnt)
        for b in writes:
            b.w = tok
            b.r = {}
        return tok

    def wait_all(self, ename, toks):
        self._wait(self.engs[ename], toks)


from contextlib import ExitStack

SEQ = 8192
NKB = SEQ // 128
NEG = -30000.0
LAMBDA_INIT_L2 = 0.8 - 0.6 * float(np.exp(-0.3 * 2))


class Prog:
    def __init__(self, nc, stack, mk):
        self.nc = nc
        self.stack = stack
        self.mk = mk

    def sb(self, name, shape, dt, stack=None):
        self.uid = getattr(self, "uid", 0) + 1
        return (stack or self.stack).enter_context(self.nc.sbuf_tensor(f"{name}_{self.uid}", shape, dt))

    def ps(self, name, shape, dt=F32, stack=None):
        self.uid = getattr(self, "uid", 0) + 1
        return (stack or self.stack).enter_context(self.nc.psum_tensor(f"{name}_{self.uid}", shape, dt))


def gcol(l, n, dc):
    return (l * 6 + n) * DC + dc


def setup_common(P, gT_ap):
    nc, mk = P.nc, P.mk
    P.xs = P.sb("xs", [128, DC, NTOK], F32)
    P.X = [[Buf(f"x{dc}_{t}") for t in range(NT)] for dc in range(DC)]
    P.ones = P.sb("ones", [128, 128], BF16)
    P.ONES = Buf("ones")
    P.epsb = P.sb("epsb", [128, 1], F32)
    P.EPSB = Buf("epsb")
    P.g = P.sb("g", [128, DEPTH * 6 * DC], F32)
    P.hg = P.sb("hg", [128, DEPTH * 6 * DC], F32)
    P.G = Buf("g")
    P.HG = Buf("hg")
    P.d_const = mk.dsem("const")
    mk.op("dve", lambda e: e.memset(P.ones[:], 1.0), writes=[P.ONES])
    mk.op("dve", lambda e: e.memset(P.epsb[:], EPS), writes=[P.EPSB])
    mk.dma("sp", P.d_const, [(P.g[:], gT_ap)], writes=[P.G])
    mk.op("dve", lambda e: e.tensor_scalar(P.hg[:], P.g[:], 0.5, None, ALU.mult),
          reads=[P.G], writes=[P.HG])


class NormWS:
    def __init__(self, P, st, tag):
        self.P = P
        self.sq = [P.sb(f"sq{i}_{tag}", [128, TT], BF16, st) for i in range(2)]
        self.SQ = [Buf(f"sq{i}") for i in range(2)]
        self.rt = P.sb(f"rt_{tag}", [128, TT], F32, st)
        self.RT = Buf("rt")
        self.rstd = P.sb(f"rstd_{tag}", [128, TT], F32, st)
        self.RSTD = Buf("rstd")
        self.ss_ps = P.ps(f"ss_ps_{tag}", [128, TT], F32, st)
        self.SSPS = Buf("ssps")
        self.k = 0

    def square(self, src_ap, SRC):
        i = self.k % 2
        self.k += 1
        self.P.mk.op("act", lambda e: e.activation(out=self.sq[i][:], in_=src_ap, func=AF.Square),
                     reads=SRC, writes=[self.SQ[i]])
        return i

    def accum(self, i, first, last):
        P = self.P
        P.mk.op("pe", lambda e: e.matmul(self.ss_ps[:], P.ones[:], self.sq[i][:], start=first, stop=last),
                reads=[P.ONES, self.SQ[i]], writes=[self.SSPS], force_inc=True)

    def finish(self, n):
        P = self.P
        P.mk.op("act", lambda e: e.activation(out=self.rt[:], in_=self.ss_ps[:], func=AF.Sqrt,
                                              bias=P.epsb[:], scale=1.0 / n),
                reads=[self.SSPS, P.EPSB], writes=[self.RT])
        P.mk.op("dve", lambda e: e.reciprocal(self.rstd[:], self.rt[:]), reads=[self.RT], writes=[self.RSTD])


def emit_norm_in(P, ws, t, l, n, xn, XN):
    mk = P.mk
    tsl = slice(t * TT, (t + 1) * TT)
    for dc in range(DC):
        i = ws.square(P.xs[:, dc, tsl], [P.X[dc][t]])
        ws.accum(i, dc == 0, dc == DC - 1)
    ws.finish(D)
    for dc in range(DC):
        c = gcol(l, n, dc)
        mk.op("dve", lambda e: e.scalar_tensor_tensor(xn[:, dc, :], P.xs[:, dc, tsl], P.g[:, c:c + 1],
                                                      ws.rstd[:], ALU.mult, ALU.mult),
              reads=[P.X[dc][t], P.G, ws.RSTD], writes=[XN[dc]])


class ProjWS:
    def __init__(self, P, st, tag):
        self.y = P.sb(f"y_{tag}", [128, DC, TT], F32, st)
        self.Y = [Buf(f"y{dc}") for dc in range(DC)]
        self.y_ps = [P.ps(f"y_ps{i}_{tag}", [128, TT], F32, st) for i in range(2)]
        self.YPS = [Buf(f"yps{i}") for i in range(2)]
        self.tmp = [P.sb(f"tmp{i}_{tag}", [128, TT], F32, st) for i in range(2)]
        self.TMP = [Buf(f"tmp{i}") for i in range(2)]


def emit_proj_norm_res(P, ws, pw, t, rhs, RHS, nfc, wres, WRES, l, n, half):
    mk = P.mk
    tsl = slice(t * TT, (t + 1) * TT)
    gt = P.hg if half else P.g
    GT = P.HG if half else P.G
    pend = None
    for dc in range(DC):
        b = dc % 2
        for fc in range(nfc):
            mk.op("pe", lambda e: e.matmul(pw.y_ps[b][:], wres[:, fc, dc * 128:(dc + 1) * 128], rhs[:, fc, :],
                                           start=(fc == 0), stop=(fc == nfc - 1)),
                  reads=[WRES, RHS[fc]], writes=[pw.YPS[b]])
        if pend is not None:
            ws.accum(pend, dc - 1 == 0, False)
        mk.op("dve", lambda e: e.tensor_copy(pw.y[:, dc, :], pw.y_ps[b][:]), reads=[pw.YPS[b]], writes=[pw.Y[dc]])
        pend = ws.square(pw.y[:, dc, :], [pw.Y[dc]])
    ws.accum(pend, False, True)
    ws.finish(D)
    for dc in range(DC):
        i = dc % 2
        c = gcol(l, n, dc)
        mk.op("dve", lambda e: e.tensor_tensor(pw.tmp[i][:], pw.y[:, dc, :], ws.rstd[:], ALU.mult),
              reads=[pw.Y[dc], ws.RSTD], writes=[pw.TMP[i]])
        mk.op("dve", lambda e: e.scalar_tensor_tensor(P.xs[:, dc, tsl], pw.tmp[i][:], gt[:, c:c + 1],
                                                      P.xs[:, dc, tsl], ALU.mult, ALU.add),
              reads=[pw.TMP[i], GT, P.X[dc][t]], writes=[P.X[dc][t]])


def emit_ffn(P, l, which, w_in_ap, w_out_ap):
    nc, mk = P.nc, P.mk
    na, nb = (0, 1) if which == 0 else (4, 5)
    GF = 2
    NG = FC // GF
    NS = 2
    tag = f"f{l}{which}"
    mk.stage_boundary()
    with ExitStack() as st:
        xn2 = [P.sb(f"xn{i}", [128, DC, TT], BF16, st) for i in range(2)]
        XN2 = [[Buf(f"xn{i}_{dc}") for dc in range(DC)] for i in range(2)]
        a = P.sb("a", [128, FC, TT], BF16, st)
        A = [Buf(f"a{fc}") for fc in range(FC)]
        win = [P.sb(f"win{s}", [128, DC, 2 * GF * 128], BF16, st) for s in range(NS)]
        WIN = [Buf(f"win{s}") for s in range(NS)]
        d_win = [mk.dsem(f"win{s}_{tag}", "pool") for s in range(NS)]
        wout = P.sb("wout", [128, FC, D], BF16, st)
        WOUT = Buf("wout")
        d_wout = mk.dsem(f"wout_{tag}", "pool")
        sg = [P.sb(f"sg{i}", [128, TT], F32, st) for i in range(2)]
        SG = [Buf(f"sg{i}") for i in range(2)]
        gate_ps = [P.ps(f"gate_ps{i}", [128, TT], F32, st) for i in range(2)]
        GPS = [Buf(f"gps{i}") for i in range(2)]
        up_ps = [P.ps(f"up_ps{i}", [128, TT], F32, st) for i in range(2)]
        UPS = [Buf(f"ups{i}") for i in range(2)]
        ws = NormWS(P, st, tag)
        pw = ProjWS(P, st, tag)

        w_in_v = w_in_ap.rearrange("(kc p) c -> p kc c", p=128)
        w_out_v = w_out_ap.rearrange("(fc p) d -> p fc d", p=128)

        def load_win(t, g):
            s = (t * NG + g) % NS
            c0 = g * GF * 128
            mk.dma("pool", d_win[s],
                   [(win[s][:, :, 0:GF * 128], w_in_v[:, :, c0:c0 + GF * 128]),
                    (win[s][:, :, GF * 128:2 * GF * 128], w_in_v[:, :, DFF + c0:DFF + c0 + GF * 128])],
                   writes=[WIN[s]])

        mk.dma("pool", d_wout,
               [(wout[:, 0:11, :], w_out_v[:, 0:11, :]), (wout[:, 11:22, :], w_out_v[:, 11:22, :])],
               writes=[WOUT])

        emit_norm_in(P, ws, 0, l, na, xn2[0], XN2[0])
        for t in range(NT):
            xn, XN = xn2[t % 2], XN2[t % 2]
            load_win(t, 0)
            k = 0
            for g in range(NG):
                s = (t * NG + g) % NS
                if g + 1 < NG:
                    load_win(t, g + 1)
                if g == NG - 3 and t + 1 < NT:
                    emit_norm_in(P, ws, t + 1, l, na, xn2[(t + 1) % 2], XN2[(t + 1) % 2])
                for j in range(GF):
                    fc = g * GF + j
                    b = k % 2
                    k += 1
                    for kc in range(DC):
                        mk.op("pe", lambda e: e.matmul(gate_ps[b][:], win[s][:, kc, j * 128:(j + 1) * 128],
                                                       xn[:, kc, :], start=(kc == 0), stop=(kc == DC - 1)),
                              reads=[WIN[s], XN[kc]], writes=[GPS[b]])
                    for kc in range(DC):
                        mk.op("pe", lambda e: e.matmul(up_ps[b][:],
                                                       win[s][:, kc, GF * 128 + j * 128:GF * 128 + (j + 1) * 128],
                                                       xn[:, kc, :], start=(kc == 0), stop=(kc == DC - 1)),
                              reads=[WIN[s], XN[kc]], writes=[UPS[b]])
                    mk.op("act", lambda e: e.activation(out=sg[b][:], in_=gate_ps[b][:], func=AF.Silu),
                          reads=[GPS[b]], writes=[SG[b]])
                    mk.op("dve", lambda e: e.tensor_tensor(a[:, fc, :], sg[b][:], up_ps[b][:], ALU.mult),
                          reads=[SG[b], UPS[b]], writes=[A[fc]])
            emit_proj_norm_res(P, ws, pw, t, a, A, FC, wout, WOUT, l, nb, True)


def load_wres(P, st, name, w_ap, nfc, dsem):
    w = P.sb(name, [128, nfc, D], BF16, st)
    W = Buf(name)
    v = w_ap.rearrange("(fc p) d -> p fc d", p=128)
    h = nfc // 2
    P.mk.dma("pool", dsem, [(w[:, 0:h, :], v[:, 0:h, :]), (w[:, h:nfc, :], v[:, h:nfc, :])], writes=[W])
    return w, W


def emit_pre(P, l, wqkv_ap, X, tag):
    mk = P.mk
    mk.stage_boundary()
    with ExitStack() as st:
        hn = P.sb("hn", [128, DC, TT], BF16, st)
        HN = [Buf(f"hn{dc}") for dc in range(DC)]
        wq = P.sb("wqkv", [128, DC, 3 * D], BF16, st)
        WQ = Buf("wqkv")
        d_w = mk.dsem(f"wqkv_{tag}", "pool")
        ws = NormWS(P, st, tag)
        ps = [P.ps(f"pre_ps{i}", [128, TT], F32, st) for i in range(2)]
        PS = [Buf(f"preps{i}") for i in range(2)]
        stg = [P.sb(f"stg{i}", [128, TT], BF16, st) for i in range(4)]
        STG = [Buf(f"stg{i}") for i in range(4)]
        d_stg = [mk.dsem(f"stg{i}_{tag}") for i in range(4)]
        wv = wqkv_ap.rearrange("(kc p) c -> p kc c", p=128)
        mk.dma("pool", d_w, [(wq[:, :, i * 768:(i + 1) * 768], wv[:, :, i * 768:(i + 1) * 768]) for i in range(4)],
               writes=[WQ])
        k = 0
        otok = {}
        for t in range(NT):
            tsl = slice(t * TT, (t + 1) * TT)
            emit_norm_in(P, ws, t, l, 2, hn, HN)
            for fcg in range(16):
                b = k % 2
                si = k % 4
                k += 1
                for kc in range(DC):
                    mk.op("pe", lambda e: e.matmul(ps[b][:], wq[:, kc, fcg * 128:(fcg + 1) * 128], hn[:, kc, :],
                                                   start=(kc == 0), stop=(kc == DC - 1)),
                          reads=[WQ, HN[kc]], writes=[PS[b]])
                if fcg < 8:
                    mk.op("act", lambda e: e.activation(out=stg[si][:], in_=ps[b][:], func=AF.Copy, scale=0.125),
                          reads=[PS[b]], writes=[STG[si]])
                    dst = X.snd_q[t][fcg * 128:(fcg + 1) * 128, :]
                else:
                    mk.op("dve", lambda e: e.tensor_copy(stg[si][:], ps[b][:]), reads=[PS[b]], writes=[STG[si]])
                    dst = X.snd_k[t][(fcg - 8) * 128:(fcg - 7) * 128, :]
                otok[si] = mk.dma("sp", d_stg[si], [(dst, stg[si][:])], reads=[STG[si]])
                if fcg == 7:
                    X.last = mk.allgather(P.csem, X.snd_q[t], X.gath_q[t], extra=list(otok.values()))
                elif fcg == 15:
                    X.last = mk.allgather(P.csem, X.snd_k[t], X.gath_k[t], extra=list(otok.values()))
            for tb in range(TT // 128):
                for hf in range(2):
                    b = k % 2
                    si = k % 4
                    k += 1
                    for kc in range(DC):
                        mk.op("pe", lambda e: e.matmul(ps[b][:], hn[:, kc, tb * 128:(tb + 1) * 128],
                                                       wq[:, kc, 2 * D + hf * 512:2 * D + (hf + 1) * 512],
                                                       start=(kc == 0), stop=(kc == DC - 1)),
                              reads=[WQ, HN[kc]], writes=[PS[b]])
                    if hf == 0:
                        mk.op("act", lambda e: e.activation(out=stg[si][:], in_=ps[b][:], func=AF.Copy),
                              reads=[PS[b]], writes=[STG[si]])
                    else:
                        mk.op("dve", lambda e: e.tensor_copy(stg[si][:], ps[b][:]), reads=[PS[b]], writes=[STG[si]])
                    pairs = []
                    for jj in range(2):
                        jb = hf * 2 + jj
                        pairs.append((X.snd_v[t][jb * TT + tb * 128:jb * TT + tb * 128 + 128, :],
                                      stg[si][:, jj * 256:(jj + 1) * 256]))
                    otok[si] = mk.dma("sp", d_stg[si], pairs, reads=[STG[si]])
            X.last = mk.allgather(P.csem, X.snd_v[t], X.gath_v[t], extra=list(otok.values()))


class Pipe:
    def __init__(self, depth, batch):
        self.q = []
        self.depth = depth
        self.batch = batch

    def _n(self):
        return sum(1 for w, _ in self.q if w)

    def push(self, fn, weight=1):
        self.q.append((weight, fn))
        while self._n() >= self.depth + self.batch:
            self._pop()

    def _pop(self):
        items = []
        while self.q and self.q[0][0] == 1 and len(items) < self.batch:
            items.append(self.q.pop(0)[1])
        order = sorted(range(len(items)), key=lambda i: items[i][0])
        for i in order:
            items[i][1]()
        for i in order:
            items[i][2]()
        while self.q and self.q[0][0] == 0:
            self.q.pop(0)[1]()

    def flush(self):
        while self.q:
            self._pop()


def load_kv(P, X, d_kv, kt, vv, KTB, VB):
    kp, vp = [], []
    for r in range(4):
        for t in range(4):
            for c in range(2):
                col0 = r * NTOK + t * TT
                kp.append((kt[:, c, col0:col0 + TT], X.gath_k[t], P.idx[:, r * 2 + c:r * 2 + c + 1]))
            for tb in range(4):
                kbi = r * 16 + t * 4 + tb
                vp.append((vv[:, kbi, :], X.gath_v[t], P.idx[:, 8 + r * 4 + tb:8 + r * 4 + tb + 1]))
    P.mk.dma("pool", d_kv, kp, reads=[P.IDX], writes=[KTB], extra=[X.last])
    P.mk.dma("pool", P.mk.dsem("vload", "pool"), vp, reads=[P.IDX], writes=[VB], extra=[X.last])


def emit_core_A(P, X, IN, biasT, tag):
    mk = P.mk
    NQG = SEQ // TT
    mk.stage_boundary()
    with ExitStack() as st:
        kt = P.sb("kt", [128, 2, SEQ], BF16, st)
        KTB = Buf("kt")
        vv = P.sb("vv", [128, NKB, 256], BF16, st)
        VB = Buf("vv")
        d_kv = mk.dsem(f"kv_{tag}", "pool")
        bias = P.sb("biasA", [128, 8 * TT], F32, st)
        BIAS = Buf("biasA")
        d_bias = mk.dsem(f"bias_{tag}")
        qs = [P.sb(f"qs{i}", [128, TT], BF16, st) for i in range(2)]
        QS = [Buf(f"qs{i}") for i in range(2)]
        d_q = [mk.dsem(f"q{i}_{tag}", "pool") for i in range(2)]
        tt_ = [P.sb(f"tA{i}", [128, TT], F32, st) for i in range(8)]
        TTB = [Buf(f"tA{i}") for i in range(8)]
        pp = [P.sb(f"pA{i}", [128, TT], BF16, st) for i in range(8)]
        PP = [Buf(f"pA{i}") for i in range(8)]
        s_ps = [P.ps(f"sA_ps{i}", [128, TT], F32, st) for i in range(4)]
        SPS = [Buf(f"sps{i}") for i in range(4)]
        o_ps = [P.ps(f"oA_ps{i}", [64, TT], F32, st) for i in range(2)]
        OPS = [Buf(f"ops{i}") for i in range(2)]
        l_ps = [P.ps(f"lA_ps{i}", [64, TT], F32, st) for i in range(2)]
        LPS = [Buf(f"lps{i}") for i in range(2)]
        rl = P.sb("rlA", [64, TT], F32, st)
        RL = Buf("rl")
        osb = P.sb("osbA", [64, TT], F32, st)
        OSB = Buf("osb")
        on = [P.sb(f"onA{i}", [64, TT], F32, st) for i in range(2)]
        ON = [Buf(f"on{i}") for i in range(2)]
        d_o = [mk.dsem(f"o{i}_{tag}") for i in range(2)]

        load_kv(P, X, d_kv, kt, vv, KTB, VB)
        it = 0
        kk = 0
        otok = {}
        pipe = Pipe(3, 4)
        for hl in range(4):
            c, u = hl // 2, hl % 2
            prow = slice(u * 64, (u + 1) * 64)
            mk.dma("sp", d_bias, [(bias[:, 0:2048], biasT[hl, :, 0:2048]), (bias[:, 2048:4096], biasT[hl, :, 2048:4096])],
                   reads=[IN], writes=[BIAS])
            for qn, qg in enumerate([rd_ * 4 + t_ for t_ in range(4) for rd_ in range(4)]):
                qi = it % 2
                it += 1
                mk.dma("pool", d_q[qi], [(qs[qi][:], X.gath_q[qg % 4], P.idx[:, (qg // 4) * 2 + c:(qg // 4) * 2 + c + 1])],
                       reads=[P.IDX], writes=[QS[qi]], extra=[X.last])
                kbs = [kb for kb in (3, 4, 0, 1, 2, 5, 6, 7) if qg * 4 - 4 + kb >= 0]
                for n_, kb in enumerate(kbs):
                    kbi = qg * 4 - 4 + kb
                    b = kk % 8
                    sb_ = kk % 4
                    kk += 1
                    first, last = (n_ == 0), (n_ == len(kbs) - 1)
                    cs = slice(64 * max(0, 2 * kb - 8), 64 * (min(7, 2 * kb + 1) + 1))
                    bs = slice(kb * TT + cs.start, kb * TT + cs.stop)
                    mk.op("pe", lambda e: e.matmul(s_ps[sb_][:, cs], kt[prow, c, kbi * 128:(kbi + 1) * 128], qs[qi][prow, cs],
                                                   start=True, stop=True),
                          reads=[KTB, QS[qi]], writes=[SPS[sb_]])
                    mk.op("dve", lambda e: e.tensor_tensor(tt_[b][:, cs], s_ps[sb_][:, cs], bias[:, bs], ALU.add),
                          reads=[SPS[sb_], BIAS], writes=[TTB[b]])
                    mk.op("act", lambda e: e.activation(out=pp[b][:, cs], in_=tt_[b][:, cs], func=AF.Exp),
                          reads=[TTB[b]], writes=[PP[b]])

                    def pv_o(b=b, kbi=kbi, hl=hl, qi=qi, first=first, last=last, cs=cs):
                        mk.op("pe", lambda e: e.matmul(o_ps[qi][:, cs], vv[:, kbi, hl * 64:(hl + 1) * 64], pp[b][:, cs],
                                                       start=first, stop=last),
                              reads=[VB, PP[b]], writes=[OPS[qi]])

                    def pv_l(b=b, qi=qi, first=first, last=last, cs=cs):
                        mk.op("pe", lambda e: e.matmul(l_ps[qi][:, cs], P.ones[:, 0:64], pp[b][:, cs], start=first, stop=last),
                              reads=[P.ONES, PP[b]], writes=[LPS[qi]])
                    pipe.push((0, pv_o, pv_l))

                def epi(qi=qi, qg=qg, c=c, u=u):
                    mk.op("dve", lambda e: e.reciprocal(rl[:], l_ps[qi][:]), reads=[LPS[qi]], writes=[RL])
                    mk.op("act", lambda e: e.activation(out=osb[:], in_=o_ps[qi][:], func=AF.Copy),
                          reads=[OPS[qi]], writes=[OSB])
                    mk.op("dve", lambda e: e.tensor_tensor(on[qi][:], osb[:], rl[:], ALU.mult),
                          reads=[OSB, RL], writes=[ON[qi]])
                    rd = qg // 4
                    otok[qi] = mk.dma("sp", d_o[qi], [(X.snd_o[qg % 4][c][rd * 128 + u * 64:rd * 128 + (u + 1) * 64, :], on[qi][:])],
                                      reads=[ON[qi]])
                pipe.push(epi, 0)
                if u == 1 and qn % 4 == 3:
                    def xchg(t_=qn // 4, c=c):
                        X.last_o = mk.allgather(P.csem, X.snd_o[t_][c], X.gath_o[t_][c], extra=list(otok.values()))
                    pipe.push(xchg, 0)
        pipe.flush()


def emit_core_C(P, X, IN, tabT, tabOff, tabD, lamB, tag, lambda_init):
    mk = P.mk
    NQT = SEQ // TT
    mk.stage_boundary()
    with ExitStack() as st:
        kt = P.sb("ktC", [128, 2, SEQ], BF16, st)
        KTB = Buf("kt")
        vv = P.sb("vvC", [128, NKB, 256], BF16, st)
        VB = Buf("vv")
        d_kv = mk.dsem(f"kv_{tag}", "pool")
        tT = P.sb("tT", [128, 2, TT], F32, st)
        tO = P.sb("tO", [128, 2 * 64], F32, st)
        tD = P.sb("tD", [128, 2, 4 * TT], F32, st)
        lam = P.sb("lam", [128, 256], F32, st)
        TAB = Buf("tabs")
        d_tab = mk.dsem(f"tab_{tag}")
        lp = P.sb("lamp", [128, 128], F32, st)
        LP = Buf("lp")
        lsum = P.sb("lsum", [128, 2], F32, st)
        LS = Buf("ls")
        lexp = P.sb("lexp", [128, 2], F32, st)
        LE = Buf("le")
        nlam = P.sb("nlam", [128, 1], F32, st)
        NL = Buf("nl")
        qs = [P.sb(f"qsC{i}", [128, TT], BF16, st) for i in range(2)]
        QS = [Buf(f"qs{i}") for i in range(2)]
        d_q = [mk.dsem(f"q{i}_{tag}", "pool") for i in range(2)]
        tt_ = [P.sb(f"tC{i}", [128, TT], F32, st) for i in range(8)]
        TTB = [Buf(f"tC{i}") for i in range(8)]
        pp = [P.sb(f"pC{i}", [128, TT], BF16, st) for i in range(8)]
        PP = [Buf(f"pC{i}") for i in range(8)]
        s_ps = [P.ps(f"sC_ps{i}", [128, TT], F32, st) for i in range(4)]
        SPS = [Buf(f"sps{i}") for i in range(4)]
        o_ps = [P.ps(f"oC_ps{i}", [128, TT], F32, st) for i in range(2)]
        OPS = [Buf(f"ops{i}") for i in range(2)]
        l_ps = [P.ps(f"lC_ps{i}", [128, TT], F32, st) for i in range(2)]
        LPS = [Buf(f"lps{i}") for i in range(2)]
        rl = [P.sb(f"rlC{i}", [128, TT], F32, st) for i in range(2)]
        RL = [Buf(f"rl{i}") for i in range(2)]
        osb = [P.sb(f"osbC{i}", [128, TT], F32, st) for i in range(2)]
        OSB = [Buf(f"osb{i}") for i in range(2)]
        am = [P.sb(f"amC{i}", [128, TT], F32, st) for i in range(2)]
        AM = [Buf(f"am{i}") for i in range(2)]
        on = [P.sb(f"onC{i}", [128, TT], F32, st) for i in range(2)]
        ON = [Buf(f"on{i}") for i in range(2)]
        d_o = [mk.dsem(f"o{i}_{tag}") for i in range(2)]

        mk.dma("sp", d_tab, [(tT[:, 0, :], tabT[0]), (tT[:, 1, :], tabT[1]), (tO[:], tabOff),
                             (tD[:, 0, :], tabD[0]), (tD[:, 1, :], tabD[1]), (lam[:], lamB)],
               reads=[IN], writes=[TAB])
        load_kv(P, X, d_kv, kt, vv, KTB, VB)
        mk.op("dve", lambda e: e.tensor_tensor(lp[:, 0:64], lam[:, 0:64], lam[:, 64:128], ALU.mult),
              reads=[TAB], writes=[LP])
        mk.op("dve", lambda e: e.tensor_tensor(lp[:, 64:128], lam[:, 128:192], lam[:, 192:256], ALU.mult),
              reads=[TAB, LP], writes=[LP])
        mk.op("dve", lambda e: e.reduce_sum(lsum[:, 0:1], lp[:, 0:64], mybir.AxisListType.X), reads=[LP], writes=[LS])
        mk.op("dve", lambda e: e.reduce_sum(lsum[:, 1:2], lp[:, 64:128], mybir.AxisListType.X), reads=[LP, LS], writes=[LS])
        mk.op("act", lambda e: e.activation(out=lexp[:], in_=lsum[:], func=AF.Exp), reads=[LS], writes=[LE])
        mk.op("dve", lambda e: e.tensor_tensor(nlam[:], lexp[:, 1:2], lexp[:, 0:1], ALU.subtract), reads=[LE], writes=[NL])
        mk.op("dve", lambda e: e.tensor_scalar(nlam[:], nlam[:], -float(lambda_init), None, ALU.add), reads=[NL], writes=[NL])

        it = 0
        kk = 0
        otok = {}
        pipe = Pipe(3, 4)
        for hl in range(2):
            for qn, qt in enumerate([rd_ * 4 + t_ for t_ in range(4) for rd_ in range(4)]):
                qi = it % 2
                it += 1
                mk.dma("pool", d_q[qi], [(qs[qi][:], X.gath_q[qt % 4], P.idx[:, (qt // 4) * 2 + hl:(qt // 4) * 2 + hl + 1])],
                       reads=[P.IDX], writes=[QS[qi]], extra=[X.last])
                nkb = 4 * qt + 4
                for kb in range(nkb):
                    diag = kb >= 4 * qt
                    first, last = (kb == 0), (kb == nkb - 1)
                    for m in range(2):
                        b = kk % 8
                        sb_ = kk % 4
                        kk += 1
                        prow = slice(m * 64, (m + 1) * 64)
                        mk.op("pe", lambda e: e.matmul(s_ps[sb_][:], kt[prow, hl, kb * 128:(kb + 1) * 128], qs[qi][prow, :],
                                                       start=True, stop=True),
                              reads=[KTB, QS[qi]], writes=[SPS[sb_]])
                        if diag:
                            j = kb - 4 * qt
                            mk.op("dve", lambda e: e.tensor_tensor(tt_[b][:], s_ps[sb_][:], tD[:, hl, j * TT:(j + 1) * TT], ALU.add),
                                  reads=[SPS[sb_], TAB], writes=[TTB[b]])
                            mk.op("act", lambda e: e.activation(out=pp[b][:], in_=tt_[b][:], func=AF.Exp),
                                  reads=[TTB[b]], writes=[PP[b]])
                        else:
                            n = 4 * qt - kb
                            mk.op("dve", lambda e: e.tensor_tensor(tt_[b][:], s_ps[sb_][:], tT[:, hl, :], ALU.add),
                                  reads=[SPS[sb_], TAB], writes=[TTB[b]])
                            mk.op("act", lambda e: e.activation(out=pp[b][:], in_=tt_[b][:], func=AF.Exp,
                                                                bias=tO[:, hl * 64 + n:hl * 64 + n + 1]),
                                  reads=[TTB[b], TAB], writes=[PP[b]])

                        def pv_o(b=b, kb=kb, hl=hl, m=m, first=first, last=last):
                            mk.op("pe", lambda e: e.matmul(o_ps[m][:], vv[:, kb, hl * 128:(hl + 1) * 128], pp[b][:],
                                                           start=first, stop=last),
                                  reads=[VB, PP[b]], writes=[OPS[m]])

                        def pv_l(b=b, m=m, first=first, last=last):
                            mk.op("pe", lambda e: e.matmul(l_ps[m][:], P.ones[:], pp[b][:], start=first, stop=last),
                                  reads=[P.ONES, PP[b]], writes=[LPS[m]])
                        pipe.push((m, pv_o, pv_l))

                def epi(qi=qi, qt=qt, hl=hl):
                    for m in range(2):
                        mk.op("dve", lambda e: e.reciprocal(rl[m][:], l_ps[m][:]), reads=[LPS[m]], writes=[RL[m]])
                        mk.op("act", lambda e: e.activation(out=osb[m][:], in_=o_ps[m][:], func=AF.Copy),
                              reads=[OPS[m]], writes=[OSB[m]])
                        mk.op("dve", lambda e: e.tensor_tensor(am[m][:], osb[m][:], rl[m][:], ALU.mult),
                              reads=[OSB[m], RL[m]], writes=[AM[m]])
                    mk.op("dve", lambda e: e.scalar_tensor_tensor(on[qi][:], am[1][:], nlam[:, 0:1], am[0][:], ALU.mult, ALU.add),
                          reads=[AM[0], AM[1], NL], writes=[ON[qi]])
                    rd = qt // 4
                    otok[qi] = mk.dma("sp", d_o[qi], [(X.snd_o[qt % 4][hl][rd * 128:(rd + 1) * 128, :], on[qi][:])],
                                      reads=[ON[qi]])
                pipe.push(epi, 0)
                if qn % 4 == 3:
                    def xchg(t_=qn // 4, hl=hl):
                        X.last_o = mk.allgather(P.csem, X.snd_o[t_][hl], X.gath_o[t_][hl], extra=list(otok.values()))
                    pipe.push(xchg, 0)
        pipe.flush()


def emit_post(P, l, X, IN, w_o_ap, tag, sublnT=None, lambda_init=None):
    mk = P.mk
    mk.stage_boundary()
    with ExitStack() as st:
        d_w = mk.dsem(f"wo_{tag}", "pool")
        wo, WO = load_wres(P, st, "wo", w_o_ap, DC, d_w)
        ws = NormWS(P, st, tag)
        pw = ProjWS(P, st, tag)
        ob = [P.sb(f"ob{i}", [128, DC, TT], F32, st) for i in range(2)]
        OB = [Buf(f"ob{i}") for i in range(2)]
        d_ob = [mk.dsem(f"ob{i}_{tag}", "pool") for i in range(2)]
        onb = P.sb("onb", [128, DC, TT], BF16, st)
        ONB = [Buf(f"onb{dc}") for dc in range(DC)]
        if sublnT is not None:
            sgc = P.sb("sgc", [128, 1], F32, st)
            SGC = Buf("sgc")
            d_sg = mk.dsem(f"sg_{tag}")
            mk.dma("sp", d_sg, [(sgc[:], sublnT)], reads=[IN], writes=[SGC])
            mk.op("dve", lambda e: e.tensor_scalar(sgc[:], sgc[:], float(1.0 - lambda_init), None, ALU.mult),
                  reads=[SGC], writes=[SGC])
        for t in range(NT):
            i = t % 2
            mk.dma("pool", d_ob[i], [(ob[i][:, 2 * r + c, :], X.gath_o[t][c], P.idx[:, 24 + r:25 + r])
                                     for r in range(4) for c in range(2)],
                   reads=[P.IDX], writes=[OB[i]], extra=[X.last_o])
            for fc in range(DC):
                if sublnT is None:
                    if fc % 2 == 0:
                        mk.op("dve", lambda e: e.tensor_copy(onb[:, fc, :], ob[i][:, fc, :]), reads=[OB[i]], writes=[ONB[fc]])
                    else:
                        mk.op("act", lambda e: e.activation(out=onb[:, fc, :], in_=ob[i][:, fc, :], func=AF.Copy),
                              reads=[OB[i]], writes=[ONB[fc]])
                else:
                    si = ws.square(ob[i][:, fc, :], [OB[i]])
                    ws.accum(si, True, True)
                    ws.finish(128)
                    mk.op("dve", lambda e: e.scalar_tensor_tensor(onb[:, fc, :], ob[i][:, fc, :], sgc[:, 0:1],
                                                                  ws.rstd[:], ALU.mult, ALU.mult),
                          reads=[OB[i], SGC, ws.RSTD], writes=[ONB[fc]])
            emit_proj_norm_res(P, ws, pw, t, onb, ONB, DC, wo, WO, l, 3, False)


def emit_mixer_B(P, l, w_in_ap, lngT, lnbT, wsT, triu, bsR, w_o_ap, IN, tag):
    mk = P.mk
    mk.stage_boundary()
    with ExitStack() as st:
        d_w = mk.dsem(f"wiB_{tag}", "pool")
        wi = P.sb("wiB", [128, DC, 2 * D], BF16, st)
        WI = Buf("wiB")
        wv = w_in_ap.rearrange("(kc p) c -> p kc c", p=128)
        mk.dma("pool", d_w, [(wi[:, :, i * 512:(i + 1) * 512], wv[:, :, i * 512:(i + 1) * 512]) for i in range(4)],
               writes=[WI])
        d_wo = mk.dsem(f"woB_{tag}", "pool")
        wo, WO = load_wres(P, st, "woB", w_o_ap, DC, d_wo)
        d_c = mk.dsem(f"cB_{tag}")
        lng = P.sb("lng", [128, 8], F32, st)
        lnb = P.sb("lnb", [128, 8], F32, st)
        addt = P.sb("addt", [128, 8 * 128], F32, st)
        ADDT = Buf("addt")
        wsb = P.sb("wsb", [128, 8 * 128], BF16, st)
        WSB = Buf("wsb")
        CB = Buf("cB")
        ps = [P.ps(f"B_ps{i}", [128, TT], F32, st) for i in range(4)]
        PS = [Buf(f"Bps{i}") for i in range(4)]
        with ExitStack() as st2:
            wsf = P.sb("wsf", [128, 8 * 128], F32, st2)
            tri = P.sb("tri", [128, 128], F32, st2)
            bsr = P.sb("bsr", [128, 8 * 128], F32, st2)
            mk.dma("sp", d_c, [(lng[:], lngT), (lnb[:], lnbT), (wsf[:], wsT), (tri[:], triu), (bsr[:], bsR)],
                   reads=[IN], writes=[CB])
            for g in range(8):
                gs = slice(g * 128, (g + 1) * 128)
                mk.op("dve", lambda e: e.tensor_tensor(wsb[:, gs], wsf[:, gs], tri[:], ALU.mult),
                      reads=[CB, WSB], writes=[WSB])
            for g in range(8):
                gs = slice(g * 128, (g + 1) * 128)
                b = g % 4
                mk.op("pe", lambda e: e.matmul(ps[b][:, 0:128], P.ones[:], wsb[:, gs], start=True, stop=True),
                      reads=[P.ONES, WSB], writes=[PS[b]])
                mk.op("dve", lambda e: e.scalar_tensor_tensor(addt[:, gs], ps[b][:, 0:128], lnb[:, g:g + 1], bsr[:, gs],
                                                              ALU.mult, ALU.add),
                      reads=[PS[b], CB, ADDT], writes=[ADDT])
        mk.stage_boundary()
        ws = NormWS(P, st, tag)
        pw = ProjWS(P, st, tag)
        hn = P.sb("hnB", [128, DC, TT], BF16, st)
        HN = [Buf(f"hn{dc}") for dc in range(DC)]
        u = P.sb("uB", [128, DC, TT], F32, st)
        U = [Buf(f"u{dc}") for dc in range(DC)]
        vf = P.sb("vfB", [128, D], F32, st)
        VF = Buf("vf")
        vn = P.sb("vnB", [128, 4, D], BF16, st)
        VN = [Buf(f"vn{tb}") for tb in range(4)]
        st4 = P.sb("st4", [128, 8], F32, st)
        ST4 = Buf("st4")
        junk = P.sb("junkB", [128, D], F32, st)
        JK = Buf("junk")
        t1 = [P.sb(f"t1B{i}", [128, TT], F32, st) for i in range(2)]
        T1 = [Buf(f"t1{i}") for i in range(2)]
        k = 0
        for t in range(NT):
            emit_norm_in(P, ws, t, l, 2, hn, HN)
            for fc in range(DC):
                b = k % 4
                k += 1
                for kc in range(DC):
                    mk.op("pe", lambda e: e.matmul(ps[b][:], wi[:, kc, fc * 128:(fc + 1) * 128], hn[:, kc, :],
                                                   start=(kc == 0), stop=(kc == DC - 1)),
                          reads=[WI, HN[kc]], writes=[PS[b]])
                mk.op("act", lambda e: e.activation(out=u[:, fc, :], in_=ps[b][:], func=AF.Gelu),
                      reads=[PS[b]], writes=[U[fc]])
            for tb in range(4):
                for hf in range(2):
                    b = k % 4
                    k += 1
                    for kc in range(DC):
                        mk.op("pe", lambda e: e.matmul(ps[b][:], hn[:, kc, tb * 128:(tb + 1) * 128],
                                                       wi[:, kc, D + hf * 512:D + (hf + 1) * 512],
                                                       start=(kc == 0), stop=(kc == DC - 1)),
                              reads=[WI, HN[kc]], writes=[PS[b]])
                    mk.op("act", lambda e: e.activation(out=vf[:, hf * 512:(hf + 1) * 512], in_=ps[b][:], func=AF.Gelu),
                          reads=[PS[b], VF], writes=[VF])
                mk.op("dve", lambda e: e.reduce_sum(st4[:, 0:1], vf[:], mybir.AxisListType.X),
                      reads=[VF, ST4], writes=[ST4])
                mk.op("act", lambda e: e.activation(out=junk[:], in_=vf[:], func=AF.Square),
                      reads=[VF, JK], writes=[JK])
                mk.op("dve", lambda e: e.reduce_sum(st4[:, 1:2], junk[:], mybir.AxisListType.X),
                      reads=[JK, ST4], writes=[ST4])
                mk.op("dve", lambda e: e.tensor_scalar(st4[:, 2:3], st4[:, 0:1], 1.0 / D, None, ALU.mult),
                      reads=[ST4], writes=[ST4])
                mk.op("dve", lambda e: e.tensor_tensor(st4[:, 5:6], st4[:, 2:3], st4[:, 2:3], ALU.mult),
                      reads=[ST4], writes=[ST4])
                mk.op("dve", lambda e: e.scalar_tensor_tensor(st4[:, 6:7], st4[:, 1:2], 1.0 / D, st4[:, 5:6],
                                                              ALU.mult, ALU.subtract),
                      reads=[ST4], writes=[ST4])
                mk.op("act", lambda e: e.activation(out=st4[:, 7:8], in_=st4[:, 6:7], func=AF.Sqrt, bias=P.epsb[:], scale=1.0),
                      reads=[ST4, P.EPSB], writes=[ST4])
                mk.op("dve", lambda e: e.reciprocal(st4[:, 3:4], st4[:, 7:8]), reads=[ST4], writes=[ST4])
                mk.op("dve", lambda e: e.scalar_tensor_tensor(st4[:, 4:5], st4[:, 2:3], -1.0, st4[:, 3:4], ALU.mult, ALU.mult),
                      reads=[ST4], writes=[ST4])
                mk.op("act", lambda e: e.activation(out=vn[:, tb, :], in_=vf[:], func=AF.Identity,
                                                    bias=st4[:, 4:5], scale=st4[:, 3:4]),
                      reads=[VF, ST4], writes=[VN[tb]])
            for g in range(8):
                gs = slice(g * 128, (g + 1) * 128)
                b = k % 4
                k += 1
                i = g % 2
                for tb in range(4):
                    mk.op("pe", lambda e: e.matmul(ps[b][:, tb * 128:(tb + 1) * 128], vn[:, tb, gs],
                                                   wsb[:, gs], start=True, stop=True),
                          reads=[VN[tb], WSB, PS[b]], writes=[PS[b]])
                for tb in range(4):
                    tbs = slice(tb * 128, (tb + 1) * 128)
                    mk.op("dve", lambda e: e.scalar_tensor_tensor(t1[i][:, tbs], ps[b][:, tbs], lng[:, g:g + 1],
                                                                  addt[:, gs], ALU.mult, ALU.add),
                          reads=[PS[b], CB, ADDT, T1[i]], writes=[T1[i]])
                mk.op("dve", lambda e: e.tensor_tensor(hn[:, g, :], t1[i][:], u[:, g, :], ALU.mult),
                      reads=[T1[i], U[g]], writes=[HN[g]])
            emit_proj_norm_res(P, ws, pw, t, hn, HN, DC, wo, WO, l, 3, False)


def load_x(P, xT_ap):
    mk = P.mk
    P.d_x = mk.dsem("xin")
    v = xT_ap.rearrange("(dc p) t -> p dc t", p=128)
    mk.dma("sp", P.d_x, [(P.xs[:, dc, :], v[:, dc, :]) for dc in range(DC)],
           writes=[b for dc in range(DC) for b in P.X[dc]])


def store_x(P, oT_ap):
    mk = P.mk
    P.d_xo = mk.dsem("xout")
    v = oT_ap.rearrange("(dc p) t -> p dc t", p=128)
    return mk.dma("sp", P.d_xo, [(v[:, dc, :], P.xs[:, dc, :]) for dc in range(DC)],
                  reads=[b for dc in range(DC) for b in P.X[dc]])


I32 = mybir.dt.int32
WNAMES = ["ff1_w_in", "ff1_w_out", "ff2_w_in", "ff2_w_out", "a_w_qkv", "a_w_o", "b_w_in", "b_w_o", "c_w_qkv", "c_w_o"]


class XChg:
    def __init__(self, nc, l):
        def dt_(name, shape, dt):
            return nc.dram_tensor(f"{name}_L{l}", list(shape), dt).ap()
        self.snd_q = [dt_(f"sq{t}", [D, TT], BF16) for t in range(NT)]
        self.snd_k = [dt_(f"sk{t}", [D, TT], BF16) for t in range(NT)]
        self.snd_v = [dt_(f"sv{t}", [4 * TT, 256], BF16) for t in range(NT)]
        self.gath_q = [dt_(f"gq{t}", [4 * D, TT], BF16) for t in range(NT)]
        self.gath_k = [dt_(f"gk{t}", [4 * D, TT], BF16) for t in range(NT)]
        self.gath_v = [dt_(f"gv{t}", [16 * TT, 256], BF16) for t in range(NT)]
        self.snd_o = [[dt_(f"so{t}_{c}", [4 * 128, TT], F32) for c in range(2)] for t in range(NT)]
        self.gath_o = [[dt_(f"go{t}_{c}", [16 * 128, TT], F32) for c in range(2)] for t in range(NT)]
        self.last = None
        self.last_o = None


def build_fused(layers=(0, 1, 2, 3), ffn=True):
    nc = bass.Bass("TRN2", target_bir_lowering=False)

    def ext(name, shape, dt=F32):
        return nc.dram_tensor(name, list(shape), dt, kind="ExternalInput").ap()
    xT = ext("xT", [D, NTOK])
    gT = ext("gT", [128, DEPTH * 6 * DC])
    idx = ext("idx", [128, 28], I32)
    W = {"ff1_w_in": ext("ff1_w_in", [DEPTH, D, 2 * DFF]), "ff1_w_out": ext("ff1_w_out", [DEPTH, DFF, D]),
         "ff2_w_in": ext("ff2_w_in", [DEPTH, D, 2 * DFF]), "ff2_w_out": ext("ff2_w_out", [DEPTH, DFF, D]),
         "a_w_qkv": ext("a_w_qkv", [2, D, 3 * D]), "a_w_o": ext("a_w_o", [2, D, D]),
         "b_w_in": ext("b_w_in", [1, D, 2 * D]), "b_w_o": ext("b_w_o", [1, D, D]),
         "c_w_qkv": ext("c_w_qkv", [1, D, 3 * D]), "c_w_o": ext("c_w_o", [1, D, D])}
    biasT = [ext("biasT0", [4, 128, 8 * TT]), ext("biasT1", [4, 128, 8 * TT])]
    tabT = ext("tabT", [2, 128, TT])
    tabOff = ext("tabOff", [128, 128])
    tabD = ext("tabD", [2, 128, 4 * TT])
    lamB = ext("lamB", [128, 256])
    sublnT = ext("sublnT", [128, 1])
    lngT = ext("lngT", [128, 8])
    lnbT = ext("lnbT", [128, 8])
    wsT = ext("wsT", [128, 8 * 128])
    triu = ext("triu", [128, 128])
    bsR = ext("bsR", [128, 8 * 128])
    xTo = nc.dram_tensor("xTo", [D, NTOK], F32, kind="ExternalOutput").ap()
    with ExitStack() as stack:
        mk = MK(nc, stack)
        P = Prog(nc, stack, mk)
        setup_common(P, gT)
        P.idx = P.sb("idx", [128, 28], I32)
        P.IDX = Buf("idx")
        mk.dma("sp", mk.dsem("idx"), [(P.idx[:], idx)], writes=[P.IDX])
        csem_h = stack.enter_context(nc.semaphore("csem"))
        P.csem = DSem("csem", csem_h)
        load_x(P, xT)
        IN = Buf("in")
        for l in layers:
            kind, jm = l % 3, l // 3
            if ffn:
                with nc.named_scope(f"L{l}_ffn1"):
                    emit_ffn(P, l, 0, W["ff1_w_in"][l], W["ff1_w_out"][l])
            if kind == 1:
                with nc.named_scope(f"L{l}_mixB"):
                    emit_mixer_B(P, l, W["b_w_in"][jm], lngT, lnbT, wsT, triu, bsR, W["b_w_o"][jm], IN, f"mB{l}")
            else:
                X = XChg(nc, l)
                if kind == 0:
                    with nc.named_scope(f"L{l}_pre"):
                        emit_pre(P, l, W["a_w_qkv"][jm], X, f"pre{l}")
                    with nc.named_scope(f"L{l}_core"):
                        emit_core_A(P, X, IN, biasT[jm], f"cA{l}")
                    with nc.named_scope(f"L{l}_post"):
                        emit_post(P, l, X, IN, W["a_w_o"][jm], f"post{l}")
                else:
                    li = 0.8 - 0.6 * float(np.exp(-0.3 * l))
                    with nc.named_scope(f"L{l}_pre"):
                        emit_pre(P, l, W["c_w_qkv"][jm], X, f"pre{l}")
                    with nc.named_scope(f"L{l}_core"):
                        emit_core_C(P, X, IN, tabT, tabOff, tabD, lamB, f"cC{l}", li)
                    with nc.named_scope(f"L{l}_post"):
                        emit_post(P, l, X, IN, W["c_w_o"][jm], f"post{l}", sublnT, li)
            if ffn:
                with nc.named_scope(f"L{l}_ffn2"):
                    emit_ffn(P, l, 1, W["ff2_w_in"][l], W["ff2_w_out"][l])
        mk.wait_all("sp", [store_x(P, xTo)])
    return nc


def gains_T(norm_g_l):
    out = np.zeros((128, DEPTH * 6 * DC), np.float32)
    out[:, :6 * DC] = norm_g_l.reshape(6, DC, 128).transpose(2, 0, 1).reshape(128, 6 * DC)
    return out


def feat_cols(v, n):
    return np.ascontiguousarray(v.reshape(n, 128).T)


def bias_tables_A(rel_bias, j):
    k_in = np.arange(128)[:, None, None]
    kb = np.arange(8)[None, :, None]
    q_rel = np.arange(512)[None, None, :]
    krel = kb * 128 + k_in
    valid = (krel // 64 >= q_rel // 64) & (krel // 64 <= q_rel // 64 + 8)
    idx = np.clip(q_rel + 512 - krel, -128, 128) + 128
    out = np.empty((4, 128, 8, 512), np.float32)
    for hl in range(4):
        g = rel_bias[4 * j + hl][idx]
        out[hl] = np.where(valid, g, np.float32(NEG))
    return out.reshape(4, 128, 8 * 512)


def tables_C(j):
    tabT = np.empty((2, 128, 512), np.float32)
    tabOff = np.empty((128, 128), np.float32)
    tabD = np.empty((2, 128, 4, 512), np.float32)
    k_in = np.arange(128)[:, None]
    q_rel = np.arange(512)[None, :]
    for hl in range(2):
        h = 2 * j + hl
        slope = np.float32(2.0 ** (-(h + 1)))
        tabT[hl] = -slope * (q_rel - k_in).astype(np.float32)
        tabOff[:, hl * 64:(hl + 1) * 64] = (-slope * 128.0 * np.arange(64, dtype=np.float32))[None, :]
        for jj in range(4):
            k_rel = jj * 128 + k_in
            allowed = k_rel < (q_rel // 64 + 1) * 64
            tabD[hl, :, jj, :] = np.where(allowed, -slope * np.abs(q_rel - k_rel).astype(np.float32), np.float32(NEG))
    return tabT, tabOff, tabD.reshape(2, 128, 4 * 512)


def gains_all(norm_g):
    return np.ascontiguousarray(norm_g.reshape(DEPTH, 6, DC, 128).transpose(3, 0, 1, 2).reshape(128, DEPTH * 6 * DC))


def idx_table(j):
    p = np.arange(128, dtype=np.int32)
    t = np.zeros((128, 28), np.int32)
    for r in range(4):
        for c in range(2):
            t[:, r * 2 + c] = r * 1024 + 256 * j + c * 128 + p
        for tb in range(4):
            t[:, 8 + r * 4 + tb] = r * 2048 + j * 512 + tb * 128 + p
        t[:, 24 + r] = r * 512 + j * 128 + p
    return t


def make_in_maps(inp):
    x = inp["x"]
    shared = {k: inp[k] for k in WNAMES}
    ws = inp["b_w_s"][0]
    shared.update({
        "gT": gains_all(inp["norm_g"]),
        "lamB": np.ascontiguousarray(np.broadcast_to(inp["c_lambda"][0].reshape(1, 256), (128, 256))),
        "sublnT": np.ascontiguousarray(inp["c_subln_g"][0].reshape(128, 1)),
        "lngT": feat_cols(inp["b_ln_g"][0], 8), "lnbT": feat_cols(inp["b_ln_b"][0], 8),
        "wsT": np.ascontiguousarray(ws.transpose(2, 0, 1).reshape(128, 8 * 128)),
        "triu": np.triu(np.ones((128, 128), np.float32)),
        "bsR": np.ascontiguousarray(np.broadcast_to(inp["b_b_s"][0].reshape(1, 8 * 128), (128, 8 * 128))),
    })
    in_maps = []
    for c in range(NCORES):
        b, j = c // 4, c % 4
        tabT, tabOff, tabD = tables_C(j)
        m = dict(shared)
        m.update({"xT": np.ascontiguousarray(x[b, NTOK * j:NTOK * (j + 1), :].T), "idx": idx_table(j),
                  "biasT0": bias_tables_A(inp["a_rel_bias"][0], j), "biasT1": bias_tables_A(inp["a_rel_bias"][1], j),
                  "tabT": tabT, "tabOff": tabOff, "tabD": tabD})
        in_maps.append(m)
    return in_maps


def kernel(**inputs):
    inp = {k: np.ascontiguousarray(np.asarray(v)) for k, v in inputs.items()}
    x = inp["x"]
    nc = build_fused()
    in_maps = make_in_maps(inp)
    res = run_bass_kernel_spmd(nc, in_maps, core_ids=list(range(NCORES)))
    out = np.empty_like(x)
    for c in range(NCORES):
        b, j = c // 4, c % 4
        out[b, NTOK * j:NTOK * (j + 1), :] = res.results[c]["xTo"].T
    return out
```
